# Optimizing a Trainium2 kernel written in Bass

```python
import math
import jax, jax.numpy as jnp
from jax import lax
import numpy as np

D_MODEL = 1024
BATCH = 8
SEQ = 2048
DEPTH = 4
DEC_BATCH = 128
DEC_SEQ = 1
PAST_LEN = 2048
PAGE_SIZE = 128

N_ATTN_LAYERS = (DEPTH + 1) // 2
N_DELTA_LAYERS = DEPTH // 2
MIX_W = D_MODEL
EPS = 1e-6
POOL_WINDOWS = (2, 4, 8, 16)
POOL_GROUPS = 4
POOL_W = MIX_W // 4
POOL_GC = POOL_W // POOL_GROUPS
POOL_PREFIX = max(POOL_WINDOWS) - 1
DIFF_W = MIX_W - POOL_W
DIFF_DV = 128
DIFF_HEADS = DIFF_W // DIFF_DV
DIFF_DH = DIFF_DV // 2
ROPE_THETA = 10000.0
Q_BLOCK = 128
DELTA_W = (3 * MIX_W) // 4
DELTA_DK = 128
DELTA_DV = 128
DELTA_HEADS = DELTA_W // DELTA_DV
CONV_WIDTH = 4
CONV_CH = 3 * DELTA_W
DELTA_CHUNK = 64
SG_W = MIX_W - DELTA_W
SG_GROUPS = 4
SG_GC = SG_W // SG_GROUPS
SG_CHUNK = 128
IN_E = POOL_W + 3 * DIFF_W + MIX_W
IN_O = CONV_CH + 2 * SG_W + 2 * DELTA_HEADS + MIX_W

kernel_name = 'pool_diffattn_gdn_sgmlp_hybrid_step'


def split_cols(x, sizes):
    out, off = [], 0
    for s in sizes:
        out.append(x[..., off:off + s])
        off += s
    return out


def rmsnorm(x, w):
    xf = x.astype(jnp.float32)
    return (xf * lax.rsqrt(jnp.mean(xf * xf, axis=-1, keepdims=True) + EPS)).astype(x.dtype) * w


def l2norm(x):
    xf = x.astype(jnp.float32)
    return (xf * lax.rsqrt(jnp.sum(xf * xf, axis=-1, keepdims=True) + EPS)).astype(x.dtype)


def rope(x, pos):
    half = x.shape[-1] // 2
    inv = 1.0 / (ROPE_THETA ** (jnp.arange(half, dtype=jnp.float32) / half))
    ang = pos.astype(jnp.float32)[:, None] * inv[None, :]
    cos = jnp.cos(ang)[None, :, None, :]
    sin = jnp.sin(ang)[None, :, None, :]
    xf = x.astype(jnp.float32)
    x1, x2 = xf[..., :half], xf[..., half:]
    return jnp.concatenate([x1 * cos - x2 * sin, x2 * cos + x1 * sin], axis=-1).astype(x.dtype)


def multiscale_pool(ext, pos):
    B, T, _ = ext.shape
    L = T - POOL_PREFIX
    cs = jnp.concatenate([jnp.zeros((B, 1, POOL_W), jnp.float32),
                          jnp.cumsum(ext.astype(jnp.float32), axis=1)], axis=1)
    means = []
    for g, w in enumerate(POOL_WINDOWS):
        ch = slice(g * POOL_GC, (g + 1) * POOL_GC)
        s = cs[:, POOL_PREFIX + 1:POOL_PREFIX + 1 + L, ch] - cs[:, POOL_PREFIX + 1 - w:POOL_PREFIX + 1 - w + L, ch]
        means.append(s / jnp.minimum(pos + 1, w).astype(jnp.float32)[None, :, None])
    mean = jnp.stack(means, axis=2)
    tok = ext[:, POOL_PREFIX:].reshape(B, L, POOL_GROUPS, POOL_GC).astype(jnp.float32)
    return (mean - tok).astype(ext.dtype)


def diff_core(q, k, v, allowed, lam):
    s = jnp.einsum('bqhcd,bkhcd->bhcqk', q, k).astype(jnp.float32) * (DIFF_DH ** -0.5)
    p = jax.nn.softmax(jnp.where(allowed, s, -jnp.inf), axis=-1)
    attn = p[:, :, 0] - lam * p[:, :, 1]
    return jnp.einsum('bhqk,bkhd->bqhd', attn.astype(v.dtype), v)


def diff_attention_prompt(q, k, v, lam):
    B, S = q.shape[0], q.shape[1]
    nb = S // Q_BLOCK
    qb = jnp.swapaxes(q.reshape(B, nb, Q_BLOCK, DIFF_HEADS, 2, DIFF_DH), 0, 1)
    kpos = jnp.arange(S)

    def one_block(args):
        q_blk, i = args
        qpos = i * Q_BLOCK + jnp.arange(Q_BLOCK)
        return diff_core(q_blk, k, v, kpos[None, :] <= qpos[:, None], lam)

    out = lax.map(one_block, (qb, jnp.arange(nb)))
    return jnp.swapaxes(out, 0, 1).reshape(B, S, DIFF_HEADS, DIFF_DV)


def _to_chunks(x, n, c):
    x = jnp.pad(x, [(0, 0), (0, n * c - x.shape[1])] + [(0, 0)] * (x.ndim - 2))
    x = x.reshape((x.shape[0], n, c) + x.shape[2:])
    return jnp.swapaxes(jnp.swapaxes(x, 0, 1), 2, 3)


def gated_delta_rule(q, k, v, beta, logg, s0):
    B, L, H, _ = q.shape
    DV = v.shape[-1]
    f32 = jnp.float32
    c = min(DELTA_CHUNK, L)
    n = -(-L // c)
    qc, kc, vc = (_to_chunks(t.astype(f32), n, c) for t in (q, k, v))
    bc = _to_chunks(beta.astype(f32), n, c)
    G = jnp.cumsum(_to_chunks(logg.astype(f32), n, c), axis=-1)
    causal = jnp.tril(jnp.ones((c, c), dtype=bool))
    strict = jnp.tril(jnp.ones((c, c), dtype=bool), -1)
    decay = jnp.exp(jnp.where(causal, G[..., :, None] - G[..., None, :], -jnp.inf))
    kb = kc * bc[..., None]
    lmat = jnp.where(strict, jnp.einsum('nbhid,nbhjd->nbhij', kb, kc) * decay, 0.0)
    eye = jnp.eye(c, dtype=f32)
    tmat = lax.linalg.triangular_solve(eye + lmat, jnp.broadcast_to(eye, lmat.shape), left_side=True, lower=True)
    u = jnp.einsum('nbhij,nbhjd->nbhid', tmat, vc * bc[..., None])
    w = jnp.einsum('nbhij,nbhjd->nbhid', tmat, kb * jnp.exp(G)[..., None])
    aqk = jnp.einsum('nbhid,nbhjd->nbhij', qc, kc) * decay

    def step(s, xs):
        q_c, k_c, u_c, w_c, g_c, a_c = xs
        delta = u_c - jnp.einsum('bhcd,bhdv->bhcv', w_c, s)
        o = jnp.einsum('bhcd,bhdv->bhcv', q_c * jnp.exp(g_c)[..., None], s) + jnp.einsum('bhij,bhjv->bhiv', a_c, delta)
        g_last = g_c[..., -1:]
        s = s * jnp.exp(g_last)[..., None] + jnp.einsum('bhcd,bhcv->bhdv', k_c * jnp.exp(g_last - g_c)[..., None], delta)
        return s, o

    s_fin, o = lax.scan(step, s0.astype(f32), (qc, kc, u, w, G, aqk))
    o = jnp.swapaxes(jnp.swapaxes(o, 2, 3), 0, 1).reshape(B, n * c, H, DV)[:, :L]
    return o.astype(v.dtype), s_fin.astype(s0.dtype)


def chunk_spatial_gate(vv, w_s, b_s):
    B, L, G, C = vv.shape
    n = -(-L // SG_CHUNK)
    vp = jnp.pad(vv, ((0, 0), (0, n * SG_CHUNK - L), (0, 0), (0, 0))).reshape(B, n, SG_CHUNK, G, C)
    mixed = jnp.einsum('gts,bnsgc->bntgc', jnp.tril(w_s), vp) + b_s.T[None, None, :, :, None]
    return mixed.reshape(B, n * SG_CHUNK, G, C)[:, :L]


def even_mixer(h, pos, pool_prefix, k_past, v_past, w_in, w_out, pool_w, pool_scale,
               qn_w, kn_w, lam_qk, subln_w, lam_init):
    B, L, _ = h.shape
    u, q, k, v, gate = split_cols(h @ w_in, (POOL_W, DIFF_W, DIFF_W, DIFF_W, MIX_W))
    ext = jnp.concatenate([pool_prefix.astype(u.dtype), u], axis=1)
    a_out = jnp.einsum('blgc,gcd->blgd', multiscale_pool(ext, pos), pool_w).reshape(B, L, POOL_W) * pool_scale
    q = rope(rmsnorm(q.reshape(B, L, 2 * DIFF_HEADS, DIFF_DH), qn_w), pos)
    k = rope(rmsnorm(k.reshape(B, L, 2 * DIFF_HEADS, DIFF_DH), kn_w), pos)
    v = v.reshape(B, L, DIFF_HEADS, DIFF_DV)
    lq = lam_qk.astype(jnp.float32)
    lam = jnp.exp(jnp.sum(lq[0] * lq[1])) - jnp.exp(jnp.sum(lq[2] * lq[3])) + lam_init
    qh = q.reshape(B, L, DIFF_HEADS, 2, DIFF_DH)
    if k_past is None:
        o = diff_attention_prompt(qh, k.reshape(B, L, DIFF_HEADS, 2, DIFF_DH), v, lam)
    else:
        p_len = k_past.shape[1]
        k_all = jnp.concatenate([k_past.astype(k.dtype), k], axis=1).reshape(B, p_len + L, DIFF_HEADS, 2, DIFF_DH)
        v_all = jnp.concatenate([v_past.astype(v.dtype), v], axis=1)
        allowed = jnp.arange(p_len + L)[None, :] <= (p_len + jnp.arange(L))[:, None]
        o = diff_core(qh, k_all, v_all, allowed, lam)
    b_out = (rmsnorm(o, subln_w) * (1.0 - lam_init)).reshape(B, L, DIFF_W)
    mixed = jnp.concatenate([a_out, b_out], axis=-1) * jax.nn.silu(gate)
    return mixed @ w_out, ext[:, -POOL_PREFIX:], k, v


def odd_mixer(h, conv_prefix, s0, w_in, w_out, conv_w, a_log, dt_bias, onorm_w, vnorm_w, w_s, b_s):
    B, L, _ = h.shape
    qkv, u_sg, v_sg, b, a, gate = split_cols(h @ w_in, (CONV_CH, SG_W, SG_W, DELTA_HEADS, DELTA_HEADS, MIX_W))
    ext = jnp.concatenate([conv_prefix.astype(qkv.dtype), qkv], axis=1)
    conv = sum(ext[:, j:j + L] * conv_w[j] for j in range(CONV_WIDTH))
    q, k, v = split_cols(jax.nn.silu(conv), (DELTA_W, DELTA_W, DELTA_W))
    q = l2norm(q.reshape(B, L, DELTA_HEADS, DELTA_DK)) * (DELTA_DK ** -0.5)
    k = l2norm(k.reshape(B, L, DELTA_HEADS, DELTA_DK))
    v = v.reshape(B, L, DELTA_HEADS, DELTA_DV)
    beta = jax.nn.sigmoid(b.astype(jnp.float32))
    logg = -jnp.exp(a_log.astype(jnp.float32)) * jax.nn.softplus(a.astype(jnp.float32) + dt_bias.astype(jnp.float32))
    o, s_new = gated_delta_rule(q, k, v, beta, logg, s0)
    c_out = rmsnorm(o, onorm_w).reshape(B, L, DELTA_W)
    vv = rmsnorm(jax.nn.gelu(v_sg).reshape(B, L, SG_GROUPS, SG_GC), vnorm_w)
    d_out = jax.nn.gelu(u_sg) * chunk_spatial_gate(vv, w_s, b_s).reshape(B, L, SG_W)
    mixed = jnp.concatenate([c_out, d_out], axis=-1) * jax.nn.silu(gate)
    return mixed @ w_out, ext[:, -(CONV_WIDTH - 1):], s_new, vv.reshape(B, L, SG_W)


def setup_inputs(seed: int = 0) -> dict:
    key = jax.random.key(seed)
    ks = jax.random.split(key, 32)
    f32 = jnp.float32
    n_pages = PAST_LEN // PAGE_SIZE
    n_used = DEC_BATCH * n_pages
    n_phys = n_used + max(1, n_used // 4)

    def nrm(k, shape, scale):
        return scale * jax.random.normal(k, shape, f32)

    NA, ND = N_ATTN_LAYERS, N_DELTA_LAYERS
    page_table = jax.random.permutation(ks[7], n_phys)[:n_used].reshape(DEC_BATCH, n_pages).astype(jnp.int32)
    return {
        'x_prompt': nrm(ks[0], (BATCH, SEQ, D_MODEL), 1.0),
        'x_sample': nrm(ks[1], (DEC_BATCH, DEC_SEQ, D_MODEL), 1.0),
        'cache_k': nrm(ks[2], (NA, n_phys, PAGE_SIZE, 2 * DIFF_HEADS, DIFF_DH), 1.0),
        'cache_v': nrm(ks[3], (NA, n_phys, PAGE_SIZE, DIFF_HEADS, DIFF_DV), 1.0),
        'state_pool': nrm(ks[4], (NA, DEC_BATCH, POOL_PREFIX, POOL_W), 1.0),
        'state_conv': nrm(ks[5], (ND, DEC_BATCH, CONV_WIDTH - 1, CONV_CH), 1.0),
        'state_delta': nrm(ks[6], (ND, DEC_BATCH, DELTA_HEADS, DELTA_DK, DELTA_DV), 0.1),
        'page_table': page_table,
        'norm_w': 1.0 + nrm(ks[8], (DEPTH, D_MODEL), 0.02),
        'w_in_e': nrm(ks[9], (NA, D_MODEL, IN_E), D_MODEL ** -0.5),
        'w_out_e': nrm(ks[10], (NA, MIX_W, D_MODEL), MIX_W ** -0.5),
        'pool_w': nrm(ks[11], (NA, POOL_GROUPS, POOL_GC, POOL_GC), POOL_GC ** -0.5),
        'pool_scale': 1.0 + nrm(ks[12], (NA, POOL_W), 0.1),
        'qn_w': 1.0 + nrm(ks[13], (NA, DIFF_DH), 0.02),
        'kn_w': 1.0 + nrm(ks[14], (NA, DIFF_DH), 0.02),
        'lam_qk': nrm(ks[15], (NA, 4, DIFF_DH), 0.1),
        'subln_w': 1.0 + nrm(ks[16], (NA, DIFF_DV), 0.02),
        'w_in_o': nrm(ks[17], (ND, D_MODEL, IN_O), D_MODEL ** -0.5),
        'w_out_o': nrm(ks[18], (ND, MIX_W, D_MODEL), MIX_W ** -0.5),
        'conv_w': nrm(ks[19], (ND, CONV_WIDTH, CONV_CH), CONV_WIDTH ** -0.5),
        'a_log': jnp.log(jax.random.uniform(ks[20], (ND, DELTA_HEADS), f32, 1.0, 16.0)),
        'dt_bias': nrm(ks[21], (ND, DELTA_HEADS), 0.1),
        'onorm_w': 1.0 + nrm(ks[22], (ND, DELTA_DV), 0.02),
        'vnorm_w': 1.0 + nrm(ks[23], (ND, SG_GROUPS, SG_GC), 0.02),
        'w_s': nrm(ks[24], (ND, SG_GROUPS, SG_CHUNK, SG_CHUNK), SG_CHUNK ** -0.5),
        'b_s': 1.0 + nrm(ks[25], (ND, SG_GROUPS, SG_CHUNK), 0.02),
    }


def reference(x_prompt, x_sample, cache_k, cache_v, state_pool, state_conv, state_delta, page_table,
              norm_w, w_in_e, w_out_e, pool_w, pool_scale, qn_w, kn_w, lam_qk, subln_w,
              w_in_o, w_out_o, conv_w, a_log, dt_bias, onorm_w, vnorm_w, w_s, b_s):
    bp, lp = x_prompt.shape[0], x_prompt.shape[1]
    bs, ls = x_sample.shape[0], x_sample.shape[1]
    past_len = page_table.shape[1] * cache_k.shape[2]
    pos_p = jnp.arange(lp)
    pos_s = past_len + jnp.arange(ls)
    xp, xs = x_prompt, x_sample
    kp_l, vp_l, ks_l, vs_l, poolp_l, pools_l = [], [], [], [], [], []
    convp_l, convs_l, deltap_l, deltas_l, sgv_l = [], [], [], [], []
    for layer in range(DEPTH):
        e = layer // 2
        hp = rmsnorm(xp, norm_w[layer])
        hs = rmsnorm(xs, norm_w[layer])
        if layer % 2 == 0:
            lam_init = 0.8 - 0.6 * math.exp(-0.3 * layer)
            w = (w_in_e[e], w_out_e[e], pool_w[e], pool_scale[e], qn_w[e], kn_w[e], lam_qk[e], subln_w[e], lam_init)
            dp, pool_p, k_p, v_p = even_mixer(hp, pos_p, jnp.zeros((bp, POOL_PREFIX, POOL_W), hp.dtype), None, None, *w)
            k_past = cache_k[e, page_table].reshape(bs, past_len, 2 * DIFF_HEADS, DIFF_DH)
            v_past = cache_v[e, page_table].reshape(bs, past_len, DIFF_HEADS, DIFF_DV)
            ds, pool_s, k_s, v_s = even_mixer(hs, pos_s, state_pool[e], k_past, v_past, *w)
            kp_l.append(k_p); vp_l.append(v_p); ks_l.append(k_s); vs_l.append(v_s)
            poolp_l.append(pool_p); pools_l.append(pool_s)
        else:
            w = (w_in_o[e], w_out_o[e], conv_w[e], a_log[e], dt_bias[e], onorm_w[e], vnorm_w[e], w_s[e], b_s[e])
            dp, conv_p, delta_p, _ = odd_mixer(hp, jnp.zeros((bp, CONV_WIDTH - 1, CONV_CH), hp.dtype),
                                               jnp.zeros((bp, DELTA_HEADS, DELTA_DK, DELTA_DV), hp.dtype), *w)
            ds, conv_s, delta_s, sgv_s = odd_mixer(hs, state_conv[e], state_delta[e], *w)
            convp_l.append(conv_p); convs_l.append(conv_s)
            deltap_l.append(delta_p); deltas_l.append(delta_s); sgv_l.append(sgv_s)
        xp = xp + dp
        xs = xs + ds
    new_k_prompt = jnp.stack(kp_l)
    new_v_prompt = jnp.stack(vp_l)
    new_k_sample = jnp.stack(ks_l)
    new_v_sample = jnp.stack(vs_l)
    new_pool_prompt = jnp.stack(poolp_l)
    new_pool_sample = jnp.stack(pools_l)
    new_conv_prompt = jnp.stack(convp_l)
    new_conv_sample = jnp.stack(convs_l)
    new_delta_prompt = jnp.stack(deltap_l)
    new_delta_sample = jnp.stack(deltas_l)
    new_sg_v_sample = jnp.stack(sgv_l)
    return (xp, xs, new_k_prompt, new_v_prompt, new_k_sample, new_v_sample,
            new_pool_prompt, new_pool_sample, new_conv_prompt, new_conv_sample,
            new_delta_prompt, new_delta_sample, new_sg_v_sample)
```

```python
import numpy as np
import concourse.bass as bass
import concourse.mybir as mybir
from concourse.alu_op_type import AluOpType as ALU

F32 = mybir.dt.float32
BF16 = mybir.dt.bfloat16
I32 = mybir.dt.int32
AF = mybir.ActivationFunctionType
AX = mybir.AxisListType

SEM_LIMIT = 30000


class Sched:
    ENGS = ("pe", "dve", "act", "pool", "sp")

    def __init__(self, nc):
        self.nc = nc
        self.prog = {e: [] for e in self.ENGS}
        self.cnt = {e: 0 for e in self.ENGS}
        self.sems = {e: [] for e in self.ENGS}
        self.seen = {e: {} for e in self.ENGS}
        self.last_w = {}
        self.readers = {}
        self.dma_sems = {}
        self.n_ops = 0
        self.epoch = 0
        self.key_epoch = {}
        self.fence_toks = []

    def fence(self):
        toks = []
        for e in self.ENGS:
            if self.cnt[e]:
                toks.append(("eng", e, self.cnt[e]))
        for key, (sem, v) in self.dma_sems.items():
            if v:
                toks.append(("dma", key, v))
        self.fence_toks = toks
        self.epoch += 1

    def _sem(self, e, idx):
        while len(self.sems[e]) <= idx:
            self.sems[e].append(self.nc.alloc_semaphore(name=f"s_{e}_{len(self.sems[e])}"))
        return self.sems[e][idx]

    def _need(self, eng, tok, waits):
        if tok is None:
            return
        if tok[0] == "eng":
            _, e, k = tok
            if e == "pe" and eng == "pe":
                return
            idx, val = (k - 1) // SEM_LIMIT, (k - 1) % SEM_LIMIT + 1
            skey = ("eng", e, idx)
            for (t2, e2, i2), v2 in self.seen[eng].items():
                if t2 == "eng" and e2 == e and i2 > idx:
                    return
            if self.seen[eng].get(skey, 0) >= val:
                return
            self.seen[eng][skey] = val
            waits[skey] = max(waits.get(skey, 0), val)
        else:
            _, key, v = tok
            skey = ("dma", key, 0)
            if self.seen[eng].get(skey, 0) >= v:
                return
            self.seen[eng][skey] = v
            waits[skey] = max(waits.get(skey, 0), v)

    def _deps(self, eng, reads, writes):
        waits = {}
        for kx in list(reads) + list(writes):
            if self.key_epoch.get(kx, -1) < self.epoch:
                self.key_epoch[kx] = self.epoch
                for t in self.fence_toks:
                    self._need(eng, t, waits)
        for r in reads:
            self._need(eng, self.last_w.get(r), waits)
        for w in writes:
            self._need(eng, self.last_w.get(w), waits)
            for t in self.readers.get(w, ()):
                self._need(eng, t, waits)
        out = []
        for skey, val in waits.items():
            if skey[0] == "eng":
                out.append((self._sem(skey[1], skey[2]), val))
            else:
                out.append((self.dma_sems[skey[1]][0], val))
        return out

    def _commit(self, tok, reads, writes):
        for r in reads:
            self.readers.setdefault(r, []).append(tok)
        for w in writes:
            self.last_w[w] = tok
            self.readers[w] = []

    def op(self, eng, fn, reads=(), writes=()):
        waits = self._deps(eng, reads, writes)
        self.cnt[eng] += 1
        k = self.cnt[eng]
        sem = self._sem(eng, (k - 1) // SEM_LIMIT)
        self.prog[eng].append((waits, fn, sem, 1))
        self._commit(("eng", eng, k), reads, writes)
        self.n_ops += 1

    def dma(self, eng, fn, reads=(), writes=(), key=None):
        if key is None:
            key = ("dmak", writes[0] if writes else reads[0])
        waits = self._deps(eng, reads, writes)
        if key not in self.dma_sems:
            self.dma_sems[key] = [self.nc.alloc_semaphore(name=f"d_{len(self.dma_sems)}"), 0]
        ent = self.dma_sems[key]
        ent[1] += 16
        self.prog[eng].append((waits, fn, ent[0], 16))
        self._commit(("dma", key, ent[1]), reads, writes)
        self.n_ops += 1

    def emit(self):
        nc = self.nc
        final = []
        for key, (sem, v) in self.dma_sems.items():
            final.append((sem, v))
        for e in self.ENGS:
            if self.cnt[e]:
                k = self.cnt[e]
                final.append((self._sem(e, (k - 1) // SEM_LIMIT), (k - 1) % SEM_LIMIT + 1))
        engmap = {"pe": "tensor", "dve": "vector", "act": "scalar", "pool": "gpsimd", "sp": "sync"}
        with nc.Block() as block:
            for e in self.ENGS:
                prog = self.prog[e]
                is_sp = e == "sp"

                def body(engine, prog=prog, is_sp=is_sp):
                    for waits, fn, sem, inc in prog:
                        for s, v in waits:
                            engine.wait_ge(s, v)
                        fn(engine).then_inc(sem, inc)
                    if is_sp:
                        for s, v in final:
                            engine.wait_ge(s, v)

                getattr(block, engmap[e])(body)


def bc(ap, shape_steps):
    a = list(ap.ap)
    return bass.AP(tensor=ap.tensor, offset=ap.offset, ap=[list(a[0])] + [list(x) for x in shape_steps])
import math
from contextlib import ExitStack
import numpy as np
from concourse.bass_utils import run_bass_kernel_spmd


EPS = 1e-6
NT = 16
NS = 16
IN_E = 3584
IN_O = 3852
NPHYS = 2560


def host_consts():
    c = {}
    idx = np.arange(128)
    c["c_ident"] = np.eye(128, dtype=np.float32)
    c["c_mle"] = (idx[:, None] <= idx[None, :]).astype(np.float32)
    c["c_mlt"] = (idx[:, None] < idx[None, :]).astype(np.float32)
    c["c_su"] = (idx[:, None] > idx[None, :]).astype(np.float32)
    c["c_ones"] = np.ones((128, 128), np.float32)
    half = 32
    inv = 1.0 / (10000.0 ** (np.arange(half, dtype=np.float32) / half))
    pos = (np.arange(NT)[None, :] * 128 + idx[:, None]).astype(np.float32)
    ang = pos[:, :, None] * inv[None, None, :]
    c["c_rope_p"] = np.concatenate([np.cos(ang), np.sin(ang)], axis=-1).astype(np.float32)
    angs = np.float32(2048.0) * inv
    c["c_rope_s"] = np.tile(np.concatenate([np.cos(angs), np.sin(angs)])[None, :], (16, 1)).astype(np.float32)
    pm = np.zeros((128, 12, 128), np.float32)
    for g, w in enumerate((2, 4, 8, 16)):
        for t in range(128):
            cnt0 = min(t + 1, w)
            for tp in range(max(0, t - w + 1), t + 1):
                pm[tp, g * 3 + 0, t] += 1.0 / cnt0
                pm[tp, g * 3 + 1, t] += 1.0 / w
            pm[t, g * 3 + 0, t] -= 1.0
            pm[t, g * 3 + 1, t] -= 1.0
            for tp in range(128):
                if tp - 128 >= t - w + 1:
                    pm[tp, g * 3 + 2, t] += 1.0 / w
    c["c_pool"] = pm
    i16 = np.eye(16, dtype=np.float32).reshape(1, 256)
    c["c_i16"] = np.tile(i16, (128, 1))
    sel = np.zeros((16, 16, 128), np.float32)
    for b in range(16):
        sel[b, b, :] = 1.0
    c["c_sel"] = sel
    c["c_iota"] = np.stack([idx + e * NPHYS * 128 for e in range(2)], axis=1).astype(np.float32)
    return c


def build():
    nc = bass.Bass("TRN2", target_bir_lowering=False)
    S = Sched(nc)
    es = ExitStack()

    def DI(name, shape, dt=F32):
        return nc.dram_tensor(name, list(shape), dt, kind="ExternalInput").ap()

    def DO(name, shape, dt=F32):
        return nc.dram_tensor(name, list(shape), dt, kind="ExternalOutput").ap()

    xp = DI("xp", [2048, 1024]); xs = DI("xs", [16, 1024])
    ck = DI("ck", [2 * NPHYS * 128, 768]); cv = DI("cv", [2 * NPHYS * 128, 768])
    spool = DI("spool", [2, 16, 15, 256]); sconv = DI("sconv", [2, 16, 3, 2304])
    sdelta = DI("sdelta", [2, 16, 6, 128, 128]); pt = DI("pt", [1, 256], I32)
    norm_w = DI("norm_w", [4, 1024]); w_in_e = DI("w_in_e", [2, 1024, IN_E]); w_out_e = DI("w_out_e", [2, 1024, 1024])
    pool_w = DI("pool_w", [2, 4, 64, 64]); pool_scale = DI("pool_scale", [2, 256])
    qn_w = DI("qn_w", [2, 64]); kn_w = DI("kn_w", [2, 64]); lam_qk = DI("lam_qk", [2, 256]); subln_w = DI("subln_w", [2, 128])
    w_in_o = DI("w_in_o", [2, 1024, IN_O]); w_out_o = DI("w_out_o", [2, 1024, 1024])
    conv_w = DI("conv_w", [2, 4, 2304]); a_log = DI("a_log", [2, 6]); dt_bias = DI("dt_bias", [2, 6])
    onorm_w = DI("onorm_w", [2, 128]); vnorm_w = DI("vnorm_w", [2, 256]); w_s = DI("w_s", [2, 4, 128, 128]); b_s = DI("b_s", [2, 4, 128])
    cst = {k: DI(k, v.shape) for k, v in host_consts().items()}

    y_p = DO("y_p", [2048, 1024]); y_s = DO("y_s", [16, 1024])
    nk_p = DO("nk_p", [2, 2048, 768]); nv_p = DO("nv_p", [2, 2048, 768])
    nk_s = DO("nk_s", [2, 16, 768]); nv_s = DO("nv_s", [2, 16, 768])
    npool_p = DO("npool_p", [2, 15, 256]); npool_s = DO("npool_s", [2, 16, 15, 256])
    nconv_p = DO("nconv_p", [2, 3, 2304]); nconv_s = DO("nconv_s", [2, 16, 3, 2304])
    ndelta_p = DO("ndelta_p", [2, 6, 128, 128]); ndelta_s = DO("ndelta_s", [2, 16, 6, 128, 128])
    nsgv_s = DO("nsgv_s", [2, 16, 256])
    scr = nc.dram_tensor("scr", [2, 6, 16, 772], F32, kind="Internal").ap()

    used_names = {}

    def T(stack, name, shape, dt=F32):
        n = used_names.get(name, 0)
        used_names[name] = n + 1
        return stack.enter_context(nc.sbuf_tensor(name if n == 0 else f"{name}_{n}", list(shape), dt))

    G = es
    pb = [nc.alloc_psum_tensor(f"pb{i}", [128, 512], F32) for i in range(6)]
    pT1 = nc.alloc_psum_tensor("pT1", [128, 8, 128], BF16)
    pT2 = nc.alloc_psum_tensor("pT2", [128, 8, 128], BF16)
    PB = [f"pb{i}" for i in range(6)]

    identf = T(G, "identf", [128, 128]); identb = T(G, "identb", [128, 128], BF16)
    mle = T(G, "mle", [128, 128]); mlt = T(G, "mlt", [128, 128]); su = T(G, "su", [128, 128]); ones = T(G, "ones", [128, 128])
    ropep = T(G, "ropep", [128, NT, 64]); ropes = T(G, "ropes", [16, 64])
    i16 = T(G, "i16", [128, 256]); iota = T(G, "iota", [128, 2])
    nwbc = T(G, "nwbc", [128, 1024])
    xs_sb = T(G, "xs_sb", [16, 1024])
    ptb = T(G, "ptb", [128, 256], I32); gidx = T(G, "gidx", [128, 2, 256], I32)
    wout = T(G, "wout", [128, 8, 1024], BF16)
    win = T(G, "win", [128, 8, IN_O], BF16)
    ssn = T(G, "ssn", [128, 4]); hb = T(G, "hb", [128, 1024], BF16); hT = T(G, "hT", [128, 8, 128], BF16)
    mT = T(G, "mT", [128, 8, 128], BF16)
    pre = T(G, "pre", [128, 1024]); sgt = T(G, "sgt", [128, 1024], BF16); mixed = T(G, "mixed", [128, 1024], BF16)
    junk = mixed
    otok = T(G, "otok", [128, 6, 128]); wk768 = T(G, "wk768", [128, 768]); s6 = T(G, "s6", [128, 16])

    def ld(eng, out, in_, wkey):
        S.dma(eng, lambda e: e.dma_start(out=out, in_=in_), writes=[wkey])

    for nm, t in (("c_ident", identf), ("c_mle", mle), ("c_mlt", mlt), ("c_su", su), ("c_ones", ones),
                  ("c_rope_p", ropep), ("c_rope_s", ropes), ("c_i16", i16), ("c_iota", iota)):
        ld("sp", t[:], cst[nm], {"c_ident": "identf", "c_mle": "mle", "c_mlt": "mlt", "c_su": "su", "c_ones": "ones", "c_rope_p": "ropep",
                                 "c_rope_s": "ropes", "c_i16": "i16", "c_iota": "iota"}[nm])
    KN = lambda t: t.name
    S.op("dve", lambda e: e.tensor_copy(out=identb[:], in_=identf[:]), reads=["identf"], writes=["identb"])
    ld("sp", xs_sb[:], xs, "xs_sb")
    S.dma("sp", lambda e: e.dma_start(out=ptb[:], in_=pt.partition_broadcast(128)), writes=["ptb"])
    for e_ in range(2):
        S.op("dve", lambda e, e_=e_: e.tensor_scalar(out=gidx[:, e_, :], in0=ptb[:], scalar1=128.0, scalar2=iota[:, e_:e_ + 1],
                                                    op0=ALU.mult, op1=ALU.add), reads=["ptb", "iota"], writes=["gidx"])

    def V(ap, steps):
        return bc(ap, steps)

    def rstd_from_ss(P, ss_ap, out_ap, n, width, rk, wk):
        S.op("act", lambda e: e.activation(out=out_ap, in_=ss_ap, func=AF.Ln, scale=1.0 / n, bias=EPS), reads=rk, writes=wk)
        S.op("act", lambda e: e.activation(out=out_ap, in_=out_ap, func=AF.Exp, scale=-0.5), reads=wk, writes=wk)

    def head(P, x_ap, xkey):
        S.op("act", lambda e: e.activation(out=junk[:P], in_=x_ap, func=AF.Square, accum_out=ssn[:P, 0:1]), reads=[xkey], writes=["mixed", "ssn"])
        rstd_from_ss(P, ssn[:P, 0:1], ssn[:P, 1:2], 1024, 1, ["ssn"], ["ssn"])
        S.op("dve", lambda e: e.scalar_tensor_tensor(out=hb[:P], in0=x_ap, scalar=ssn[:P, 1:2], in1=nwbc[:P], op0=ALU.mult, op1=ALU.mult),
             reads=[xkey, "ssn", "nwbc"], writes=["hb"])
        for kc in range(8):
            S.op("pe", lambda e, kc=kc: e.transpose(out=pT1[:, kc, :P], in_=hb[:P, kc * 128:(kc + 1) * 128], identity=identb[:P, :P]),
                 reads=["hb", "identb"], writes=["pT1"])
        S.op("act", lambda e: e.copy(out=hT[:, :, :P], in_=pT1[:, :, :P]), reads=["pT1"], writes=["hT"])

    def inproj(P, c0, n, ps_ap, pkey):
        for kc in range(8):
            S.op("pe", lambda e, kc=kc: e.matmul(ps_ap, lhsT=hT[:, kc, :P], rhs=win[:, kc, c0:c0 + n], start=(kc == 0), stop=(kc == 7)),
                 reads=["hT", "win"], writes=[pkey])

    def tail(P, x_ap, xkey, xo_ap, xokey):
        S.op("dve", lambda e: e.tensor_tensor(out=mixed[:P], in0=pre[:P], in1=sgt[:P], op=ALU.mult), reads=["pre", "sgt"], writes=["mixed"])
        for kc in range(8):
            S.op("pe", lambda e, kc=kc: e.transpose(out=pT1[:, kc, :P], in_=mixed[:P, kc * 128:(kc + 1) * 128], identity=identb[:P, :P]),
                 reads=["mixed", "identb"], writes=["pT1"])
        S.op("act", lambda e: e.copy(out=mT[:, :, :P], in_=pT1[:, :, :P]), reads=["pT1"], writes=["mT"])
        for hf in range(2):
            for kc in range(8):
                S.op("pe", lambda e, kc=kc, hf=hf: e.matmul(pb[hf][:P, :], lhsT=mT[:, kc, :P], rhs=wout[:, kc, hf * 512:(hf + 1) * 512],
                                                          start=(kc == 0), stop=(kc == 7)), reads=["mT", "wout"], writes=[PB[hf]])
            S.op("dve", lambda e, hf=hf: e.tensor_tensor(out=xo_ap[:, hf * 512:(hf + 1) * 512], in0=pb[hf][:P, :], in1=x_ap[:, hf * 512:(hf + 1) * 512], op=ALU.add),
                 reads=[PB[hf], xkey], writes=[xokey])

    def gate_silu(P, c0):
        for hf in range(2):
            inproj(P, c0 + hf * 512, 512, pb[2 + hf][:P, :], PB[2 + hf])
            S.op("act", lambda e, hf=hf: e.activation(out=sgt[:P, hf * 512:(hf + 1) * 512], in_=pb[2 + hf][:P, :], func=AF.Silu),
                 reads=[PB[2 + hf]], writes=["sgt"])

    def headnorm(P, wbc, wkey, factor, c0):
        S.op("dve", lambda e: e.tensor_tensor(out=wk768[:P], in0=otok[:P].rearrange("p h d -> p (h d)"), in1=otok[:P].rearrange("p h d -> p (h d)"), op=ALU.mult),
             reads=["otok"], writes=["wk768"])
        S.op("dve", lambda e: e.tensor_reduce(out=s6[:P, 0:6], in_=wk768[:P].rearrange("p (h d) -> p h d", d=128), axis=AX.X, op=ALU.add),
             reads=["wk768"], writes=["s6"])
        rstd_from_ss(P, s6[:P, 0:6], s6[:P, 6:12], 128, 6, ["s6"], ["s6"])
        S.op("dve", lambda e: e.tensor_tensor(out=otok[:P], in0=otok[:P], in1=V(s6[:P, 6:12], [[1, 6], [0, 128]]), op=ALU.mult),
             reads=["otok", "s6"], writes=["otok"])
        S.op("dve", lambda e: e.scalar_tensor_tensor(out=pre[:P, c0:c0 + 768].rearrange("p (h d) -> p h d", d=128), in0=otok[:P], scalar=float(factor),
                                                    in1=V(wbc[:P, 0:128], [[0, 6], [1, 128]]), op0=ALU.mult, op1=ALU.mult),
             reads=["otok", wkey], writes=["pre"])

    def qk_norm_rope(P, src_aps, srckeys, wbc, wkey, cos_ap, sin_ap, ropekey, dst, dkey, W):
        sq, xn, t1, t2, s12 = W
        for i, (sa, sk) in enumerate(zip(src_aps, srckeys)):
            S.op("act", lambda e, sa=sa, i=i: e.activation(out=sq[:P, i * 384:(i + 1) * 384], in_=sa, func=AF.Square), reads=[sk], writes=["qk_sq"])
        S.op("dve", lambda e: e.tensor_reduce(out=s12[:P, 0:12], in_=sq[:P].rearrange("p (g d) -> p g d", d=64), axis=AX.X, op=ALU.add),
             reads=["qk_sq"], writes=["qk_s12"])
        rstd_from_ss(P, s12[:P, 0:12], s12[:P, 12:24], 64, 12, ["qk_s12"], ["qk_s12"])
        for i, (sa, sk) in enumerate(zip(src_aps, srckeys)):
            S.op("dve", lambda e, sa=sa, i=i: e.tensor_tensor(out=xn[:P, i * 6:(i + 1) * 6, :], in0=sa.rearrange("p (g d) -> p g d", d=64),
                                                            in1=V(s12[:P, 12 + i * 6:18 + i * 6], [[1, 6], [0, 64]]), op=ALU.mult),
                 reads=[sk, "qk_s12"], writes=["qk_xn"])
        S.op("dve", lambda e: e.tensor_tensor(out=xn[:P], in0=xn[:P], in1=V(wbc[:P, 0:64], [[0, 12], [1, 64]]), op=ALU.mult),
             reads=["qk_xn", wkey], writes=["qk_xn"])
        cosb = V(cos_ap, [[0, 12], [1, 32]]); sinb = V(sin_ap, [[0, 12], [1, 32]])
        x1 = xn[:P, :, 0:32]; x2 = xn[:P, :, 32:64]
        S.op("dve", lambda e: e.tensor_tensor(out=t1[:P], in0=x1, in1=cosb, op=ALU.mult), reads=["qk_xn", ropekey], writes=["qk_t1"])
        S.op("dve", lambda e: e.tensor_tensor(out=t2[:P], in0=x2, in1=sinb, op=ALU.mult), reads=["qk_xn", ropekey], writes=["qk_t2"])
        S.op("dve", lambda e: e.tensor_tensor(out=dst[:P, :, 0:32], in0=t1[:P], in1=t2[:P], op=ALU.subtract), reads=["qk_t1", "qk_t2"], writes=[dkey])
        S.op("dve", lambda e: e.tensor_tensor(out=t1[:P], in0=x2, in1=cosb, op=ALU.mult), reads=["qk_xn", ropekey, dkey], writes=["qk_t1"])
        S.op("dve", lambda e: e.tensor_tensor(out=t2[:P], in0=x1, in1=sinb, op=ALU.mult), reads=["qk_xn", ropekey, dkey], writes=["qk_t2"])
        S.op("dve", lambda e: e.tensor_tensor(out=dst[:P, :, 32:64], in0=t1[:P], in1=t2[:P], op=ALU.add), reads=["qk_t1", "qk_t2"], writes=[dkey])

    def load_weights(layer):
        e_ = layer // 2
        if layer % 2 == 0:
            src, n, so = w_in_e[e_], IN_E, w_out_e[e_]
        else:
            src, n, so = w_in_o[e_], IN_O, w_out_o[e_]
        for kc in range(8):
            S.dma("pool", lambda e, kc=kc: e.dma_start(out=win[:, kc, 0:n], in_=src[kc * 128:(kc + 1) * 128, :]), writes=["win"])
        S.dma("pool", lambda e: e.dma_start(out=wout[:], in_=so.rearrange("(kc p) n -> p kc n", p=128)), writes=["wout"])
        S.dma("sp", lambda e: e.dma_start(out=nwbc[:], in_=norm_w[layer:layer + 1, :].partition_broadcast(128).rearrange("p o n -> p (o n)")), writes=["nwbc"])

    def even_layer(layer):
        e_ = layer // 2
        lam_init = 0.8 - 0.6 * math.exp(-0.3 * layer)
        load_weights(layer)
        with ExitStack() as L:
            pwt = T(L, "pwt", [64, 4, 64]); psc = T(L, "psc", [128, 256]); qnb = T(L, "qnb", [128, 64]); knb = T(L, "knb", [128, 64])
            lqb = T(L, "lqb", [128, 256]); slb = T(L, "slb", [128, 128]); lamc = T(L, "lamc", [128, 8])
            S.dma("sp", lambda e: e.dma_start(out=pwt[:], in_=pool_w[e_].rearrange("g c d -> c g d")), writes=["pwt"])
            for t_, src_, kn_ in ((psc, pool_scale, "psc"), (qnb, qn_w, "qnb"), (knb, kn_w, "knb"), (lqb, lam_qk, "lqb"), (slb, subln_w, "slb")):
                S.dma("sp", lambda e, t_=t_, src_=src_: e.dma_start(out=t_[:], in_=src_[e_:e_ + 1, :].partition_broadcast(128).rearrange("p o n -> p (o n)")), writes=[kn_])
            S.op("dve", lambda e: e.tensor_tensor(out=wk768[:, 0:64], in0=lqb[:, 0:64], in1=lqb[:, 64:128], op=ALU.mult), reads=["lqb"], writes=["wk768"])
            S.op("dve", lambda e: e.tensor_tensor(out=wk768[:, 64:128], in0=lqb[:, 128:192], in1=lqb[:, 192:256], op=ALU.mult), reads=["lqb"], writes=["wk768"])
            S.op("dve", lambda e: e.tensor_reduce(out=lamc[:, 0:2], in_=wk768[:, 0:128].rearrange("p (a d) -> p a d", d=64), axis=AX.X, op=ALU.add), reads=["wk768"], writes=["lamc"])
            S.op("act", lambda e: e.activation(out=lamc[:, 2:4], in_=lamc[:, 0:2], func=AF.Exp), reads=["lamc"], writes=["lamc"])
            S.op("dve", lambda e: e.tensor_tensor(out=lamc[:, 4:5], in0=lamc[:, 2:3], in1=lamc[:, 3:4], op=ALU.subtract), reads=["lamc"], writes=["lamc"])
            S.op("dve", lambda e: e.tensor_scalar(out=lamc[:, 5:6], in0=lamc[:, 4:5], scalar1=float(lam_init), scalar2=-1.0, op0=ALU.add, op1=ALU.mult), reads=["lamc"], writes=["lamc"])

            qkW = (T(L, "qk_sq", [128, 768]), T(L, "qk_xn", [128, 12, 64]), T(L, "qk_t1", [128, 12, 32]), T(L, "qk_t2", [128, 12, 32]), T(L, "qk_s12", [128, 24]))
            with ExitStack() as P1:
                spin = T(P1, "spin", [16, IN_E]); qs = T(P1, "qs", [16, 12, 64]); ks = T(P1, "ks", [16, 12, 64])
                PGU = 2
                NU = 16 // PGU
                Kp = [T(P1, f"Kp{i}", [128, PGU, 768], BF16) for i in range(2)]
                Vp = [T(P1, f"Vp{i}", [128, PGU, 772], BF16) for i in range(2)]
                qbc = T(P1, "qbc", [128, 768], BF16); sc = T(P1, "sc", [128, PGU, 12]); pexp = T(P1, "pexp", [128, PGU, 2, 6], BF16)
                sel = T(P1, "sel", [16, 16, 128])
                ld("sp", sel[:], cst["c_sel"], "sel")
                oall = [T(P1, f"oall{c}", [6, 772]) for c in range(2)]
                otk = [T(P1, f"otk{c}", [16, 6, 129]) for c in range(2)]
                spl = T(P1, "spl", [16, 26, 64]); pl = T(P1, "pl", [16, 256]); plT = T(P1, "plT", [64, 4, 16])
                sm = T(P1, "sm", [16, 64]); tmpv = T(P1, "tmpv", [16, 6, 128]); num = [T(P1, f"num{c}", [16, 6, 128]) for c in range(2)]
                for i in range(2):
                    S.op("pool", lambda e, i=i: e.memset(Vp[i][:, :, 768:772], 1.0), writes=[f"Vp{i}"])
                head(16, xs_sb[:16], "xs_sb")
                for blk in range(7):
                    inproj(16, blk * 512, 512, pb[blk % 4][:16, :], PB[blk % 4])
                    S.op("act" if blk % 2 else "dve",
                         (lambda e, blk=blk: e.copy(out=spin[:, blk * 512:(blk + 1) * 512], in_=pb[blk % 4][:16, :])) if blk % 2 else
                         (lambda e, blk=blk: e.tensor_copy(out=spin[:, blk * 512:(blk + 1) * 512], in_=pb[blk % 4][:16, :])),
                         reads=[PB[blk % 4]], writes=["spin"])
                S.op("act", lambda e: e.activation(out=sgt[:16], in_=spin[:, 2560:3584], func=AF.Silu), reads=["spin"], writes=["sgt"])
                qk_norm_rope(16, [spin[:, 256:640], spin[:, 640:1024]], ["spin", "spin"], qnb, "qnb", ropes[:16, 0:32], ropes[:16, 32:64], "ropes", qs, "qs", qkW)
                qk_norm_rope(16, [spin[:, 1024:1408], spin[:, 1408:1792]], ["spin", "spin"], knb, "knb", ropes[:16, 0:32], ropes[:16, 32:64], "ropes", ks, "ks", qkW)
                S.dma("sp", lambda e: e.dma_start(out=nk_s[e_], in_=ks[:].rearrange("p g d -> p (g d)")), reads=["ks"])
                S.dma("sp", lambda e: e.dma_start(out=nv_s[e_], in_=spin[:, 1792:2560]), reads=["spin"])
                S.op("dve", lambda e: e.tensor_scalar(out=qs[:], in0=qs[:], scalar1=0.125, scalar2=None, op0=ALU.mult), reads=["qs"], writes=["qs"])
                S.dma("sp", lambda e: e.dma_start(out=npool_s[e_, :, 0:14, :], in_=spool[e_, :, 1:15, :]), key="npool_s1")
                ro = 0
                for g, w in enumerate((2, 4, 8, 16)):
                    S.dma("sp", lambda e, g=g, w=w, ro=ro: e.dma_start(out=spl[:, ro:ro + w - 1, :], in_=spool[e_, :, 16 - w:15, g * 64:(g + 1) * 64]), writes=["spl"])
                    ro += w - 1
                S.dma("sp", lambda e: e.dma_start(out=npool_s[e_, :, 14, :], in_=spin[:, 0:256]), reads=["spin"], key="npool_s2")
                ro = 0
                for g, w in enumerate((2, 4, 8, 16)):
                    S.op("dve", lambda e, g=g, w=w, ro=ro: e.tensor_reduce(out=pl[:, g * 64:(g + 1) * 64], in_=spl[:, ro:ro + w - 1, :].rearrange("p r c -> p c r"),
                                                                  axis=AX.X, op=ALU.add), reads=["spl"], writes=["pl"])
                    ro += w - 1
                    S.op("dve", lambda e, g=g, w=w: e.tensor_scalar(out=pl[:, g * 64:(g + 1) * 64], in0=pl[:, g * 64:(g + 1) * 64], scalar1=1.0 / w, scalar2=None, op0=ALU.mult),
                         reads=["pl"], writes=["pl"])
                    S.op("dve", lambda e, g=g, w=w: e.scalar_tensor_tensor(out=pl[:, g * 64:(g + 1) * 64], in0=spin[:, g * 64:(g + 1) * 64], scalar=1.0 / w - 1.0,
                                                                         in1=pl[:, g * 64:(g + 1) * 64], op0=ALU.mult, op1=ALU.add), reads=["pl", "spin"], writes=["pl"])
                for g in range(4):
                    S.op("pe", lambda e, g=g: e.transpose(out=pb[4][:64, g * 16:(g + 1) * 16], in_=pl[:, g * 64:(g + 1) * 64], identity=identf[:16, :16]),
                         reads=["pl", "identf"], writes=[PB[4]])
                S.op("act", lambda e: e.copy(out=plT[:].rearrange("p g b -> p (g b)"), in_=pb[4][:64, 0:64]), reads=[PB[4]], writes=["plT"])
                for g in range(4):
                    S.op("pe", lambda e, g=g: e.matmul(pb[5][:16, g * 64:(g + 1) * 64], lhsT=plT[:, g, :], rhs=pwt[:, g, :], start=True, stop=True),
                         reads=["plT", "pwt"], writes=[PB[5]])
                S.op("dve", lambda e: e.tensor_tensor(out=pre[:16, 0:256], in0=pb[5][:16, 0:256], in1=psc[:16], op=ALU.mult), reads=[PB[5], "psc"], writes=["pre"])
                for unit in range(16 * NU):
                    b, hf = unit // NU, unit % NU
                    par = unit % 2
                    for pg in range(PGU):
                        col = b * 16 + hf * PGU + pg
                        S.dma("pool", lambda e, pg=pg, col=col, par=par: e.indirect_dma_start(
                            out=Kp[par][:, pg, :], out_offset=None, in_=ck,
                            in_offset=bass.IndirectOffsetOnAxis(ap=gidx[:, e_, col:col + 1], axis=0)), reads=["gidx"], writes=[f"Kp{par}"])
                        S.dma("pool", lambda e, pg=pg, col=col, par=par: e.indirect_dma_start(
                            out=Vp[par][:, pg, 0:768], out_offset=None, in_=cv,
                            in_offset=bass.IndirectOffsetOnAxis(ap=gidx[:, e_, col:col + 1], axis=0)), reads=["gidx"], writes=[f"Vp{par}"])
                    if hf == 0:
                        for h2 in range(2):
                            S.op("pe", lambda e, b=b, h2=h2: e.matmul(pb[4 + h2][:, 0:384], lhsT=sel[:, b, :], rhs=qs[:].rearrange("p g d -> p (g d)")[:, h2 * 384:(h2 + 1) * 384],
                                                                    start=True, stop=True), reads=["sel", "qs"], writes=[PB[4 + h2]])
                            S.op("act", lambda e, h2=h2: e.copy(out=qbc[:, h2 * 384:(h2 + 1) * 384], in_=pb[4 + h2][:, 0:384]), reads=[PB[4 + h2]], writes=["qbc"])
                    S.op("dve", lambda e, par=par: e.tensor_tensor(out=Kp[par][:], in0=Kp[par][:], in1=V(qbc[:], [[0, PGU], [1, 768]]), op=ALU.mult),
                         reads=[f"Kp{par}", "qbc"], writes=[f"Kp{par}"])
                    S.op("dve", lambda e, par=par: e.tensor_reduce(out=sc[:].rearrange("p a g -> p (a g)"), in_=Kp[par][:].rearrange("p a (g d) -> p (a g) d", d=64),
                                                                 axis=AX.X, op=ALU.add), reads=[f"Kp{par}"], writes=["sc"])
                    S.op("act", lambda e: e.activation(out=V(pexp[:, 0, 0, 0:1], [[12, PGU], [1, 6], [6, 2]]), in_=sc[:].rearrange("p a (h c) -> p a h c", c=2), func=AF.Exp),
                         reads=["sc"], writes=["pexp"])
                    for c in range(2):
                        for h2 in range(2):
                            n = 384 if h2 == 0 else 385
                            for pg in range(PGU):
                                S.op("pe", lambda e, c=c, h2=h2, pg=pg, n=n, par=par, hf=hf: e.matmul(
                                    pb[c * 2 + h2][:6, 0:n], lhsT=pexp[:, pg, c, :], rhs=Vp[par][:, pg, h2 * 384:h2 * 384 + n],
                                    start=(hf == 0 and pg == 0), stop=(hf == NU - 1 and pg == PGU - 1)), reads=["pexp", f"Vp{par}"], writes=[PB[c * 2 + h2]])
                    if hf == NU - 1:
                        for c in range(2):
                            for h2 in range(2):
                                n = 384 if h2 == 0 else 385
                                S.op("act" if h2 else "dve",
                                     (lambda e, c=c, h2=h2, n=n, b=b: e.copy(out=oall[c][:, h2 * 384:h2 * 384 + n], in_=pb[c * 2 + h2][:6, 0:n])) if h2 else
                                     (lambda e, c=c, h2=h2, n=n, b=b: e.tensor_copy(out=oall[c][:, h2 * 384:h2 * 384 + n], in_=pb[c * 2 + h2][:6, 0:n])),
                                     reads=[PB[c * 2 + h2]], writes=[f"oall{c}"])
                            S.dma("sp", lambda e, c=c, b=b: e.dma_start(out=scr[c, :, b, :], in_=oall[c][:]), reads=[f"oall{c}"], writes=[f"scr{c}"])
                for c in range(2):
                    for h in range(6):
                        S.dma("sp", lambda e, c=c, h=h: e.dma_start(out=otk[c][:, h, 0:128], in_=scr[c, h, :, h * 128:(h + 1) * 128]), reads=[f"scr{c}"], writes=[f"otk{c}"])
                        S.dma("sp", lambda e, c=c, h=h: e.dma_start(out=otk[c][:, h, 128:129], in_=scr[c, h, :, 768:769], allow_slow_non_contiguous=True), reads=[f"scr{c}"], writes=[f"otk{c}"])
                S.op("dve", lambda e: e.tensor_tensor(out=wk768[:16], in0=qs[:].rearrange("p g d -> p (g d)"), in1=ks[:].rearrange("p g d -> p (g d)"), op=ALU.mult),
                     reads=["qs", "ks"], writes=["wk768"])
                S.op("dve", lambda e: e.tensor_reduce(out=sm[:, 0:12], in_=wk768[:16].rearrange("p (g d) -> p g d", d=64), axis=AX.X, op=ALU.add), reads=["wk768"], writes=["sm"])
                S.op("act", lambda e: e.activation(out=sm[:, 12:24], in_=sm[:, 0:12], func=AF.Exp), reads=["sm"], writes=["sm"])
                vs3 = spin[:, 1792:2560].rearrange("p (h d) -> p h d", d=128)
                for c in range(2):
                    pn = V(sm[:, 12 + c:13 + c], [[2, 6], [0, 128]])
                    S.op("dve", lambda e, pn=pn: e.tensor_tensor(out=tmpv[:], in0=vs3, in1=pn, op=ALU.mult), reads=["spin", "sm"], writes=["tmpv"])
                    S.op("dve", lambda e, c=c: e.tensor_tensor(out=num[c][:], in0=tmpv[:], in1=otk[c][:, :, 0:128], op=ALU.add), reads=["tmpv", f"otk{c}"], writes=[f"num{c}"])
                    S.op("dve", lambda e, c=c: e.tensor_tensor(out=sm[:, 24 + c * 6:30 + c * 6], in0=otk[c][:, :, 128], in1=V(sm[:, 12 + c:13 + c], [[2, 6]]), op=ALU.add),
                         reads=["sm", f"otk{c}"], writes=["sm"])
                S.op("dve", lambda e: e.reciprocal(out=sm[:, 36:48], in_=sm[:, 24:36]), reads=["sm"], writes=["sm"])
                S.op("dve", lambda e: e.tensor_scalar(out=sm[:, 42:48], in0=sm[:, 42:48], scalar1=lamc[:16, 5:6], scalar2=None, op0=ALU.mult), reads=["sm", "lamc"], writes=["sm"])
                for c in range(2):
                    S.op("dve", lambda e, c=c: e.tensor_tensor(out=num[c][:], in0=num[c][:], in1=V(sm[:, 36 + c * 6:42 + c * 6], [[1, 6], [0, 128]]), op=ALU.mult),
                         reads=[f"num{c}", "sm"], writes=[f"num{c}"])
                S.op("dve", lambda e: e.tensor_tensor(out=otok[:16], in0=num[0][:], in1=num[1][:], op=ALU.add), reads=["num0", "num1"], writes=["otok"])
                headnorm(16, slb, "slb", 1.0 - lam_init, 256)
                tail(16, xs_sb[:16], "xs_sb", xs_sb[:16], "xs_sb")
            S.fence()
            with ExitStack() as P2:
                KT = T(P2, "KT", [128, 6, 2048], BF16); Va = T(P2, "Va", [128, NT, 6, 129], BF16)
                ub = [T(P2, f"ub{i}", [128, 256]) for i in range(2)]
                qr = T(P2, "qr", [128, 12, 64]); kr = T(P2, "kr", [128, 12, 64]); vf = wk768
                qrb = T(P2, "qrb", [128, 768], BF16); krb = T(P2, "krb", [128, 768], BF16); qT = T(P2, "qT", [128, 6, 128], BF16)
                plT2 = T(P2, "plT2", [64, 4, 128]); ex = [T(P2, f"ex{i}", [128, 4, 128], BF16) for i in range(2)]
                rc = T(P2, "rc", [128, 4]); tmpo = T(P2, "tmpo", [128, 128])
                xin = [T(P2, f"xin{i}", [128, 1024]) for i in range(2)]
                xout = [pre] * 2
                S.op("pool", lambda e: e.memset(Va[:, :, :, 128:129], 1.0), writes=["Va"])
                pool_m = T(P2, "pool_m", [128, 12, 128])
                ld("sp", pool_m[:], cst["c_pool"], "pool_m")
                for i in range(NT):
                    par = i % 2
                    src = xp if layer == 0 else y_p
                    S.dma("sp", lambda e, i=i, par=par, src=src: e.dma_start(out=xin[par][:], in_=src[i * 128:(i + 1) * 128, :]), reads=[f"yd{i}"], writes=[f"xin{par}"])
                    head(128, xin[par][:], f"xin{par}")
                    inproj(128, 0, 256, pb[0][:, 0:256], PB[0])
                    S.op("act", lambda e, par=par: e.copy(out=ub[par][:], in_=pb[0][:, 0:256]), reads=[PB[0]], writes=[f"ub{par}"])
                    if i == NT - 1:
                        S.dma("sp", lambda e, par=par: e.dma_start(out=npool_p[e_], in_=ub[par][113:128, :]), reads=[f"ub{par}"])
                    for g in range(4):
                        mi = g * 3 + (0 if i == 0 else 1)
                        S.op("pe", lambda e, g=g, mi=mi, par=par: e.matmul(pb[1][:64, g * 128:(g + 1) * 128], lhsT=ub[par][:, g * 64:(g + 1) * 64], rhs=pool_m[:, mi, :],
                                                                         start=True, stop=(i == 0)), reads=[f"ub{par}", "pool_m"], writes=[PB[1]])
                        if i > 0:
                            S.op("pe", lambda e, g=g, par=par: e.matmul(pb[1][:64, g * 128:(g + 1) * 128], lhsT=ub[1 - par][:, g * 64:(g + 1) * 64], rhs=pool_m[:, g * 3 + 2, :],
                                                                      start=False, stop=True), reads=[f"ub{1 - par}", "pool_m"], writes=[PB[1]])
                    S.op("act", lambda e: e.copy(out=plT2[:].rearrange("p g t -> p (g t)"), in_=pb[1][:64, :]), reads=[PB[1]], writes=["plT2"])
                    for g in range(4):
                        S.op("pe", lambda e, g=g: e.matmul(pb[0][:, 256 + g * 64:256 + (g + 1) * 64], lhsT=plT2[:, g, :], rhs=pwt[:, g, :], start=True, stop=True),
                             reads=["plT2", "pwt"], writes=[PB[0]])
                    S.op("dve", lambda e: e.tensor_tensor(out=pre[:, 0:256], in0=pb[0][:, 256:512], in1=psc[:], op=ALU.mult), reads=[PB[0], "psc"], writes=["pre"])
                    for h2 in range(2):
                        inproj(128, 256 + h2 * 384, 384, pb[2 + h2][:, 0:384], PB[2 + h2])
                    qk_norm_rope(128, [pb[2][:, 0:384], pb[3][:, 0:384]], [PB[2], PB[3]], qnb, "qnb", ropep[:, i, 0:32], ropep[:, i, 32:64], "ropep", qr, "qr", qkW)
                    S.op("act", lambda e: e.activation(out=qrb[:], in_=qr[:].rearrange("p g d -> p (g d)"), func=AF.Copy, scale=0.125), reads=["qr"], writes=["qrb"])
                    for h in range(6):
                        S.op("pe", lambda e, h=h: e.transpose(out=pT2[:, h, :], in_=qrb[:, h * 128:(h + 1) * 128], identity=identb[:]), reads=["qrb", "identb"], writes=["pT2"])
                    S.op("dve", lambda e: e.tensor_copy(out=qT[:], in_=pT2[:, 0:6, :]), reads=["pT2"], writes=["qT"])
                    for h2 in range(2):
                        inproj(128, 1024 + h2 * 384, 384, pb[4 + h2][:, 0:384], PB[4 + h2])
                    qk_norm_rope(128, [pb[4][:, 0:384], pb[5][:, 0:384]], [PB[4], PB[5]], knb, "knb", ropep[:, i, 0:32], ropep[:, i, 32:64], "ropep", kr, "kr", qkW)
                    S.dma("sp", lambda e, i=i: e.dma_start(out=nk_p[e_, i * 128:(i + 1) * 128, :], in_=kr[:].rearrange("p g d -> p (g d)")), reads=["kr"])
                    S.op("act", lambda e: e.copy(out=krb[:], in_=kr[:].rearrange("p g d -> p (g d)")), reads=["kr"], writes=["krb"])
                    for h in range(6):
                        S.op("pe", lambda e, h=h: e.transpose(out=pT2[:, h, :], in_=krb[:, h * 128:(h + 1) * 128], identity=identb[:]), reads=["krb", "identb"], writes=["pT2"])
                    S.op("dve", lambda e, i=i: e.tensor_copy(out=KT[:, :, i * 128:(i + 1) * 128], in_=pT2[:, 0:6, :]), reads=["pT2"], writes=["KT"])
                    for h2 in range(2):
                        inproj(128, 1792 + h2 * 384, 384, pb[2 + h2][:, 0:384], PB[2 + h2])
                        S.op("act", lambda e, h2=h2: e.copy(out=vf[:, h2 * 384:(h2 + 1) * 384], in_=pb[2 + h2][:, 0:384]), reads=[PB[2 + h2]], writes=["wk768"])
                    S.dma("sp", lambda e, i=i: e.dma_start(out=nv_p[e_, i * 128:(i + 1) * 128, :], in_=vf[:]), reads=["wk768"])
                    S.op("dve", lambda e, i=i: e.tensor_copy(out=Va[:, i, :, 0:128], in_=vf[:].rearrange("p (h d) -> p h d", d=128)), reads=["wk768"], writes=["Va"])
                    gate_silu(128, 2560)
                    ngrp = (i + 4) // 4
                    groups = []
                    for h in range(6):
                        for c in range(2):
                            for jg in range(ngrp):
                                groups.append((h, c, jg, list(range(jg * 4, min(jg * 4 + 4, i + 1)))))

                    def emit_qk(gidx_):
                        h, c, jg, js = groups[gidx_]
                        sl = gidx_ % 2
                        for jj, j in enumerate(js):
                            S.op("pe", lambda e, jj=jj, j=j, h=h, c=c, sl=sl: e.matmul(pb[sl][:, jj * 128:(jj + 1) * 128], lhsT=KT[c * 64:(c + 1) * 64, h, j * 128:(j + 1) * 128],
                                                                                     rhs=qT[c * 64:(c + 1) * 64, h, :], start=True, stop=True), reads=["KT", "qT"], writes=[PB[sl]])
                    emit_qk(0)
                    for gidx_, (h, c, jg, js) in enumerate(groups):
                        sl = gidx_ % 2
                        ob = 4 if h % 2 == 0 else 2
                        if gidx_ + 1 < len(groups):
                            emit_qk(gidx_ + 1)
                        n = len(js)
                        S.op("act", lambda e, n=n, sl=sl: e.activation(out=ex[sl][:, 0:n, :].rearrange("p a q -> p (a q)"), in_=pb[sl][:, 0:n * 128], func=AF.Exp),
                             reads=[PB[sl]], writes=[f"ex{sl}"])
                        if js[-1] == i:
                            S.op("dve", lambda e, n=n, sl=sl: e.tensor_tensor(out=ex[sl][:, n - 1, :], in0=ex[sl][:, n - 1, :], in1=mle[:], op=ALU.mult),
                                 reads=[f"ex{sl}", "mle"], writes=[f"ex{sl}"])
                        for jj, j in enumerate(js):
                            S.op("pe", lambda e, jj=jj, j=j, h=h, c=c, sl=sl, ob=ob: e.matmul(pb[ob + c][:, 0:129], lhsT=ex[sl][:, jj, :], rhs=Va[:, j, h, :],
                                                                                            start=(j == 0), stop=(j == i)), reads=[f"ex{sl}", "Va"], writes=[PB[ob + c]])
                        if jg == ngrp - 1:
                            S.op("dve", lambda e, c=c, ob=ob: e.reciprocal(out=rc[:, c:c + 1], in_=pb[ob + c][:, 128:129]), reads=[PB[ob + c]], writes=["rc"])
                            if c == 1:
                                S.op("dve", lambda e: e.tensor_tensor(out=rc[:, 2:3], in0=rc[:, 1:2], in1=lamc[:, 5:6], op=ALU.mult), reads=["rc", "lamc"], writes=["rc"])
                                S.op("act", lambda e, ob=ob: e.activation(out=tmpo[:], in_=pb[ob + 1][:, 0:128], func=AF.Copy, scale=rc[:, 2:3]), reads=[PB[ob + 1], "rc"], writes=["tmpo"])
                                S.op("dve", lambda e, h=h, ob=ob: e.scalar_tensor_tensor(out=otok[:, h, :], in0=pb[ob][:, 0:128], scalar=rc[:, 0:1], in1=tmpo[:], op0=ALU.mult, op1=ALU.add),
                                     reads=[PB[ob], "rc", "tmpo"], writes=["otok"])
                    headnorm(128, slb, "slb", 1.0 - lam_init, 256)
                    tail(128, xin[par][:], f"xin{par}", pre[:], "pre")
                    S.dma("sp", lambda e, i=i, par=par: e.dma_start(out=y_p[i * 128:(i + 1) * 128, :], in_=pre[:]), reads=["pre"], writes=[f"yd{i}"])
            S.fence()

    GC1 = 0.044715
    GC2 = 1.5957691216057308

    def gelu(P, src_ap, srckey, x2t, gl):
        S.op("act", lambda e: e.activation(out=x2t[:P], in_=src_ap, func=AF.Square), reads=[srckey], writes=["x2t"])
        S.op("dve", lambda e: e.tensor_scalar(out=x2t[:P], in0=x2t[:P], scalar1=GC1, scalar2=1.0, op0=ALU.mult, op1=ALU.add), reads=["x2t"], writes=["x2t"])
        S.op("dve", lambda e: e.tensor_tensor(out=x2t[:P], in0=x2t[:P], in1=src_ap, op=ALU.mult), reads=["x2t", srckey], writes=["x2t"])
        S.op("act", lambda e: e.activation(out=x2t[:P], in_=x2t[:P], func=AF.Sigmoid, scale=GC2), reads=["x2t"], writes=["x2t"])
        S.op("dve", lambda e: e.tensor_tensor(out=gl[:P], in0=x2t[:P], in1=src_ap, op=ALU.mult), reads=["x2t", srckey], writes=["gl"])

    def sg_vv(P, x2t, gl, g8, vnw, vv_out, vvkey):
        S.op("dve", lambda e: e.tensor_tensor(out=x2t[:P, 0:256], in0=gl[:P, 256:512], in1=gl[:P, 256:512], op=ALU.mult), reads=["gl"], writes=["x2t"])
        S.op("dve", lambda e: e.tensor_reduce(out=g8[:P, 0:4], in_=x2t[:P, 0:256].rearrange("p (g d) -> p g d", d=64), axis=AX.X, op=ALU.add), reads=["x2t"], writes=["g8"])
        rstd_from_ss(P, g8[:P, 0:4], g8[:P, 4:8], 64, 4, ["g8"], ["g8"])
        S.op("dve", lambda e: e.tensor_tensor(out=x2t[:P, 0:256].rearrange("p (g d) -> p g d", d=64), in0=gl[:P, 256:512].rearrange("p (g d) -> p g d", d=64),
                                              in1=V(g8[:P, 4:8], [[1, 4], [0, 64]]), op=ALU.mult), reads=["gl", "g8", "x2t"], writes=["x2t"])
        S.op("dve", lambda e: e.tensor_tensor(out=vv_out, in0=x2t[:P, 0:256], in1=vnw[:P], op=ALU.mult), reads=["x2t", "vnw"], writes=[vvkey])

    def gate_scalars(P, b_ap, a_ap, srckey, gs, dtb, nea):
        S.op("act", lambda e: e.activation(out=gs[:P, 12:18], in_=b_ap, func=AF.Exp, scale=-1.0), reads=[srckey], writes=["gs"])
        S.op("dve", lambda e: e.tensor_scalar(out=gs[:P, 12:18], in0=gs[:P, 12:18], scalar1=1.0, scalar2=None, op0=ALU.add), reads=["gs"], writes=["gs"])
        S.op("dve", lambda e: e.reciprocal(out=gs[:P, 0:6], in_=gs[:P, 12:18]), reads=["gs"], writes=["gs"])
        S.op("dve", lambda e: e.tensor_tensor(out=gs[:P, 18:24], in0=a_ap, in1=dtb[:P], op=ALU.add), reads=[srckey, "dtb"], writes=["gs"])
        S.op("dve", lambda e: e.tensor_scalar(out=gs[:P, 24:30], in0=gs[:P, 18:24], scalar1=-1.0, scalar2=None, op0=ALU.mult), reads=["gs"], writes=["gs"])
        S.op("dve", lambda e: e.tensor_tensor(out=gs[:P, 24:30], in0=gs[:P, 24:30], in1=gs[:P, 18:24], op=ALU.max), reads=["gs"], writes=["gs"])
        S.op("act", lambda e: e.activation(out=gs[:P, 24:30], in_=gs[:P, 24:30], func=AF.Exp, scale=-1.0), reads=["gs"], writes=["gs"])
        S.op("act", lambda e: e.activation(out=gs[:P, 24:30], in_=gs[:P, 24:30], func=AF.Ln, bias=1.0), reads=["gs"], writes=["gs"])
        S.op("dve", lambda e: e.tensor_scalar(out=gs[:P, 18:24], in0=gs[:P, 18:24], scalar1=0.0, scalar2=None, op0=ALU.max), reads=["gs"], writes=["gs"])
        S.op("dve", lambda e: e.tensor_tensor(out=gs[:P, 18:24], in0=gs[:P, 18:24], in1=gs[:P, 24:30], op=ALU.add), reads=["gs"], writes=["gs"])
        S.op("dve", lambda e: e.tensor_tensor(out=gs[:P, 6:12], in0=gs[:P, 18:24], in1=nea[:P], op=ALU.mult), reads=["gs", "nea"], writes=["gs"])

    def odd_layer(layer):
        e_ = layer // 2
        DKS = 128 ** -0.5
        load_weights(layer)
        with ExitStack() as L:
            onb = T(L, "onb", [128, 128]); vnw = T(L, "vnw", [128, 256]); dtb = T(L, "dtb", [128, 6]); nea = T(L, "nea", [128, 6])
            gs = T(L, "gs", [128, 64]); g8 = T(L, "g8", [128, 8]); x2t = T(L, "x2t", [128, 512]); gl = T(L, "gl", [128, 512])
            for t_, src_, kn_ in ((onb, onorm_w, "onb"), (vnw, vnorm_w, "vnw"), (dtb, dt_bias, "dtb"), (nea, a_log, "nea")):
                S.dma("sp", lambda e, t_=t_, src_=src_: e.dma_start(out=t_[:], in_=src_[e_:e_ + 1, :].partition_broadcast(128).rearrange("p o n -> p (o n)")), writes=[kn_])
            S.op("act", lambda e: e.activation(out=nea[:], in_=nea[:], func=AF.Exp), reads=["nea"], writes=["nea"])
            S.op("dve", lambda e: e.tensor_scalar(out=nea[:], in0=nea[:], scalar1=-1.0, scalar2=None, op0=ALU.mult), reads=["nea"], writes=["nea"])
            with ExitStack() as P1:
                spin = T(P1, "spino", [16, IN_O]); cwb = T(P1, "cwb", [16, 4, 576]); scv = T(P1, "scv", [16, 3, 576])
                cvs = T(P1, "cvs", [16, 2304]); tmpc = T(P1, "tmpc", [16, 576])
                qn = T(P1, "qn", [16, 6, 128]); kn = T(P1, "kn", [16, 6, 128]); kqT = T(P1, "kqT", [128, 12, 16])
                kTm = T(P1, "kTm", [128, 16, 16]); qTm = T(P1, "qTm", [128, 16, 16])
                Sst = [T(P1, f"Sst{i}", [128, 16, 128]) for i in range(2)]
                Rh = T(P1, "Rh", [16, 16, 128]); dl_s = T(P1, "dl_s", [16, 6, 128]); tmph = T(P1, "tmph", [16, 128])
                Eg = T(P1, "Eg", [16, 6, 16]); egbc = T(P1, "egbc", [128, 6, 16]); tmpS = T(P1, "tmpS", [128, 4, 128])
                wsb = T(P1, "wsb", [16, 8]); vvf = T(P1, "vvf", [16, 256])
                head(16, xs_sb[:16], "xs_sb")
                for blk in range(8):
                    n = 512 if blk < 7 else IN_O - 7 * 512
                    inproj(16, blk * 512, n, pb[blk % 4][:16, 0:n], PB[blk % 4])
                    S.op("act" if blk % 2 else "dve",
                         (lambda e, blk=blk, n=n: e.copy(out=spin[:, blk * 512:blk * 512 + n], in_=pb[blk % 4][:16, 0:n])) if blk % 2 else
                         (lambda e, blk=blk, n=n: e.tensor_copy(out=spin[:, blk * 512:blk * 512 + n], in_=pb[blk % 4][:16, 0:n])),
                         reads=[PB[blk % 4]], writes=["spin"])
                S.op("act", lambda e: e.activation(out=sgt[:16], in_=spin[:, 2828:3852], func=AF.Silu), reads=["spin"], writes=["sgt"])
                S.dma("sp", lambda e: e.dma_start(out=nconv_s[e_, :, 0:2, :], in_=sconv[e_, :, 1:3, :]), key="nconv_s1")
                S.dma("sp", lambda e: e.dma_start(out=nconv_s[e_, :, 2, :], in_=spin[:, 0:2304]), reads=["spin"], key="nconv_s2")
                for ch in range(4):
                    c0 = ch * 576
                    S.dma("sp", lambda e, c0=c0: e.dma_start(out=cwb[:], in_=conv_w[e_, :, c0:c0 + 576].partition_broadcast(16)), writes=["cwb"])
                    S.dma("sp", lambda e, c0=c0: e.dma_start(out=scv[:], in_=sconv[e_, :, :, c0:c0 + 576]), writes=["scv"])
                    S.op("dve", lambda e, c0=c0: e.tensor_tensor(out=cvs[:, c0:c0 + 576], in0=spin[:, c0:c0 + 576], in1=cwb[:, 3, :], op=ALU.mult), reads=["spin", "cwb"], writes=["cvs"])
                    for j in range(3):
                        S.op("dve", lambda e, j=j: e.tensor_tensor(out=tmpc[:], in0=scv[:, j, :], in1=cwb[:, j, :], op=ALU.mult), reads=["scv", "cwb"], writes=["tmpc"])
                        S.op("dve", lambda e, c0=c0: e.tensor_tensor(out=cvs[:, c0:c0 + 576], in0=cvs[:, c0:c0 + 576], in1=tmpc[:], op=ALU.add), reads=["cvs", "tmpc"], writes=["cvs"])
                S.op("act", lambda e: e.activation(out=cvs[:], in_=cvs[:], func=AF.Silu), reads=["cvs"], writes=["cvs"])
                for part, dst, dkey, scl in ((0, qn, "qn", DKS), (1, kn, "kn", 1.0)):
                    xa = cvs[:, part * 768:(part + 1) * 768]
                    S.op("dve", lambda e, xa=xa: e.tensor_tensor(out=wk768[:16], in0=xa, in1=xa, op=ALU.mult), reads=["cvs"], writes=["wk768"])
                    S.op("dve", lambda e: e.tensor_reduce(out=s6[:16, 0:6], in_=wk768[:16].rearrange("p (h d) -> p h d", d=128), axis=AX.X, op=ALU.add), reads=["wk768"], writes=["s6"])
                    rstd_from_ss(16, s6[:16, 0:6], s6[:16, 6:12], 1.0, 6, ["s6"], ["s6"])
                    S.op("dve", lambda e, xa=xa, dst=dst, scl=scl: e.scalar_tensor_tensor(out=dst[:].rearrange("p h d -> p (h d)"), in0=xa, scalar=float(scl),
                                                                                       in1=V(s6[:16, 6:12], [[1, 6], [0, 128]]), op0=ALU.mult, op1=ALU.mult),
                         reads=["cvs", "s6"], writes=[dkey])
                gate_scalars(16, spin[:, 2816:2822], spin[:, 2822:2828], "spin", gs, dtb, nea)
                S.op("act", lambda e: e.activation(out=gs[:16, 30:36], in_=gs[:16, 6:12], func=AF.Exp), reads=["gs"], writes=["gs"])
                S.op("dve", lambda e: e.tensor_tensor(out=wk768[:16], in0=qn[:].rearrange("p h d -> p (h d)"), in1=kn[:].rearrange("p h d -> p (h d)"), op=ALU.mult),
                     reads=["qn", "kn"], writes=["wk768"])
                S.op("dve", lambda e: e.tensor_reduce(out=gs[:16, 36:42], in_=wk768[:16].rearrange("p (h d) -> p h d", d=128), axis=AX.X, op=ALU.add), reads=["wk768"], writes=["gs"])
                for h in range(6):
                    S.op("pe", lambda e, h=h: e.transpose(out=pb[0][:, h * 16:(h + 1) * 16], in_=kn[:16, h, :], identity=identf[:16, :16]), reads=["kn", "identf"], writes=[PB[0]])
                    S.op("pe", lambda e, h=h: e.transpose(out=pb[0][:, 96 + h * 16:96 + (h + 1) * 16], in_=qn[:16, h, :], identity=identf[:16, :16]), reads=["qn", "identf"], writes=[PB[0]])
                S.op("act", lambda e: e.copy(out=kqT[:].rearrange("p a b -> p (a b)"), in_=pb[0][:, 0:192]), reads=[PB[0]], writes=["kqT"])
                S.op("dve", lambda e: e.tensor_tensor(out=Eg[:], in0=V(gs[:16, 30:36], [[1, 6], [0, 16]]), in1=V(identf[:16, 0:16], [[0, 6], [1, 16]]), op=ALU.mult),
                     reads=["gs", "identf"], writes=["Eg"])
                S.op("pe", lambda e: e.matmul(pb[5][:, 0:96], lhsT=ones[:16, :], rhs=Eg[:].rearrange("p h b -> p (h b)"), start=True, stop=True), reads=["ones", "Eg"], writes=[PB[5]])
                S.op("act", lambda e: e.copy(out=egbc[:].rearrange("p h b -> p (h b)"), in_=pb[5][:, 0:96]), reads=[PB[5]], writes=["egbc"])
                i16v = i16[:].rearrange("p (a b) -> p a b", b=16)
                for h in range(6):
                    par = h % 2
                    S.dma("sp", lambda e, h=h, par=par: e.dma_start(out=Sst[par][:], in_=sdelta[e_, :, h].rearrange("b k v -> k b v")), writes=[f"Sst{par}"])
                    S.op("dve", lambda e, h=h: e.tensor_tensor(out=kTm[:], in0=V(kqT[:, h, 0:1], [[0, 16], [1, 16]]), in1=i16v, op=ALU.mult), reads=["kqT", "i16"], writes=["kTm"])
                    S.op("dve", lambda e, h=h: e.tensor_tensor(out=qTm[:], in0=V(kqT[:, 6 + h, 0:1], [[0, 16], [1, 16]]), in1=i16v, op=ALU.mult), reads=["kqT", "i16"], writes=["qTm"])
                    for b in range(16):
                        S.op("pe", lambda e, b=b, par=par: e.matmul(pb[1][:16, 0:128], lhsT=kTm[:, b, :], rhs=Sst[par][:, b, :], start=(b == 0), stop=(b == 15)),
                             reads=["kTm", f"Sst{par}"], writes=[PB[1]])
                    for b in range(16):
                        S.op("pe", lambda e, b=b, par=par: e.matmul(pb[2][:16, 0:128], lhsT=qTm[:, b, :], rhs=Sst[par][:, b, :], start=(b == 0), stop=(b == 15)),
                             reads=["qTm", f"Sst{par}"], writes=[PB[2]])
                    vh = cvs[:, 1536 + h * 128:1536 + (h + 1) * 128]
                    S.op("dve", lambda e, h=h: e.tensor_scalar(out=tmph[:], in0=pb[1][:16, 0:128], scalar1=gs[:16, 30 + h:31 + h], scalar2=None, op0=ALU.mult), reads=[PB[1], "gs"], writes=["tmph"])
                    S.op("dve", lambda e, vh=vh: e.tensor_tensor(out=tmph[:], in0=vh, in1=tmph[:], op=ALU.subtract), reads=["cvs", "tmph"], writes=["tmph"])
                    S.op("dve", lambda e, h=h: e.tensor_scalar(out=dl_s[:, h, :], in0=tmph[:], scalar1=gs[:16, h:h + 1], scalar2=None, op0=ALU.mult), reads=["tmph", "gs"], writes=["dl_s"])
                    S.op("dve", lambda e, h=h: e.tensor_scalar(out=tmph[:], in0=pb[2][:16, 0:128], scalar1=gs[:16, 30 + h:31 + h], scalar2=None, op0=ALU.mult), reads=[PB[2], "gs", "dl_s"], writes=["tmph"])
                    S.op("dve", lambda e, h=h: e.scalar_tensor_tensor(out=otok[:16, h, :], in0=dl_s[:, h, :], scalar=gs[:16, 36 + h:37 + h], in1=tmph[:], op0=ALU.mult, op1=ALU.add),
                         reads=["dl_s", "gs", "tmph"], writes=["otok"])
                    S.op("dve", lambda e, h=h: e.tensor_tensor(out=Rh[:], in0=V(dl_s[:, h, 0:1], [[0, 16], [1, 128]]), in1=V(identf[:16, 0:1], [[1, 16], [0, 128]]), op=ALU.mult),
                         reads=["dl_s", "identf"], writes=["Rh"])
                    for q4 in range(4):
                        bank = 3 + q4 % 2
                        S.op("pe", lambda e, q4=q4, h=h, bank=bank: e.matmul(pb[bank][:, :], lhsT=kn[:16, h, :], rhs=Rh[:, q4 * 4:(q4 + 1) * 4, :].rearrange("p b d -> p (b d)"),
                                                                          start=True, stop=True), reads=["kn", "Rh"], writes=[PB[bank]])
                        S.op("dve", lambda e, q4=q4, h=h, par=par: e.tensor_tensor(out=tmpS[:], in0=Sst[par][:, q4 * 4:(q4 + 1) * 4, :], in1=V(egbc[:, h, q4 * 4:q4 * 4 + 1], [[1, 4], [0, 128]]), op=ALU.mult),
                             reads=[f"Sst{par}", "egbc"], writes=["tmpS"])
                        S.op("dve", lambda e, q4=q4, par=par, bank=bank: e.tensor_tensor(out=Sst[par][:, q4 * 4:(q4 + 1) * 4, :], in0=tmpS[:], in1=pb[bank][:, :].rearrange("p (b d) -> p b d", d=128), op=ALU.add),
                             reads=["tmpS", PB[bank]], writes=[f"Sst{par}"])
                    S.dma("sp", lambda e, h=h, par=par: e.dma_start(out=ndelta_s[e_, :, h].rearrange("b k v -> k b v"), in_=Sst[par][:]), reads=[f"Sst{par}"])
                headnorm(16, onb, "onb", 1.0, 0)
                gelu(16, spin[:, 2304:2816], "spin", x2t, gl)
                sg_vv(16, x2t, gl, g8, vnw, vvf[:], "vvf")
                S.dma("sp", lambda e: e.dma_start(out=nsgv_s[e_], in_=vvf[:]), reads=["vvf"])
                S.dma("sp", lambda e: e.dma_start(out=wsb[:, 0:4], in_=w_s[e_, :, 0, 0:1].rearrange("g o -> o g").partition_broadcast(16).rearrange("p o g -> p (o g)"), allow_slow_non_contiguous=True), writes=["wsb"])
                S.dma("sp", lambda e: e.dma_start(out=wsb[:, 4:8], in_=b_s[e_, :, 0:1].rearrange("g o -> o g").partition_broadcast(16).rearrange("p o g -> p (o g)"), allow_slow_non_contiguous=True), writes=["wsb"])
                S.op("dve", lambda e: e.tensor_tensor(out=vvf[:].rearrange("p (g d) -> p g d", d=64), in0=vvf[:].rearrange("p (g d) -> p g d", d=64), in1=V(wsb[:, 0:4], [[1, 4], [0, 64]]), op=ALU.mult),
                     reads=["vvf", "wsb"], writes=["vvf"])
                S.op("dve", lambda e: e.tensor_tensor(out=vvf[:].rearrange("p (g d) -> p g d", d=64), in0=vvf[:].rearrange("p (g d) -> p g d", d=64), in1=V(wsb[:, 4:8], [[1, 4], [0, 64]]), op=ALU.add),
                     reads=["vvf", "wsb"], writes=["vvf"])
                S.op("dve", lambda e: e.tensor_tensor(out=pre[:16, 768:1024], in0=vvf[:], in1=gl[:16, 0:256], op=ALU.mult), reads=["vvf", "gl"], writes=["pre"])
                tail(16, xs_sb[:16], "xs_sb", xs_sb[:16], "xs_sb")
            S.fence()
            with ExitStack() as P2:
                xin = [T(P2, f"xin{i}", [128, 1024]) for i in range(2)]
                ext = [T(P2, f"ext{i}", [128, 18, 131]) for i in range(2)]
                cvb = T(P2, "cvb", [128, 18, 128]); sqrn = T(P2, "sqrn", [128, 12, 128])
                qkT = T(P2, "qkT", [128, 12, 128], BF16); vT = T(P2, "vT", [128, 6, 128], BF16)
                ktok = T(P2, "ktok", [128, 6, 128], BF16); vtok = T(P2, "vtok", [128, 6, 128], BF16)
                LT = T(P2, "LT", [128, 6, 128]); dec = T(P2, "dec", [128, 6, 128]); dlt = T(P2, "dlt", [128, 6, 128]); dle = dec
                Am = [T(P2, f"Am{i}", [128, 6, 128], BF16) for i in range(2)]
                Bm = [T(P2, f"Bm{i}", [128, 6, 128], BF16) for i in range(2)]
                Qm = [T(P2, f"Qm{i}", [128, 6, 128], BF16) for i in range(2)]
                aq = T(P2, "aq", [128, 6, 128], BF16); rr = T(P2, "rr", [128, 6, 128], BF16); dlb = T(P2, "dlb", [128, 6, 128], BF16); kd = T(P2, "kd", [128, 6, 128], BF16)
                otmp = T(P2, "otmp", [128, 6, 128]); Sf = T(P2, "Sf", [128, 6, 128]); Sb = T(P2, "Sb", [128, 6, 128], BF16)
                gs2 = T(P2, "gs2", [128, 32]); vvb = T(P2, "vvb", [128, 256], BF16)
                wsf = T(P2, "wsf", [128, 4, 128]); WsT = T(P2, "WsT", [128, 4, 128], BF16); bsT = T(P2, "bsT", [128, 4]); cw = T(P2, "cw", [128, 18, 4])
                S.dma("sp", lambda e: e.dma_start(out=wsf[:], in_=w_s[e_].rearrange("g t s -> t g s")), writes=["wsf"])
                S.dma("sp", lambda e: e.dma_start(out=bsT[:], in_=b_s[e_].rearrange("g t -> t g"), allow_slow_non_contiguous=True), writes=["bsT"])
                for j in range(4):
                    S.dma("sp", lambda e, j=j: e.dma_start(out=cw[:, :, j], in_=conv_w[e_, j:j + 1, :].rearrange("o (cb p) -> p (o cb)", p=128), allow_slow_non_contiguous=True), writes=["cw"])
                for g in range(4):
                    S.op("pe", lambda e, g=g: e.transpose(out=pb[0][:, g * 128:(g + 1) * 128], in_=wsf[:, g, :], identity=identf[:]), reads=["wsf", "identf"], writes=[PB[0]])
                S.op("dve", lambda e: e.tensor_tensor(out=WsT[:], in0=pb[0][:, :].rearrange("p (g t) -> p g t", t=128), in1=V(mle[:, 0:1], [[0, 4], [1, 128]]), op=ALU.mult),
                     reads=[PB[0], "mle"], writes=["WsT"])
                S.op("pool", lambda e: e.memset(Sf[:], 0.0), writes=["Sf"])
                S.op("pool", lambda e: e.memset(Sb[:], 0.0), writes=["Sb"])
                CV = [f"cvb{cb}" for cb in range(18)]
                for i in range(NT):
                    par = i % 2
                    S.dma("sp", lambda e, i=i, par=par: e.dma_start(out=xin[par][:], in_=y_p[i * 128:(i + 1) * 128, :]), reads=[f"yd{i}"], writes=[f"xin{par}"])
                    head(128, xin[par][:], f"xin{par}")
                    inproj(128, 2304, 512, pb[0][:, :], PB[0])
                    inproj(128, 2816, 12, pb[1][:, 0:12], PB[1])
                    gate_silu(128, 2828)
                    gelu(128, pb[0][:, :], PB[0], x2t, gl)
                    sg_vv(128, x2t, gl, g8, vnw, vvb[:], "vvb")
                    for g in range(4):
                        S.op("pe", lambda e, g=g: e.matmul(pb[4][:, g * 64:(g + 1) * 64], lhsT=WsT[:, g, :], rhs=vvb[:, g * 64:(g + 1) * 64], start=True, stop=True),
                             reads=["WsT", "vvb"], writes=[PB[4]])
                    for g in range(4):
                        S.op("dve", lambda e, g=g: e.scalar_tensor_tensor(out=pre[:, 768 + g * 64:768 + (g + 1) * 64], in0=pb[4][:, g * 64:(g + 1) * 64], scalar=bsT[:, g:g + 1],
                                                                        in1=gl[:, g * 64:(g + 1) * 64], op0=ALU.add, op1=ALU.mult), reads=[PB[4], "bsT", "gl"], writes=["pre"])
                    gate_scalars(128, pb[1][:, 0:6], pb[1][:, 6:12], PB[1], gs, dtb, nea)
                    if i == 0:
                        S.op("pool", lambda e, par=par: e.memset(ext[par][:, :, 0:3], 0.0), writes=[f"ext{par}"])
                    else:
                        S.op("pool", lambda e, par=par: e.tensor_copy(out=ext[par][:, :, 0:3], in_=ext[1 - par][:, :, 128:131]), reads=[f"ext{1 - par}"], writes=[f"ext{par}"])
                    for bk in range(5):
                        cbs = list(range(bk * 4, min(bk * 4 + 4, 18)))
                        bank = 2 + bk % 2
                        for jj, cb in enumerate(cbs):
                            for kc in range(8):
                                S.op("pe", lambda e, jj=jj, cb=cb, kc=kc, bank=bank: e.matmul(pb[bank][:, jj * 128:(jj + 1) * 128], lhsT=win[:, kc, cb * 128:(cb + 1) * 128], rhs=hT[:, kc, :],
                                                                                          start=(kc == 0), stop=(kc == 7)), reads=["win", "hT"], writes=[PB[bank]])
                        n = len(cbs)
                        if bk % 2:
                            S.op("act", lambda e, n=n, bank=bank, cb0=cbs[0], par=par: e.copy(out=ext[par][:, cb0:cb0 + n, 3:131], in_=pb[bank][:, 0:n * 128].rearrange("p (a t) -> p a t", t=128)),
                                 reads=[PB[bank]], writes=[f"ext{par}"])
                        else:
                            S.op("dve", lambda e, n=n, bank=bank, cb0=cbs[0], par=par: e.tensor_copy(out=ext[par][:, cb0:cb0 + n, 3:131], in_=pb[bank][:, 0:n * 128].rearrange("p (a t) -> p a t", t=128)),
                                 reads=[PB[bank]], writes=[f"ext{par}"])
                    if i == NT - 1:
                        for r_ in range(3):
                            S.dma("sp", lambda e, par=par, r_=r_: e.dma_start(out=nconv_p[e_, r_:r_ + 1, :].rearrange("o (cb p) -> p (o cb)", p=128), in_=ext[par][:, :, 128 + r_], allow_slow_non_contiguous=True), reads=[f"ext{par}"])
                    for j in range(4):
                        for cb in range(18):
                            if j == 0:
                                S.op("dve", lambda e, cb=cb, par=par: e.tensor_scalar(out=cvb[:, cb, :], in0=ext[par][:, cb, 0:128], scalar1=cw[:, cb, 0:1], scalar2=None, op0=ALU.mult),
                                     reads=[f"ext{par}", "cw"], writes=[CV[cb]])
                            else:
                                S.op("dve", lambda e, cb=cb, par=par, j=j: e.scalar_tensor_tensor(out=cvb[:, cb, :], in0=ext[par][:, cb, j:j + 128], scalar=cw[:, cb, j:j + 1], in1=cvb[:, cb, :],
                                                                                               op0=ALU.mult, op1=ALU.add), reads=[f"ext{par}", "cw", CV[cb]], writes=[CV[cb]])
                    S.op("act", lambda e: e.activation(out=cvb[:].rearrange("p a t -> p (a t)"), in_=cvb[:].rearrange("p a t -> p (a t)"), func=AF.Silu), reads=CV, writes=CV)
                    S.op("act", lambda e: e.activation(out=sqrn[:].rearrange("p a t -> p (a t)"), in_=cvb[:, 0:12, :].rearrange("p a t -> p (a t)"), func=AF.Square), reads=CV, writes=["sqrn"])
                    for m in range(3):
                        S.op("pe", lambda e, m=m: e.matmul(pb[m][:, :], lhsT=ones[:], rhs=sqrn[:, m * 4:(m + 1) * 4, :].rearrange("p a t -> p (a t)"), start=True, stop=True),
                             reads=["ones", "sqrn"], writes=[PB[m]])
                    for m in range(3):
                        S.op("act", lambda e, m=m: e.activation(out=sqrn[:, m * 4:(m + 1) * 4, :].rearrange("p a t -> p (a t)"), in_=pb[m][:, :], func=AF.Ln, bias=EPS), reads=[PB[m], "sqrn"], writes=["sqrn"])
                    S.op("act", lambda e: e.activation(out=sqrn[:].rearrange("p a t -> p (a t)"), in_=sqrn[:].rearrange("p a t -> p (a t)"), func=AF.Exp, scale=-0.5), reads=["sqrn"], writes=["sqrn"])
                    S.op("dve", lambda e: e.scalar_tensor_tensor(out=qkT[:, 0:6, :].rearrange("p a t -> p (a t)"), in0=cvb[:, 0:6, :].rearrange("p a t -> p (a t)"), scalar=float(DKS),
                                                                in1=sqrn[:, 0:6, :].rearrange("p a t -> p (a t)"), op0=ALU.mult, op1=ALU.mult), reads=CV + ["sqrn"], writes=["qkT"])
                    S.op("dve", lambda e: e.tensor_tensor(out=qkT[:, 6:12, :], in0=cvb[:, 6:12, :], in1=sqrn[:, 6:12, :], op=ALU.mult), reads=CV + ["sqrn"], writes=["qkT"])
                    S.op("act", lambda e: e.copy(out=vT[:], in_=cvb[:, 12:18, :]), reads=CV, writes=["vT"])
                    for h in range(6):
                        S.op("pe", lambda e, h=h: e.transpose(out=pT1[:, h, :], in_=qkT[:, 6 + h, :], identity=identb[:]), reads=["qkT", "identb"], writes=["pT1"])
                        S.op("pe", lambda e, h=h: e.transpose(out=pT2[:, h, :], in_=vT[:, h, :], identity=identb[:]), reads=["vT", "identb"], writes=["pT2"])
                    S.op("dve", lambda e: e.tensor_copy(out=ktok[:], in_=pT1[:, 0:6, :]), reads=["pT1"], writes=["ktok"])
                    S.op("act", lambda e: e.copy(out=vtok[:], in_=pT2[:, 0:6, :]), reads=["pT2"], writes=["vtok"])
                    S.op("dve", lambda e: e.tensor_tensor(out=LT[:], in0=V(mle[:, 0:1], [[0, 6], [1, 128]]), in1=V(gs[:, 6:7], [[1, 6], [0, 128]]), op=ALU.mult), reads=["mle", "gs"], writes=["LT"])
                    for m in range(2):
                        S.op("pe", lambda e, m=m: e.matmul(pb[3 + m][:, 0:384], lhsT=su[:], rhs=LT[:, m * 3:(m + 1) * 3, :].rearrange("p a t -> p (a t)"), start=True, stop=True),
                             reads=["su", "LT"], writes=[PB[3 + m]])
                        S.op("act", lambda e, m=m: e.activation(out=dec[:, m * 3:(m + 1) * 3, :].rearrange("p a t -> p (a t)"), in_=pb[3 + m][:, 0:384], func=AF.Exp), reads=[PB[3 + m]], writes=["dec"])
                    S.op("dve", lambda e: e.tensor_tensor(out=dlt[:], in0=dec[:], in1=V(mlt[:, 0:1], [[0, 6], [1, 128]]), op=ALU.mult), reads=["dec", "mlt"], writes=["dlt"])
                    S.op("dve", lambda e: e.tensor_tensor(out=dec[:], in0=dec[:], in1=V(mle[:, 0:1], [[0, 6], [1, 128]]), op=ALU.mult), reads=["dec", "mle"], writes=["dec"])
                    for m, lm in enumerate((mle, su, ones)):
                        S.op("pe", lambda e, m=m, lm=lm: e.matmul(pb[5][:, m * 6:(m + 1) * 6], lhsT=lm[:], rhs=gs[:, 6:12], start=True, stop=True), reads=["mle", "su", "ones", "gs"], writes=[PB[5]])
                    S.op("act", lambda e: e.activation(out=gs2[:, 0:18], in_=pb[5][:, 0:18], func=AF.Exp), reads=[PB[5]], writes=["gs2"])
                    S.op("dve", lambda e: e.tensor_scalar(out=gs2[:, 18:24], in0=gs[:, 0:6], scalar1=-1.0, scalar2=None, op0=ALU.mult), reads=["gs"], writes=["gs2"])
                    S.op("dve", lambda e: e.tensor_scalar(out=gs2[:, 24:30], in0=gs2[:, 0:6], scalar1=-1.0, scalar2=None, op0=ALU.mult), reads=["gs2"], writes=["gs2"])

                    def hb_(h):
                        return h // 3, slice((h % 3) * 128, (h % 3 + 1) * 128)
                    for h in range(6):
                        bq, cs = hb_(h)
                        S.op("pe", lambda e, h=h, bq=bq, cs=cs: e.matmul(pb[0 + bq][:, cs], lhsT=qkT[:, 6 + h, :], rhs=qkT[:, 6 + h, :], start=True, stop=True), reads=["qkT"], writes=[PB[0 + bq]])
                        S.op("pe", lambda e, h=h, bq=bq, cs=cs: e.matmul(pb[2 + bq][:, cs], lhsT=qkT[:, 6 + h, :], rhs=qkT[:, h, :], start=True, stop=True), reads=["qkT"], writes=[PB[2 + bq]])
                    for h in range(6):
                        bq, cs = hb_(h)
                        S.op("dve", lambda e, h=h, bq=bq, cs=cs: e.scalar_tensor_tensor(out=Bm[0][:, h, :], in0=pb[bq][:, cs], scalar=gs2[:, 18 + h:19 + h], in1=dlt[:, h, :], op0=ALU.mult, op1=ALU.mult),
                             reads=[PB[bq], "gs2", "dlt"], writes=["Bm0"])
                    for m in range(2):
                        S.op("dve", lambda e, m=m: e.tensor_tensor(out=aq[:, m * 3:(m + 1) * 3, :].rearrange("p a t -> p (a t)"), in0=pb[2 + m][:, 0:384], in1=dle[:, m * 3:(m + 1) * 3, :].rearrange("p a t -> p (a t)"), op=ALU.mult),
                             reads=[PB[2 + m], "dec"], writes=["aq"])
                    for h in range(6):
                        S.op("pe", lambda e, h=h: e.transpose(out=pT1[:, h, :], in_=Bm[0][:, h, :], identity=identb[:]), reads=["Bm0", "identb"], writes=["pT1"])
                    S.op("act", lambda e: e.copy(out=Am[0][:], in_=pT1[:, 0:6, :]), reads=["pT1"], writes=["Am0"])
                    S.op("dve", lambda e: e.tensor_tensor(out=Qm[0][:], in0=Bm[0][:], in1=V(identb[:, 0:1], [[0, 6], [1, 128]]), op=ALU.add), reads=["Bm0", "identb"], writes=["Qm0"])
                    for s_ in range(6):
                        c_, n_ = s_ % 2, (s_ + 1) % 2
                        for h in range(6):
                            bq, cs = hb_(h)
                            S.op("pe", lambda e, h=h, bq=bq, cs=cs, c_=c_: e.matmul(pb[bq][:, cs], lhsT=Bm[c_][:, h, :], rhs=Am[c_][:, h, :], start=True, stop=True),
                                 reads=[f"Bm{c_}", f"Am{c_}"], writes=[PB[bq]])
                        for m in range(2):
                            S.op("act", lambda e, m=m, n_=n_: e.copy(out=Am[n_][:, m * 3:(m + 1) * 3, :].rearrange("p a t -> p (a t)"), in_=pb[m][:, 0:384]), reads=[PB[m]], writes=[f"Am{n_}"])
                        if s_ < 5:
                            for h in range(6):
                                bq, cs = hb_(h)
                                S.op("pe", lambda e, h=h, bq=bq, cs=cs, c_=c_: e.matmul(pb[2 + bq][:, cs], lhsT=Am[c_][:, h, :], rhs=Bm[c_][:, h, :], start=True, stop=True),
                                     reads=[f"Bm{c_}", f"Am{c_}"], writes=[PB[2 + bq]])
                            for m in range(2):
                                S.op("dve", lambda e, m=m, n_=n_: e.tensor_copy(out=Bm[n_][:, m * 3:(m + 1) * 3, :].rearrange("p a t -> p (a t)"), in_=pb[2 + m][:, 0:384]), reads=[PB[2 + m]], writes=[f"Bm{n_}"])
                        for h in range(6):
                            bq, cs = hb_(h)
                            S.op("pe", lambda e, h=h, bq=bq, cs=cs, c_=c_, n_=n_: e.matmul(pb[4 + bq][:, cs], lhsT=Am[n_][:, h, :], rhs=Qm[c_][:, h, :], start=True, stop=True),
                                 reads=[f"Am{n_}", f"Qm{c_}"], writes=[PB[4 + bq]])
                        for m in range(2):
                            S.op("dve", lambda e, m=m, c_=c_, n_=n_: e.tensor_tensor(out=Qm[n_][:, m * 3:(m + 1) * 3, :].rearrange("p a t -> p (a t)"), in0=pb[4 + m][:, 0:384],
                                                                                 in1=Qm[c_][:, m * 3:(m + 1) * 3, :].rearrange("p a t -> p (a t)"), op=ALU.add), reads=[PB[4 + m], f"Qm{c_}"], writes=[f"Qm{n_}"])
                    Qf = Qm[0]
                    for h in range(6):
                        bq, cs = hb_(h)
                        S.op("pe", lambda e, h=h, bq=bq, cs=cs: e.matmul(pb[bq][:, cs], lhsT=qkT[:, 6 + h, :], rhs=Sb[:, h, :], start=True, stop=True), reads=["qkT", "Sb"], writes=[PB[bq]])
                    for h in range(6):
                        bq, cs = hb_(h)
                        S.op("dve", lambda e, h=h, bq=bq, cs=cs: e.scalar_tensor_tensor(out=rr[:, h, :], in0=pb[bq][:, cs], scalar=gs2[:, 24 + h:25 + h], in1=vtok[:, h, :], op0=ALU.mult, op1=ALU.add),
                             reads=[PB[bq], "gs2", "vtok"], writes=["rr"])
                    for h in range(6):
                        bq, cs = hb_(h)
                        S.op("pe", lambda e, h=h, bq=bq, cs=cs: e.matmul(pb[2 + bq][:, cs], lhsT=Qf[:, h, :], rhs=rr[:, h, :], start=True, stop=True), reads=["Qm0", "rr"], writes=[PB[2 + bq]])
                    for h in range(6):
                        bq, cs = hb_(h)
                        S.op("act", lambda e, h=h, bq=bq, cs=cs: e.activation(out=dlb[:, h, :], in_=pb[2 + bq][:, cs], func=AF.Copy, scale=gs[:, h:h + 1]), reads=[PB[2 + bq], "gs"], writes=["dlb"])
                    for h in range(6):
                        bq, cs = hb_(h)
                        S.op("pe", lambda e, h=h, bq=bq, cs=cs: e.matmul(pb[bq][:, cs], lhsT=qkT[:, h, :], rhs=Sb[:, h, :], start=True, stop=True), reads=["qkT", "Sb"], writes=[PB[bq]])
                        S.op("pe", lambda e, h=h, bq=bq, cs=cs: e.matmul(pb[4 + bq][:, cs], lhsT=aq[:, h, :], rhs=dlb[:, h, :], start=True, stop=True), reads=["aq", "dlb"], writes=[PB[4 + bq]])
                    for h in range(6):
                        bq, cs = hb_(h)
                        S.op("act", lambda e, h=h, bq=bq, cs=cs: e.activation(out=otmp[:, h, :], in_=pb[bq][:, cs], func=AF.Copy, scale=gs2[:, h:h + 1]), reads=[PB[bq], "gs2"], writes=["otmp"])
                    for m in range(2):
                        S.op("dve", lambda e, m=m: e.tensor_tensor(out=otok[:, m * 3:(m + 1) * 3, :].rearrange("p a t -> p (a t)"), in0=pb[4 + m][:, 0:384],
                                                                 in1=otmp[:, m * 3:(m + 1) * 3, :].rearrange("p a t -> p (a t)"), op=ALU.add), reads=[PB[4 + m], "otmp"], writes=["otok"])
                    S.op("dve", lambda e: e.tensor_tensor(out=kd[:], in0=ktok[:], in1=V(gs2[:, 6:7], [[1, 6], [0, 128]]), op=ALU.mult), reads=["ktok", "gs2"], writes=["kd"])
                    for h in range(6):
                        bq, cs = hb_(h)
                        S.op("pe", lambda e, h=h, bq=bq, cs=cs: e.matmul(pb[2 + bq][:, cs], lhsT=kd[:, h, :], rhs=dlb[:, h, :], start=True, stop=True), reads=["kd", "dlb"], writes=[PB[2 + bq]])
                    for h in range(6):
                        bq, cs = hb_(h)
                        S.op("dve", lambda e, h=h, bq=bq, cs=cs: e.scalar_tensor_tensor(out=Sf[:, h, :], in0=Sf[:, h, :], scalar=gs2[:, 12 + h:13 + h], in1=pb[2 + bq][:, cs], op0=ALU.mult, op1=ALU.add),
                             reads=["Sf", "gs2", PB[2 + bq]], writes=["Sf"])
                    S.op("act", lambda e: e.copy(out=Sb[:], in_=Sf[:]), reads=["Sf"], writes=["Sb"])
                    if i == NT - 1:
                        S.dma("sp", lambda e: e.dma_start(out=ndelta_p[e_].rearrange("h k v -> k h v"), in_=Sf[:]), reads=["Sf"])
                    headnorm(128, onb, "onb", 1.0, 0)
                    tail(128, xin[par][:], f"xin{par}", pre[:], "pre")
                    S.dma("sp", lambda e, i=i: e.dma_start(out=y_p[i * 128:(i + 1) * 128, :], in_=pre[:]), reads=["pre"], writes=[f"yd{i}"])
            S.fence()

    return nc, S, es, locals()


def _finish_build(nlayers=4):
    nc, S, es, env = build()
    even_layer = env["even_layer"]
    odd_layer = env.get("odd_layer")
    for layer in range(nlayers):
        if layer % 2 == 0:
            even_layer(layer)
        elif odd_layer is not None:
            odd_layer(layer)
    xs_sb = env["xs_sb"]; y_s = env["y_s"]
    S.dma("sp", lambda e: e.dma_start(out=y_s, in_=xs_sb[:16]), reads=["xs_sb"], key="final_ys")
    S.emit()
    es.close()
    return nc


OUT_NAMES = ["y_p", "y_s", "nk_p", "nv_p", "nk_s", "nv_s", "npool_p", "npool_s", "nconv_p", "nconv_s", "ndelta_p", "ndelta_s", "nsgv_s"]
_NC_CACHE = {}


def kernel(x_prompt, x_sample, cache_k, cache_v, state_pool, state_conv, state_delta, page_table,
           norm_w, w_in_e, w_out_e, pool_w, pool_scale, qn_w, kn_w, lam_qk, subln_w,
           w_in_o, w_out_o, conv_w, a_log, dt_bias, onorm_w, vnorm_w, w_s, b_s, _nlayers=4):
    f = lambda a: np.ascontiguousarray(np.asarray(a, dtype=np.float32))
    if _nlayers not in _NC_CACHE:
        _NC_CACHE[_nlayers] = _finish_build(_nlayers)
    nc = _NC_CACHE[_nlayers]
    ckf = f(cache_k).reshape(2 * NPHYS * 128, 768)
    cvf = f(cache_v).reshape(2 * NPHYS * 128, 768)
    consts = host_consts()
    shared = dict(ck=ckf, cv=cvf, norm_w=f(norm_w), w_in_e=f(w_in_e), w_out_e=f(w_out_e), pool_w=f(pool_w), pool_scale=f(pool_scale),
                  qn_w=f(qn_w), kn_w=f(kn_w), lam_qk=f(lam_qk).reshape(2, 256), subln_w=f(subln_w), w_in_o=f(w_in_o), w_out_o=f(w_out_o),
                  conv_w=f(conv_w), a_log=f(a_log), dt_bias=f(dt_bias), onorm_w=f(onorm_w), vnorm_w=f(vnorm_w).reshape(2, 256), w_s=f(w_s), b_s=f(b_s))
    shared.update(consts)
    xpr = f(x_prompt); xsa = f(x_sample)[:, 0, :]
    sp_ = f(state_pool); scv = f(state_conv); sdl = f(state_delta)
    ptab = np.ascontiguousarray(np.asarray(page_table, dtype=np.int32))
    in_maps = []
    for c in range(8):
        sl = slice(c * 16, (c + 1) * 16)
        m = dict(shared)
        m.update(xp=xpr[c], xs=np.ascontiguousarray(xsa[sl]), spool=np.ascontiguousarray(sp_[:, sl]), sconv=np.ascontiguousarray(scv[:, sl]),
                 sdelta=np.ascontiguousarray(sdl[:, sl]), pt=np.ascontiguousarray(ptab[sl].reshape(1, 256)))
        in_maps.append(m)
    res = run_bass_kernel_spmd(nc, in_maps, core_ids=list(range(8)))
    R = res.results
    g = lambda n: [np.asarray(r[n], dtype=np.float32) for r in R]
    y_prompt = np.stack(g("y_p"), 0)
    y_sample = np.concatenate(g("y_s"), 0)[:, None, :]
    nkp = np.stack(g("nk_p"), 1).reshape(2, 8, 2048, 12, 64)
    nvp = np.stack(g("nv_p"), 1).reshape(2, 8, 2048, 6, 128)
    nks = np.concatenate(g("nk_s"), 1).reshape(2, 128, 1, 12, 64)
    nvs = np.concatenate(g("nv_s"), 1).reshape(2, 128, 1, 6, 128)
    npp = np.stack(g("npool_p"), 1)
    nps = np.concatenate(g("npool_s"), 1)
    ncp = np.stack(g("nconv_p"), 1)
    ncs = np.concatenate(g("nconv_s"), 1)
    ndp = np.stack(g("ndelta_p"), 1)
    nds = np.concatenate(g("ndelta_s"), 1)
    nsg = np.concatenate(g("nsgv_s"), 1)[:, :, None, :]
    return (y_prompt, y_sample, nkp, nvp, nks, nvs, npp, nps, ncp, ncs, ndp, nds, nsg)
```

```python
import numpy as np
import concourse.bass as bass
import concourse.mybir as mybir
from concourse.alu_op_type import AluOpType as ALU

F32 = mybir.dt.float32
BF16 = mybir.dt.bfloat16
I32 = mybir.dt.int32
AF = mybir.ActivationFunctionType
AX = mybir.AxisListType

SEM_LIMIT = 30000


class Sched:
    ENGS = ("pe", "dve", "act", "pool", "sp")

    def __init__(self, nc):
        self.nc = nc
        self.prog = {e: [] for e in self.ENGS}
        self.cnt = {e: 0 for e in self.ENGS}
        self.sems = {e: [] for e in self.ENGS}
        self.seen = {e: {} for e in self.ENGS}
        self.last_w = {}
        self.readers = {}
        self.dma_sems = {}
        self.n_ops = 0
        self.epoch = 0
        self.key_epoch = {}
        self.fence_toks = []

    def fence(self):
        toks = []
        for e in self.ENGS:
            if self.cnt[e]:
                toks.append(("eng", e, self.cnt[e]))
        for key, (sem, v) in self.dma_sems.items():
            if v:
                toks.append(("dma", key, v))
        self.fence_toks = toks
        self.epoch += 1

    def _sem(self, e, idx):
        while len(self.sems[e]) <= idx:
            self.sems[e].append(self.nc.alloc_semaphore(name=f"s_{e}_{len(self.sems[e])}"))
        return self.sems[e][idx]

    def _need(self, eng, tok, waits):
        if tok is None:
            return
        if tok[0] == "eng":
            _, e, k = tok
            if e == "pe" and eng == "pe":
                return
            idx, val = (k - 1) // SEM_LIMIT, (k - 1) % SEM_LIMIT + 1
            skey = ("eng", e, idx)
            for (t2, e2, i2), v2 in self.seen[eng].items():
                if t2 == "eng" and e2 == e and i2 > idx:
                    return
            if self.seen[eng].get(skey, 0) >= val:
                return
            self.seen[eng][skey] = val
            waits[skey] = max(waits.get(skey, 0), val)
        else:
            _, key, v = tok
            skey = ("dma", key, 0)
            if self.seen[eng].get(skey, 0) >= v:
                return
            self.seen[eng][skey] = v
            waits[skey] = max(waits.get(skey, 0), v)

    def _deps(self, eng, reads, writes):
        waits = {}
        for kx in list(reads) + list(writes):
            if self.key_epoch.get(kx, -1) < self.epoch:
                self.key_epoch[kx] = self.epoch
                for t in self.fence_toks:
                    self._need(eng, t, waits)
        for r in reads:
            self._need(eng, self.last_w.get(r), waits)
        for w in writes:
            self._need(eng, self.last_w.get(w), waits)
            for t in self.readers.get(w, ()):
                self._need(eng, t, waits)
        out = []
        for skey, val in waits.items():
            if skey[0] == "eng":
                out.append((self._sem(skey[1], skey[2]), val))
            else:
                out.append((self.dma_sems[skey[1]][0], val))
        return out

    def _commit(self, tok, reads, writes):
        for r in reads:
            self.readers.setdefault(r, []).append(tok)
        for w in writes:
            self.last_w[w] = tok
            self.readers[w] = []

    def op(self, eng, fn, reads=(), writes=()):
        waits = self._deps(eng, reads, writes)
        self.cnt[eng] += 1
        k = self.cnt[eng]
        sem = self._sem(eng, (k - 1) // SEM_LIMIT)
        self.prog[eng].append((waits, fn, sem, 1))
        self._commit(("eng", eng, k), reads, writes)
        self.n_ops += 1

    def dma(self, eng, fn, reads=(), writes=(), key=None):
        if key is None:
            key = ("dmak", writes[0] if writes else reads[0])
        waits = self._deps(eng, reads, writes)
        if key not in self.dma_sems:
            self.dma_sems[key] = [self.nc.alloc_semaphore(name=f"d_{len(self.dma_sems)}"), 0]
        ent = self.dma_sems[key]
        ent[1] += 16
        self.prog[eng].append((waits, fn, ent[0], 16))
        self._commit(("dma", key, ent[1]), reads, writes)
        self.n_ops += 1

    def emit(self):
        nc = self.nc
        final = []
        for key, (sem, v) in self.dma_sems.items():
            final.append((sem, v))
        for e in self.ENGS:
            if self.cnt[e]:
                k = self.cnt[e]
                final.append((self._sem(e, (k - 1) // SEM_LIMIT), (k - 1) % SEM_LIMIT + 1))
        engmap = {"pe": "tensor", "dve": "vector", "act": "scalar", "pool": "gpsimd", "sp": "sync"}
        with nc.Block() as block:
            for e in self.ENGS:
                prog = self.prog[e]
                is_sp = e == "sp"

                def body(engine, prog=prog, is_sp=is_sp):
                    for waits, fn, sem, inc in prog:
                        for s, v in waits:
                            engine.wait_ge(s, v)
                        fn(engine).then_inc(sem, inc)
                    if is_sp:
                        for s, v in final:
                            engine.wait_ge(s, v)

                getattr(block, engmap[e])(body)


def bc(ap, shape_steps):
    a = list(ap.ap)
    return bass.AP(tensor=ap.tensor, offset=ap.offset, ap=[list(a[0])] + [list(x) for x in shape_steps])
import math
from contextlib import ExitStack
import numpy as np
from concourse.bass_utils import run_bass_kernel_spmd


EPS = 1e-6
NT = 16
NS = 16
IN_E = 3584
IN_O = 3852
NPHYS = 2560


def host_consts():
    c = {}
    idx = np.arange(128)
    c["c_ident"] = np.eye(128, dtype=np.float32)
    c["c_mle"] = (idx[:, None] <= idx[None, :]).astype(np.float32)
    c["c_mlt"] = (idx[:, None] < idx[None, :]).astype(np.float32)
    c["c_su"] = (idx[:, None] > idx[None, :]).astype(np.float32)
    c["c_ones"] = np.ones((128, 128), np.float32)
    half = 32
    inv = 1.0 / (10000.0 ** (np.arange(half, dtype=np.float32) / half))
    pos = (np.arange(NT)[None, :] * 128 + idx[:, None]).astype(np.float32)
    ang = pos[:, :, None] * inv[None, None, :]
    c["c_rope_p"] = np.concatenate([np.cos(ang), np.sin(ang)], axis=-1).astype(np.float32)
    angs = np.float32(2048.0) * inv
    c["c_rope_s"] = np.tile(np.concatenate([np.cos(angs), np.sin(angs)])[None, :], (16, 1)).astype(np.float32)
    pm = np.zeros((128, 12, 128), np.float32)
    for g, w in enumerate((2, 4, 8, 16)):
        for t in range(128):
            cnt0 = min(t + 1, w)
            for tp in range(max(0, t - w + 1), t + 1):
                pm[tp, g * 3 + 0, t] += 1.0 / cnt0
                pm[tp, g * 3 + 1, t] += 1.0 / w
            pm[t, g * 3 + 0, t] -= 1.0
            pm[t, g * 3 + 1, t] -= 1.0
            for tp in range(128):
                if tp - 128 >= t - w + 1:
                    pm[tp, g * 3 + 2, t] += 1.0 / w
    c["c_pool"] = pm
    i16 = np.eye(16, dtype=np.float32).reshape(1, 256)
    c["c_i16"] = np.tile(i16, (128, 1))
    sel = np.zeros((16, 16, 128), np.float32)
    for b in range(16):
        sel[b, b, :] = 1.0
    c["c_sel"] = sel
    c["c_iota"] = np.stack([idx + e * NPHYS * 128 for e in range(2)], axis=1).astype(np.float32)
    return c


def build():
    nc = bass.Bass("TRN2", target_bir_lowering=False)
    S = Sched(nc)
    es = ExitStack()

    def DI(name, shape, dt=F32):
        return nc.dram_tensor(name, list(shape), dt, kind="ExternalInput").ap()

    def DO(name, shape, dt=F32):
        return nc.dram_tensor(name, list(shape), dt, kind="ExternalOutput").ap()

    xp = DI("xp", [2048, 1024]); xs = DI("xs", [16, 1024])
    ck = DI("ck", [2 * NPHYS * 128, 768]); cv = DI("cv", [2 * NPHYS * 128, 768])
    spool = DI("spool", [2, 16, 15, 256]); sconv = DI("sconv", [2, 16, 3, 2304])
    sdelta = DI("sdelta", [2, 16, 6, 128, 128]); pt = DI("pt", [1, 256], I32)
    norm_w = DI("norm_w", [4, 1024]); w_in_e = DI("w_in_e", [2, 1024, IN_E]); w_out_e = DI("w_out_e", [2, 1024, 1024])
    pool_w = DI("pool_w", [2, 4, 64, 64]); pool_scale = DI("pool_scale", [2, 256])
    qn_w = DI("qn_w", [2, 64]); kn_w = DI("kn_w", [2, 64]); lam_qk = DI("lam_qk", [2, 256]); subln_w = DI("subln_w", [2, 128])
    w_in_o = DI("w_in_o", [2, 1024, IN_O]); w_out_o = DI("w_out_o", [2, 1024, 1024])
    conv_w = DI("conv_w", [2, 4, 2304]); a_log = DI("a_log", [2, 6]); dt_bias = DI("dt_bias", [2, 6])
    onorm_w = DI("onorm_w", [2, 128]); vnorm_w = DI("vnorm_w", [2, 256]); w_s = DI("w_s", [2, 4, 128, 128]); b_s = DI("b_s", [2, 4, 128])
    cst = {k: DI(k, v.shape) for k, v in host_consts().items()}

    y_p = DO("y_p", [2048, 1024]); y_s = DO("y_s", [16, 1024])
    nk_p = DO("nk_p", [2, 2048, 768]); nv_p = DO("nv_p", [2, 2048, 768])
    nk_s = DO("nk_s", [2, 16, 768]); nv_s = DO("nv_s", [2, 16, 768])
    npool_p = DO("npool_p", [2, 15, 256]); npool_s = DO("npool_s", [2, 16, 15, 256])
    nconv_p = DO("nconv_p", [2, 3, 2304]); nconv_s = DO("nconv_s", [2, 16, 3, 2304])
    ndelta_p = DO("ndelta_p", [2, 6, 128, 128]); ndelta_s = DO("ndelta_s", [2, 16, 6, 128, 128])
    nsgv_s = DO("nsgv_s", [2, 16, 256])
    scr = nc.dram_tensor("scr", [2, 6, 16, 772], F32, kind="Internal").ap()

    used_names = {}

    def T(stack, name, shape, dt=F32):
        n = used_names.get(name, 0)
        used_names[name] = n + 1
        return stack.enter_context(nc.sbuf_tensor(name if n == 0 else f"{name}_{n}", list(shape), dt))

    G = es
    pb = [nc.alloc_psum_tensor(f"pb{i}", [128, 512], F32) for i in range(6)]
    pT1 = nc.alloc_psum_tensor("pT1", [128, 8, 128], BF16)
    pT2 = nc.alloc_psum_tensor("pT2", [128, 8, 128], BF16)
    PB = [f"pb{i}" for i in range(6)]

    identf = T(G, "identf", [128, 128]); identb = T(G, "identb", [128, 128], BF16)
    mle = T(G, "mle", [128, 128]); mlt = T(G, "mlt", [128, 128]); su = T(G, "su", [128, 128]); ones = T(G, "ones", [128, 128])
    ropep = T(G, "ropep", [128, NT, 64]); ropes = T(G, "ropes", [16, 64])
    i16 = T(G, "i16", [128, 256]); iota = T(G, "iota", [128, 2])
    nwbc = T(G, "nwbc", [128, 1024])
    xs_sb = T(G, "xs_sb", [16, 1024])
    ptb = T(G, "ptb", [128, 256], I32); gidx = T(G, "gidx", [128, 2, 256], I32)
    wout = T(G, "wout", [128, 8, 1024], BF16)
    win = T(G, "win", [128, 8, IN_O], BF16)
    ssn = T(G, "ssn", [128, 4]); hb = T(G, "hb", [128, 1024], BF16); hT = T(G, "hT", [128, 8, 128], BF16)
    mT = T(G, "mT", [128, 8, 128], BF16)
    pre = T(G, "pre", [128, 1024]); sgt = T(G, "sgt", [128, 1024], BF16); mixed = T(G, "mixed", [128, 1024], BF16)
    junk = mixed
    otok = T(G, "otok", [128, 6, 128]); wk768 = T(G, "wk768", [128, 768]); s6 = T(G, "s6", [128, 16])

    def ld(eng, out, in_, wkey):
        S.dma(eng, lambda e: e.dma_start(out=out, in_=in_), writes=[wkey])

    for nm, t in (("c_ident", identf), ("c_mle", mle), ("c_mlt", mlt), ("c_su", su), ("c_ones", ones),
                  ("c_rope_p", ropep), ("c_rope_s", ropes), ("c_i16", i16), ("c_iota", iota)):
        ld("sp", t[:], cst[nm], {"c_ident": "identf", "c_mle": "mle", "c_mlt": "mlt", "c_su": "su", "c_ones": "ones", "c_rope_p": "ropep",
                                 "c_rope_s": "ropes", "c_i16": "i16", "c_iota": "iota"}[nm])
    KN = lambda t: t.name
    S.op("dve", lambda e: e.tensor_copy(out=identb[:], in_=identf[:]), reads=["identf"], writes=["identb"])
    ld("sp", xs_sb[:], xs, "xs_sb")
    S.dma("sp", lambda e: e.dma_start(out=ptb[:], in_=pt.partition_broadcast(128)), writes=["ptb"])
    for e_ in range(2):
        S.op("dve", lambda e, e_=e_: e.tensor_scalar(out=gidx[:, e_, :], in0=ptb[:], scalar1=128.0, scalar2=iota[:, e_:e_ + 1],
                                                    op0=ALU.mult, op1=ALU.add), reads=["ptb", "iota"], writes=["gidx"])

    def V(ap, steps):
        return bc(ap, steps)

    def rstd_from_ss(P, ss_ap, out_ap, n, width, rk, wk):
        S.op("act", lambda e: e.activation(out=out_ap, in_=ss_ap, func=AF.Ln, scale=1.0 / n, bias=EPS), reads=rk, writes=wk)
        S.op("act", lambda e: e.activation(out=out_ap, in_=out_ap, func=AF.Exp, scale=-0.5), reads=wk, writes=wk)

    def head(P, x_ap, xkey):
        S.op("act", lambda e: e.activation(out=junk[:P], in_=x_ap, func=AF.Square, accum_out=ssn[:P, 0:1]), reads=[xkey], writes=["mixed", "ssn"])
        rstd_from_ss(P, ssn[:P, 0:1], ssn[:P, 1:2], 1024, 1, ["ssn"], ["ssn"])
        S.op("dve", lambda e: e.scalar_tensor_tensor(out=hb[:P], in0=x_ap, scalar=ssn[:P, 1:2], in1=nwbc[:P], op0=ALU.mult, op1=ALU.mult),
             reads=[xkey, "ssn", "nwbc"], writes=["hb"])
        for kc in range(8):
            S.op("pe", lambda e, kc=kc: e.transpose(out=pT1[:, kc, :P], in_=hb[:P, kc * 128:(kc + 1) * 128], identity=identb[:P, :P]),
                 reads=["hb", "identb"], writes=["pT1"])
        S.op("act", lambda e: e.copy(out=hT[:, :, :P], in_=pT1[:, :, :P]), reads=["pT1"], writes=["hT"])

    def inproj(P, c0, n, ps_ap, pkey):
        for kc in range(8):
            S.op("pe", lambda e, kc=kc: e.matmul(ps_ap, lhsT=hT[:, kc, :P], rhs=win[:, kc, c0:c0 + n], start=(kc == 0), stop=(kc == 7)),
                 reads=["hT", "win"], writes=[pkey])

    def tail(P, x_ap, xkey, xo_ap, xokey):
        S.op("dve", lambda e: e.tensor_tensor(out=mixed[:P], in0=pre[:P], in1=sgt[:P], op=ALU.mult), reads=["pre", "sgt"], writes=["mixed"])
        for kc in range(8):
            S.op("pe", lambda e, kc=kc: e.transpose(out=pT1[:, kc, :P], in_=mixed[:P, kc * 128:(kc + 1) * 128], identity=identb[:P, :P]),
                 reads=["mixed", "identb"], writes=["pT1"])
        S.op("act", lambda e: e.copy(out=mT[:, :, :P], in_=pT1[:, :, :P]), reads=["pT1"], writes=["mT"])
        for hf in range(2):
            for kc in range(8):
                S.op("pe", lambda e, kc=kc, hf=hf: e.matmul(pb[hf][:P, :], lhsT=mT[:, kc, :P], rhs=wout[:, kc, hf * 512:(hf + 1) * 512],
                                                          start=(kc == 0), stop=(kc == 7)), reads=["mT", "wout"], writes=[PB[hf]])
            S.op("dve", lambda e, hf=hf: e.tensor_tensor(out=xo_ap[:, hf * 512:(hf + 1) * 512], in0=pb[hf][:P, :], in1=x_ap[:, hf * 512:(hf + 1) * 512], op=ALU.add),
                 reads=[PB[hf], xkey], writes=[xokey])

    def gate_silu(P, c0):
        for hf in range(2):
            inproj(P, c0 + hf * 512, 512, pb[2 + hf][:P, :], PB[2 + hf])
            S.op("act", lambda e, hf=hf: e.activation(out=sgt[:P, hf * 512:(hf + 1) * 512], in_=pb[2 + hf][:P, :], func=AF.Silu),
                 reads=[PB[2 + hf]], writes=["sgt"])

    def headnorm(P, wbc, wkey, factor, c0):
        S.op("dve", lambda e: e.tensor_tensor(out=wk768[:P], in0=otok[:P].rearrange("p h d -> p (h d)"), in1=otok[:P].rearrange("p h d -> p (h d)"), op=ALU.mult),
             reads=["otok"], writes=["wk768"])
        S.op("dve", lambda e: e.tensor_reduce(out=s6[:P, 0:6], in_=wk768[:P].rearrange("p (h d) -> p h d", d=128), axis=AX.X, op=ALU.add),
             reads=["wk768"], writes=["s6"])
        rstd_from_ss(P, s6[:P, 0:6], s6[:P, 6:12], 128, 6, ["s6"], ["s6"])
        S.op("dve", lambda e: e.tensor_tensor(out=otok[:P], in0=otok[:P], in1=V(s6[:P, 6:12], [[1, 6], [0, 128]]), op=ALU.mult),
             reads=["otok", "s6"], writes=["otok"])
        S.op("dve", lambda e: e.scalar_tensor_tensor(out=pre[:P, c0:c0 + 768].rearrange("p (h d) -> p h d", d=128), in0=otok[:P], scalar=float(factor),
                                                    in1=V(wbc[:P, 0:128], [[0, 6], [1, 128]]), op0=ALU.mult, op1=ALU.mult),
             reads=["otok", wkey], writes=["pre"])

    def qk_norm_rope(P, src_aps, srckeys, wbc, wkey, cos_ap, sin_ap, ropekey, dst, dkey, W):
        sq, xn, t1, t2, s12 = W
        for i, (sa, sk) in enumerate(zip(src_aps, srckeys)):
            S.op("act", lambda e, sa=sa, i=i: e.activation(out=sq[:P, i * 384:(i + 1) * 384], in_=sa, func=AF.Square), reads=[sk], writes=["qk_sq"])
        S.op("dve", lambda e: e.tensor_reduce(out=s12[:P, 0:12], in_=sq[:P].rearrange("p (g d) -> p g d", d=64), axis=AX.X, op=ALU.add),
             reads=["qk_sq"], writes=["qk_s12"])
        rstd_from_ss(P, s12[:P, 0:12], s12[:P, 12:24], 64, 12, ["qk_s12"], ["qk_s12"])
        for i, (sa, sk) in enumerate(zip(src_aps, srckeys)):
            S.op("dve", lambda e, sa=sa, i=i: e.tensor_tensor(out=xn[:P, i * 6:(i + 1) * 6, :], in0=sa.rearrange("p (g d) -> p g d", d=64),
                                                            in1=V(s12[:P, 12 + i * 6:18 + i * 6], [[1, 6], [0, 64]]), op=ALU.mult),
                 reads=[sk, "qk_s12"], writes=["qk_xn"])
        S.op("dve", lambda e: e.tensor_tensor(out=xn[:P], in0=xn[:P], in1=V(wbc[:P, 0:64], [[0, 12], [1, 64]]), op=ALU.mult),
             reads=["qk_xn", wkey], writes=["qk_xn"])
        cosb = V(cos_ap, [[0, 12], [1, 32]]); sinb = V(sin_ap, [[0, 12], [1, 32]])
        x1 = xn[:P, :, 0:32]; x2 = xn[:P, :, 32:64]
        S.op("dve", lambda e: e.tensor_tensor(out=t1[:P], in0=x1, in1=cosb, op=ALU.mult), reads=["qk_xn", ropekey], writes=["qk_t1"])
        S.op("dve", lambda e: e.tensor_tensor(out=t2[:P], in0=x2, in1=sinb, op=ALU.mult), reads=["qk_xn", ropekey], writes=["qk_t2"])
        S.op("dve", lambda e: e.tensor_tensor(out=dst[:P, :, 0:32], in0=t1[:P], in1=t2[:P], op=ALU.subtract), reads=["qk_t1", "qk_t2"], writes=[dkey])
        S.op("dve", lambda e: e.tensor_tensor(out=t1[:P], in0=x2, in1=cosb, op=ALU.mult), reads=["qk_xn", ropekey, dkey], writes=["qk_t1"])
        S.op("dve", lambda e: e.tensor_tensor(out=t2[:P], in0=x1, in1=sinb, op=ALU.mult), reads=["qk_xn", ropekey, dkey], writes=["qk_t2"])
        S.op("dve", lambda e: e.tensor_tensor(out=dst[:P, :, 32:64], in0=t1[:P], in1=t2[:P], op=ALU.add), reads=["qk_t1", "qk_t2"], writes=[dkey])

    def load_weights(layer):
        e_ = layer // 2
        if layer % 2 == 0:
            src, n, so = w_in_e[e_], IN_E, w_out_e[e_]
        else:
            src, n, so = w_in_o[e_], IN_O, w_out_o[e_]
        for kc in range(8):
            S.dma("pool", lambda e, kc=kc: e.dma_start(out=win[:, kc, 0:n], in_=src[kc * 128:(kc + 1) * 128, :]), writes=["win"])
        S.dma("pool", lambda e: e.dma_start(out=wout[:], in_=so.rearrange("(kc p) n -> p kc n", p=128)), writes=["wout"])
        S.dma("sp", lambda e: e.dma_start(out=nwbc[:], in_=norm_w[layer:layer + 1, :].partition_broadcast(128).rearrange("p o n -> p (o n)")), writes=["nwbc"])

    def even_layer(layer):
        e_ = layer // 2
        lam_init = 0.8 - 0.6 * math.exp(-0.3 * layer)
        load_weights(layer)
        with ExitStack() as L:
            pwt = T(L, "pwt", [64, 4, 64]); psc = T(L, "psc", [128, 256]); qnb = T(L, "qnb", [128, 64]); knb = T(L, "knb", [128, 64])
            lqb = T(L, "lqb", [128, 256]); slb = T(L, "slb", [128, 128]); lamc = T(L, "lamc", [128, 8])
            S.dma("sp", lambda e: e.dma_start(out=pwt[:], in_=pool_w[e_].rearrange("g c d -> c g d")), writes=["pwt"])
            for t_, src_, kn_ in ((psc, pool_scale, "psc"), (qnb, qn_w, "qnb"), (knb, kn_w, "knb"), (lqb, lam_qk, "lqb"), (slb, subln_w, "slb")):
                S.dma("sp", lambda e, t_=t_, src_=src_: e.dma_start(out=t_[:], in_=src_[e_:e_ + 1, :].partition_broadcast(128).rearrange("p o n -> p (o n)")), writes=[kn_])
            S.op("dve", lambda e: e.tensor_tensor(out=wk768[:, 0:64], in0=lqb[:, 0:64], in1=lqb[:, 64:128], op=ALU.mult), reads=["lqb"], writes=["wk768"])
            S.op("dve", lambda e: e.tensor_tensor(out=wk768[:, 64:128], in0=lqb[:, 128:192], in1=lqb[:, 192:256], op=ALU.mult), reads=["lqb"], writes=["wk768"])
            S.op("dve", lambda e: e.tensor_reduce(out=lamc[:, 0:2], in_=wk768[:, 0:128].rearrange("p (a d) -> p a d", d=64), axis=AX.X, op=ALU.add), reads=["wk768"], writes=["lamc"])
            S.op("act", lambda e: e.activation(out=lamc[:, 2:4], in_=lamc[:, 0:2], func=AF.Exp), reads=["lamc"], writes=["lamc"])
            S.op("dve", lambda e: e.tensor_tensor(out=lamc[:, 4:5], in0=lamc[:, 2:3], in1=lamc[:, 3:4], op=ALU.subtract), reads=["lamc"], writes=["lamc"])
            S.op("dve", lambda e: e.tensor_scalar(out=lamc[:, 5:6], in0=lamc[:, 4:5], scalar1=float(lam_init), scalar2=-1.0, op0=ALU.add, op1=ALU.mult), reads=["lamc"], writes=["lamc"])

            qkW = (T(L, "qk_sq", [128, 768]), T(L, "qk_xn", [128, 12, 64]), T(L, "qk_t1", [128, 12, 32]), T(L, "qk_t2", [128, 12, 32]), T(L, "qk_s12", [128, 24]))
            with ExitStack() as P1:
                spin = T(P1, "spin", [16, IN_E]); qs = T(P1, "qs", [16, 12, 64]); ks = T(P1, "ks", [16, 12, 64])
                PGU = 2
                NU = 16 // PGU
                NBUF = 4
                Kp = [T(P1, f"Kp{i}", [128, PGU, 768], BF16) for i in range(NBUF)]
                Vp = [T(P1, f"Vp{i}", [128, PGU, 772], BF16) for i in range(NBUF)]
                qbc = T(P1, "qbc", [128, 768], BF16); sc = T(P1, "sc", [128, PGU, 12]); pexp = T(P1, "pexp", [128, PGU, 2, 6], BF16)
                sel = T(P1, "sel", [16, 16, 128])
                ld("sp", sel[:], cst["c_sel"], "sel")
                oall = [T(P1, f"oall{c}", [6, 772]) for c in range(2)]
                otk = [T(P1, f"otk{c}", [16, 6, 129]) for c in range(2)]
                spl = T(P1, "spl", [16, 26, 64]); pl = T(P1, "pl", [16, 256]); plT = T(P1, "plT", [64, 4, 16])
                sm = T(P1, "sm", [16, 64]); tmpv = T(P1, "tmpv", [16, 6, 128])
                num = [otok[:16], wk768[:16].rearrange("p (h d) -> p h d", d=128)]
                NK = ["otok", "wk768"]
                for i in range(NBUF):
                    S.op("pool", lambda e, i=i: e.memset(Vp[i][:, :, 768:772], 1.0), writes=[f"Vp{i}"])
                head(16, xs_sb[:16], "xs_sb")
                for blk in range(7):
                    inproj(16, blk * 512, 512, pb[blk % 4][:16, :], PB[blk % 4])
                    S.op("act" if blk % 2 else "dve",
                         (lambda e, blk=blk: e.copy(out=spin[:, blk * 512:(blk + 1) * 512], in_=pb[blk % 4][:16, :])) if blk % 2 else
                         (lambda e, blk=blk: e.tensor_copy(out=spin[:, blk * 512:(blk + 1) * 512], in_=pb[blk % 4][:16, :])),
                         reads=[PB[blk % 4]], writes=["spin"])
                S.op("act", lambda e: e.activation(out=sgt[:16], in_=spin[:, 2560:3584], func=AF.Silu), reads=["spin"], writes=["sgt"])
                qk_norm_rope(16, [spin[:, 256:640], spin[:, 640:1024]], ["spin", "spin"], qnb, "qnb", ropes[:16, 0:32], ropes[:16, 32:64], "ropes", qs, "qs", qkW)
                qk_norm_rope(16, [spin[:, 1024:1408], spin[:, 1408:1792]], ["spin", "spin"], knb, "knb", ropes[:16, 0:32], ropes[:16, 32:64], "ropes", ks, "ks", qkW)
                S.dma("sp", lambda e: e.dma_start(out=nk_s[e_], in_=ks[:].rearrange("p g d -> p (g d)")), reads=["ks"])
                S.dma("sp", lambda e: e.dma_start(out=nv_s[e_], in_=spin[:, 1792:2560]), reads=["spin"])
                S.op("dve", lambda e: e.tensor_scalar(out=qs[:], in0=qs[:], scalar1=0.125, scalar2=None, op0=ALU.mult), reads=["qs"], writes=["qs"])
                S.dma("sp", lambda e: e.dma_start(out=npool_s[e_, :, 0:14, :], in_=spool[e_, :, 1:15, :]), key="npool_s1")
                ro = 0
                for g, w in enumerate((2, 4, 8, 16)):
                    S.dma("sp", lambda e, g=g, w=w, ro=ro: e.dma_start(out=spl[:, ro:ro + w - 1, :], in_=spool[e_, :, 16 - w:15, g * 64:(g + 1) * 64]), writes=["spl"])
                    ro += w - 1
                S.dma("sp", lambda e: e.dma_start(out=npool_s[e_, :, 14, :], in_=spin[:, 0:256]), reads=["spin"], key="npool_s2")
                ro = 0
                for g, w in enumerate((2, 4, 8, 16)):
                    S.op("dve", lambda e, g=g, w=w, ro=ro: e.tensor_reduce(out=pl[:, g * 64:(g + 1) * 64], in_=spl[:, ro:ro + w - 1, :].rearrange("p r c -> p c r"),
                                                                  axis=AX.X, op=ALU.add), reads=["spl"], writes=["pl"])
                    ro += w - 1
                    S.op("dve", lambda e, g=g, w=w: e.tensor_scalar(out=pl[:, g * 64:(g + 1) * 64], in0=pl[:, g * 64:(g + 1) * 64], scalar1=1.0 / w, scalar2=None, op0=ALU.mult),
                         reads=["pl"], writes=["pl"])
                    S.op("dve", lambda e, g=g, w=w: e.scalar_tensor_tensor(out=pl[:, g * 64:(g + 1) * 64], in0=spin[:, g * 64:(g + 1) * 64], scalar=1.0 / w - 1.0,
                                                                         in1=pl[:, g * 64:(g + 1) * 64], op0=ALU.mult, op1=ALU.add), reads=["pl", "spin"], writes=["pl"])
                for g in range(4):
                    S.op("pe", lambda e, g=g: e.transpose(out=pb[4][:64, g * 16:(g + 1) * 16], in_=pl[:, g * 64:(g + 1) * 64], identity=identf[:16, :16]),
                         reads=["pl", "identf"], writes=[PB[4]])
                S.op("act", lambda e: e.copy(out=plT[:].rearrange("p g b -> p (g b)"), in_=pb[4][:64, 0:64]), reads=[PB[4]], writes=["plT"])
                for g in range(4):
                    S.op("pe", lambda e, g=g: e.matmul(pb[5][:16, g * 64:(g + 1) * 64], lhsT=plT[:, g, :], rhs=pwt[:, g, :], start=True, stop=True),
                         reads=["plT", "pwt"], writes=[PB[5]])
                S.op("dve", lambda e: e.tensor_tensor(out=pre[:16, 0:256], in0=pb[5][:16, 0:256], in1=psc[:16], op=ALU.mult), reads=[PB[5], "psc"], writes=["pre"])
                for unit in range(16 * NU):
                    b, hf = unit // NU, unit % NU
                    par = unit % NBUF
                    for pg in range(PGU):
                        col = b * 16 + hf * PGU + pg
                        S.dma("pool", lambda e, pg=pg, col=col, par=par: e.indirect_dma_start(
                            out=Kp[par][:, pg, :], out_offset=None, in_=ck,
                            in_offset=bass.IndirectOffsetOnAxis(ap=gidx[:, e_, col:col + 1], axis=0)), reads=["gidx"], writes=[f"Kp{par}"])
                        S.dma("pool", lambda e, pg=pg, col=col, par=par: e.indirect_dma_start(
                            out=Vp[par][:, pg, 0:768], out_offset=None, in_=cv,
                            in_offset=bass.IndirectOffsetOnAxis(ap=gidx[:, e_, col:col + 1], axis=0)), reads=["gidx"], writes=[f"Vp{par}"])
                    if hf == 0:
                        for h2 in range(2):
                            S.op("pe", lambda e, b=b, h2=h2: e.matmul(pb[4 + h2][:, 0:384], lhsT=sel[:, b, :], rhs=qs[:].rearrange("p g d -> p (g d)")[:, h2 * 384:(h2 + 1) * 384],
                                                                    start=True, stop=True), reads=["sel", "qs"], writes=[PB[4 + h2]])
                            S.op("act", lambda e, h2=h2: e.copy(out=qbc[:, h2 * 384:(h2 + 1) * 384], in_=pb[4 + h2][:, 0:384]), reads=[PB[4 + h2]], writes=["qbc"])
                    S.op("dve", lambda e, par=par: e.tensor_tensor(out=Kp[par][:], in0=Kp[par][:], in1=V(qbc[:], [[0, PGU], [1, 768]]), op=ALU.mult),
                         reads=[f"Kp{par}", "qbc"], writes=[f"Kp{par}"])
                    S.op("dve", lambda e, par=par: e.tensor_reduce(out=sc[:].rearrange("p a g -> p (a g)"), in_=Kp[par][:].rearrange("p a (g d) -> p (a g) d", d=64),
                                                                 axis=AX.X, op=ALU.add), reads=[f"Kp{par}"], writes=["sc"])
                    S.op("act", lambda e: e.activation(out=V(pexp[:, 0, 0, 0:1], [[12, PGU], [1, 6], [6, 2]]), in_=sc[:].rearrange("p a (h c) -> p a h c", c=2), func=AF.Exp),
                         reads=["sc"], writes=["pexp"])
                    for c in range(2):
                        for h2 in range(2):
                            n = 384 if h2 == 0 else 385
                            for pg in range(PGU):
                                S.op("pe", lambda e, c=c, h2=h2, pg=pg, n=n, par=par, hf=hf: e.matmul(
                                    pb[c * 2 + h2][:6, 0:n], lhsT=pexp[:, pg, c, :], rhs=Vp[par][:, pg, h2 * 384:h2 * 384 + n],
                                    start=(hf == 0 and pg == 0), stop=(hf == NU - 1 and pg == PGU - 1)), reads=["pexp", f"Vp{par}"], writes=[PB[c * 2 + h2]])
                    if hf == NU - 1:
                        for c in range(2):
                            for h2 in range(2):
                                n = 384 if h2 == 0 else 385
                                S.op("act" if h2 else "dve",
                                     (lambda e, c=c, h2=h2, n=n, b=b: e.copy(out=oall[c][:, h2 * 384:h2 * 384 + n], in_=pb[c * 2 + h2][:6, 0:n])) if h2 else
                                     (lambda e, c=c, h2=h2, n=n, b=b: e.tensor_copy(out=oall[c][:, h2 * 384:h2 * 384 + n], in_=pb[c * 2 + h2][:6, 0:n])),
                                     reads=[PB[c * 2 + h2]], writes=[f"oall{c}"])
                            S.dma("sp", lambda e, c=c, b=b: e.dma_start(out=scr[c, :, b, :], in_=oall[c][:]), reads=[f"oall{c}"], writes=[f"scr{c}"])
                for c in range(2):
                    for h in range(6):
                        S.dma("sp", lambda e, c=c, h=h: e.dma_start(out=otk[c][:, h, 0:128], in_=scr[c, h, :, h * 128:(h + 1) * 128]), reads=[f"scr{c}"], writes=[f"otk{c}"])
                        S.dma("sp", lambda e, c=c, h=h: e.dma_start(out=otk[c][:, h, 128:129], in_=scr[c, h, :, 768:769], allow_slow_non_contiguous=True), reads=[f"scr{c}"], writes=[f"otk{c}"])
                S.op("dve", lambda e: e.tensor_tensor(out=wk768[:16], in0=qs[:].rearrange("p g d -> p (g d)"), in1=ks[:].rearrange("p g d -> p (g d)"), op=ALU.mult),
                     reads=["qs", "ks"], writes=["wk768"])
                S.op("dve", lambda e: e.tensor_reduce(out=sm[:, 0:12], in_=wk768[:16].rearrange("p (g d) -> p g d", d=64), axis=AX.X, op=ALU.add), reads=["wk768"], writes=["sm"])
                S.op("act", lambda e: e.activation(out=sm[:, 12:24], in_=sm[:, 0:12], func=AF.Exp), reads=["sm"], writes=["sm"])
                vs3 = spin[:, 1792:2560].rearrange("p (h d) -> p h d", d=128)
                for c in range(2):
                    pn = V(sm[:, 12 + c:13 + c], [[2, 6], [0, 128]])
                    S.op("dve", lambda e, pn=pn: e.tensor_tensor(out=tmpv[:], in0=vs3, in1=pn, op=ALU.mult), reads=["spin", "sm"], writes=["tmpv"])
                    S.op("dve", lambda e, c=c: e.tensor_tensor(out=num[c], in0=tmpv[:], in1=otk[c][:, :, 0:128], op=ALU.add), reads=["tmpv", f"otk{c}"], writes=[NK[c]])
                    S.op("dve", lambda e, c=c: e.tensor_tensor(out=sm[:, 24 + c * 6:30 + c * 6], in0=otk[c][:, :, 128], in1=V(sm[:, 12 + c:13 + c], [[2, 6]]), op=ALU.add),
                         reads=["sm", f"otk{c}"], writes=["sm"])
                S.op("dve", lambda e: e.reciprocal(out=sm[:, 36:48], in_=sm[:, 24:36]), reads=["sm"], writes=["sm"])
                S.op("dve", lambda e: e.tensor_scalar(out=sm[:, 42:48], in0=sm[:, 42:48], scalar1=lamc[:16, 5:6], scalar2=None, op0=ALU.mult), reads=["sm", "lamc"], writes=["sm"])
                for c in range(2):
                    S.op("dve", lambda e, c=c: e.tensor_tensor(out=num[c], in0=num[c], in1=V(sm[:, 36 + c * 6:42 + c * 6], [[1, 6], [0, 128]]), op=ALU.mult),
                         reads=[NK[c], "sm"], writes=[NK[c]])
                S.op("dve", lambda e: e.tensor_tensor(out=otok[:16], in0=num[0], in1=num[1], op=ALU.add), reads=["otok", "wk768"], writes=["otok"])
                headnorm(16, slb, "slb", 1.0 - lam_init, 256)
                tail(16, xs_sb[:16], "xs_sb", xs_sb[:16], "xs_sb")
            S.fence()
            with ExitStack() as P2:
                KT = T(P2, "KT", [128, 6, 2048], BF16); Va = T(P2, "Va", [128, NT, 6, 129], BF16)
                ub = [T(P2, f"ub{i}", [128, 256]) for i in range(2)]
                qr = T(P2, "qr", [128, 12, 64]); kr = T(P2, "kr", [128, 12, 64]); vf = wk768
                qrb = T(P2, "qrb", [128, 768], BF16); krb = T(P2, "krb", [128, 768], BF16); qT = T(P2, "qT", [128, 6, 128], BF16)
                plT2 = T(P2, "plT2", [64, 4, 128]); ex = [T(P2, f"ex{i}", [128, 4, 128], BF16) for i in range(2)]
                rc = T(P2, "rc", [128, 4]); tmpo = T(P2, "tmpo", [128, 128])
                xin = [T(P2, f"xin{i}", [128, 1024]) for i in range(2)]
                xout = [pre] * 2
                S.op("pool", lambda e: e.memset(Va[:, :, :, 128:129], 1.0), writes=["Va"])
                pool_m = T(P2, "pool_m", [128, 12, 128])
                ld("sp", pool_m[:], cst["c_pool"], "pool_m")
                for i in range(NT):
                    par = i % 2
                    src = xp if layer == 0 else y_p
                    for ii in ([0, 1] if i == 0 else [i + 1]):
                        if ii < NT:
                            S.dma("sp", lambda e, ii=ii, src=src: e.dma_start(out=xin[ii % 2][:], in_=src[ii * 128:(ii + 1) * 128, :]), reads=[f"yd{ii}"], writes=[f"xin{ii % 2}"])
                    head(128, xin[par][:], f"xin{par}")
                    inproj(128, 0, 256, pb[0][:, 0:256], PB[0])
                    S.op("act", lambda e, par=par: e.copy(out=ub[par][:], in_=pb[0][:, 0:256]), reads=[PB[0]], writes=[f"ub{par}"])
                    if i == NT - 1:
                        S.dma("sp", lambda e, par=par: e.dma_start(out=npool_p[e_], in_=ub[par][113:128, :]), reads=[f"ub{par}"])
                    for g in range(4):
                        mi = g * 3 + (0 if i == 0 else 1)
                        S.op("pe", lambda e, g=g, mi=mi, par=par: e.matmul(pb[1][:64, g * 128:(g + 1) * 128], lhsT=ub[par][:, g * 64:(g + 1) * 64], rhs=pool_m[:, mi, :],
                                                                         start=True, stop=(i == 0)), reads=[f"ub{par}", "pool_m"], writes=[PB[1]])
                        if i > 0:
                            S.op("pe", lambda e, g=g, par=par: e.matmul(pb[1][:64, g * 128:(g + 1) * 128], lhsT=ub[1 - par][:, g * 64:(g + 1) * 64], rhs=pool_m[:, g * 3 + 2, :],
                                                                      start=False, stop=True), reads=[f"ub{1 - par}", "pool_m"], writes=[PB[1]])
                    S.op("act", lambda e: e.copy(out=plT2[:].rearrange("p g t -> p (g t)"), in_=pb[1][:64, :]), reads=[PB[1]], writes=["plT2"])
                    for g in range(4):
                        S.op("pe", lambda e, g=g: e.matmul(pb[0][:, 256 + g * 64:256 + (g + 1) * 64], lhsT=plT2[:, g, :], rhs=pwt[:, g, :], start=True, stop=True),
                             reads=["plT2", "pwt"], writes=[PB[0]])
                    S.op("dve", lambda e: e.tensor_tensor(out=pre[:, 0:256], in0=pb[0][:, 256:512], in1=psc[:], op=ALU.mult), reads=[PB[0], "psc"], writes=["pre"])
                    for h2 in range(2):
                        inproj(128, 256 + h2 * 384, 384, pb[2 + h2][:, 0:384], PB[2 + h2])
                    qk_norm_rope(128, [pb[2][:, 0:384], pb[3][:, 0:384]], [PB[2], PB[3]], qnb, "qnb", ropep[:, i, 0:32], ropep[:, i, 32:64], "ropep", qr, "qr", qkW)
                    S.op("act", lambda e: e.activation(out=qrb[:], in_=qr[:].rearrange("p g d -> p (g d)"), func=AF.Copy, scale=0.125), reads=["qr"], writes=["qrb"])
                    for h in range(6):
                        S.op("pe", lambda e, h=h: e.transpose(out=pT2[:, h, :], in_=qrb[:, h * 128:(h + 1) * 128], identity=identb[:]), reads=["qrb", "identb"], writes=["pT2"])
                    S.op("dve", lambda e: e.tensor_copy(out=qT[:], in_=pT2[:, 0:6, :]), reads=["pT2"], writes=["qT"])
                    for h2 in range(2):
                        inproj(128, 1024 + h2 * 384, 384, pb[4 + h2][:, 0:384], PB[4 + h2])
                    qk_norm_rope(128, [pb[4][:, 0:384], pb[5][:, 0:384]], [PB[4], PB[5]], knb, "knb", ropep[:, i, 0:32], ropep[:, i, 32:64], "ropep", kr, "kr", qkW)
                    S.dma("sp", lambda e, i=i: e.dma_start(out=nk_p[e_, i * 128:(i + 1) * 128, :], in_=kr[:].rearrange("p g d -> p (g d)")), reads=["kr"])
                    S.op("act", lambda e: e.copy(out=krb[:], in_=kr[:].rearrange("p g d -> p (g d)")), reads=["kr"], writes=["krb"])
                    for h in range(6):
                        S.op("pe", lambda e, h=h: e.transpose(out=pT2[:, h, :], in_=krb[:, h * 128:(h + 1) * 128], identity=identb[:]), reads=["krb", "identb"], writes=["pT2"])
                    S.op("dve", lambda e, i=i: e.tensor_copy(out=KT[:, :, i * 128:(i + 1) * 128], in_=pT2[:, 0:6, :]), reads=["pT2"], writes=["KT"])
                    for h2 in range(2):
                        inproj(128, 1792 + h2 * 384, 384, pb[2 + h2][:, 0:384], PB[2 + h2])
                        S.op("act", lambda e, h2=h2: e.copy(out=vf[:, h2 * 384:(h2 + 1) * 384], in_=pb[2 + h2][:, 0:384]), reads=[PB[2 + h2]], writes=["wk768"])
                    S.dma("sp", lambda e, i=i: e.dma_start(out=nv_p[e_, i * 128:(i + 1) * 128, :], in_=vf[:]), reads=["wk768"])
                    S.op("dve", lambda e, i=i: e.tensor_copy(out=Va[:, i, :, 0:128], in_=vf[:].rearrange("p (h d) -> p h d", d=128)), reads=["wk768"], writes=["Va"])
                    gate_silu(128, 2560)
                    ngrp = (i + 4) // 4
                    groups = []
                    for h in range(6):
                        for c in range(2):
                            for jg in range(ngrp):
                                groups.append((h, c, jg, list(range(jg * 4, min(jg * 4 + 4, i + 1)))))

                    def emit_qk(gidx_):
                        h, c, jg, js = groups[gidx_]
                        sl = gidx_ % 2
                        for jj, j in enumerate(js):
                            S.op("pe", lambda e, jj=jj, j=j, h=h, c=c, sl=sl: e.matmul(pb[sl][:, jj * 128:(jj + 1) * 128], lhsT=KT[c * 64:(c + 1) * 64, h, j * 128:(j + 1) * 128],
                                                                                     rhs=qT[c * 64:(c + 1) * 64, h, :], start=True, stop=True), reads=["KT", "qT"], writes=[PB[sl]])
                    emit_qk(0)
                    for gidx_, (h, c, jg, js) in enumerate(groups):
                        sl = gidx_ % 2
                        ob = 4 if h % 2 == 0 else 2
                        if gidx_ + 1 < len(groups):
                            emit_qk(gidx_ + 1)
                        n = len(js)
                        S.op("act", lambda e, n=n, sl=sl: e.activation(out=ex[sl][:, 0:n, :].rearrange("p a q -> p (a q)"), in_=pb[sl][:, 0:n * 128], func=AF.Exp),
                             reads=[PB[sl]], writes=[f"ex{sl}"])
                        if js[-1] == i:
                            S.op("dve", lambda e, n=n, sl=sl: e.tensor_tensor(out=ex[sl][:, n - 1, :], in0=ex[sl][:, n - 1, :], in1=mle[:], op=ALU.mult),
                                 reads=[f"ex{sl}", "mle"], writes=[f"ex{sl}"])
                        for jj, j in enumerate(js):
                            S.op("pe", lambda e, jj=jj, j=j, h=h, c=c, sl=sl, ob=ob: e.matmul(pb[ob + c][:, 0:129], lhsT=ex[sl][:, jj, :], rhs=Va[:, j, h, :],
                                                                                            start=(j == 0), stop=(j == i)), reads=[f"ex{sl}", "Va"], writes=[PB[ob + c]])
                        if jg == ngrp - 1:
                            S.op("dve", lambda e, c=c, ob=ob: e.reciprocal(out=rc[:, c:c + 1], in_=pb[ob + c][:, 128:129]), reads=[PB[ob + c]], writes=["rc"])
                            if c == 1:
                                S.op("dve", lambda e: e.tensor_tensor(out=rc[:, 2:3], in0=rc[:, 1:2], in1=lamc[:, 5:6], op=ALU.mult), reads=["rc", "lamc"], writes=["rc"])
                                S.op("act", lambda e, ob=ob: e.activation(out=tmpo[:], in_=pb[ob + 1][:, 0:128], func=AF.Copy, scale=rc[:, 2:3]), reads=[PB[ob + 1], "rc"], writes=["tmpo"])
                                S.op("dve", lambda e, h=h, ob=ob: e.scalar_tensor_tensor(out=otok[:, h, :], in0=pb[ob][:, 0:128], scalar=rc[:, 0:1], in1=tmpo[:], op0=ALU.mult, op1=ALU.add),
                                     reads=[PB[ob], "rc", "tmpo"], writes=["otok"])
                    headnorm(128, slb, "slb", 1.0 - lam_init, 256)
                    tail(128, xin[par][:], f"xin{par}", pre[:], "pre")
                    S.dma("sp", lambda e, i=i, par=par: e.dma_start(out=y_p[i * 128:(i + 1) * 128, :], in_=pre[:]), reads=["pre"], writes=[f"yd{i}"])
            S.fence()

    GC1 = 0.044715
    GC2 = 1.5957691216057308

    def gelu(P, src_ap, srckey, x2t, gl):
        S.op("act", lambda e: e.activation(out=x2t[:P], in_=src_ap, func=AF.Square), reads=[srckey], writes=["x2t"])
        S.op("dve", lambda e: e.tensor_scalar(out=x2t[:P], in0=x2t[:P], scalar1=GC1, scalar2=1.0, op0=ALU.mult, op1=ALU.add), reads=["x2t"], writes=["x2t"])
        S.op("dve", lambda e: e.tensor_tensor(out=x2t[:P], in0=x2t[:P], in1=src_ap, op=ALU.mult), reads=["x2t", srckey], writes=["x2t"])
        S.op("act", lambda e: e.activation(out=x2t[:P], in_=x2t[:P], func=AF.Sigmoid, scale=GC2), reads=["x2t"], writes=["x2t"])
        S.op("dve", lambda e: e.tensor_tensor(out=gl[:P], in0=x2t[:P], in1=src_ap, op=ALU.mult), reads=["x2t", srckey], writes=["gl"])

    def sg_vv(P, x2t, gl, g8, vnw, vv_out, vvkey):
        S.op("dve", lambda e: e.tensor_tensor(out=x2t[:P, 0:256], in0=gl[:P, 256:512], in1=gl[:P, 256:512], op=ALU.mult), reads=["gl"], writes=["x2t"])
        S.op("dve", lambda e: e.tensor_reduce(out=g8[:P, 0:4], in_=x2t[:P, 0:256].rearrange("p (g d) -> p g d", d=64), axis=AX.X, op=ALU.add), reads=["x2t"], writes=["g8"])
        rstd_from_ss(P, g8[:P, 0:4], g8[:P, 4:8], 64, 4, ["g8"], ["g8"])
        S.op("dve", lambda e: e.tensor_tensor(out=x2t[:P, 0:256].rearrange("p (g d) -> p g d", d=64), in0=gl[:P, 256:512].rearrange("p (g d) -> p g d", d=64),
                                              in1=V(g8[:P, 4:8], [[1, 4], [0, 64]]), op=ALU.mult), reads=["gl", "g8", "x2t"], writes=["x2t"])
        S.op("dve", lambda e: e.tensor_tensor(out=vv_out, in0=x2t[:P, 0:256], in1=vnw[:P], op=ALU.mult), reads=["x2t", "vnw"], writes=[vvkey])

    def gate_scalars(P, b_ap, a_ap, srckey, gs, dtb, nea):
        S.op("act", lambda e: e.activation(out=gs[:P, 12:18], in_=b_ap, func=AF.Exp, scale=-1.0), reads=[srckey], writes=["gs"])
        S.op("dve", lambda e: e.tensor_scalar(out=gs[:P, 12:18], in0=gs[:P, 12:18], scalar1=1.0, scalar2=None, op0=ALU.add), reads=["gs"], writes=["gs"])
        S.op("dve", lambda e: e.reciprocal(out=gs[:P, 0:6], in_=gs[:P, 12:18]), reads=["gs"], writes=["gs"])
        S.op("dve", lambda e: e.tensor_tensor(out=gs[:P, 18:24], in0=a_ap, in1=dtb[:P], op=ALU.add), reads=[srckey, "dtb"], writes=["gs"])
        S.op("dve", lambda e: e.tensor_scalar(out=gs[:P, 24:30], in0=gs[:P, 18:24], scalar1=-1.0, scalar2=None, op0=ALU.mult), reads=["gs"], writes=["gs"])
        S.op("dve", lambda e: e.tensor_tensor(out=gs[:P, 24:30], in0=gs[:P, 24:30], in1=gs[:P, 18:24], op=ALU.max), reads=["gs"], writes=["gs"])
        S.op("act", lambda e: e.activation(out=gs[:P, 24:30], in_=gs[:P, 24:30], func=AF.Exp, scale=-1.0), reads=["gs"], writes=["gs"])
        S.op("act", lambda e: e.activation(out=gs[:P, 24:30], in_=gs[:P, 24:30], func=AF.Ln, bias=1.0), reads=["gs"], writes=["gs"])
        S.op("dve", lambda e: e.tensor_scalar(out=gs[:P, 18:24], in0=gs[:P, 18:24], scalar1=0.0, scalar2=None, op0=ALU.max), reads=["gs"], writes=["gs"])
        S.op("dve", lambda e: e.tensor_tensor(out=gs[:P, 18:24], in0=gs[:P, 18:24], in1=gs[:P, 24:30], op=ALU.add), reads=["gs"], writes=["gs"])
        S.op("dve", lambda e: e.tensor_tensor(out=gs[:P, 6:12], in0=gs[:P, 18:24], in1=nea[:P], op=ALU.mult), reads=["gs", "nea"], writes=["gs"])

    def odd_layer(layer):
        e_ = layer // 2
        DKS = 128 ** -0.5
        load_weights(layer)
        with ExitStack() as L:
            onb = T(L, "onb", [128, 128]); vnw = T(L, "vnw", [128, 256]); dtb = T(L, "dtb", [128, 6]); nea = T(L, "nea", [128, 6])
            gs = T(L, "gs", [128, 64]); g8 = T(L, "g8", [128, 8]); x2t = T(L, "x2t", [128, 512]); gl = T(L, "gl", [128, 512])
            for t_, src_, kn_ in ((onb, onorm_w, "onb"), (vnw, vnorm_w, "vnw"), (dtb, dt_bias, "dtb"), (nea, a_log, "nea")):
                S.dma("sp", lambda e, t_=t_, src_=src_: e.dma_start(out=t_[:], in_=src_[e_:e_ + 1, :].partition_broadcast(128).rearrange("p o n -> p (o n)")), writes=[kn_])
            S.op("act", lambda e: e.activation(out=nea[:], in_=nea[:], func=AF.Exp), reads=["nea"], writes=["nea"])
            S.op("dve", lambda e: e.tensor_scalar(out=nea[:], in0=nea[:], scalar1=-1.0, scalar2=None, op0=ALU.mult), reads=["nea"], writes=["nea"])
            with ExitStack() as P1:
                spin = T(P1, "spino", [16, IN_O]); cwb = T(P1, "cwb", [16, 4, 576]); scv = T(P1, "scv", [16, 3, 576])
                cvs = T(P1, "cvs", [16, 2304]); tmpc = T(P1, "tmpc", [16, 576])
                qn = T(P1, "qn", [16, 6, 128]); kn = T(P1, "kn", [16, 6, 128]); kqT = T(P1, "kqT", [128, 12, 16])
                kTm = T(P1, "kTm", [128, 16, 16]); qTm = T(P1, "qTm", [128, 16, 16])
                Sst = [T(P1, f"Sst{i}", [128, 16, 128]) for i in range(2)]
                Rh = T(P1, "Rh", [16, 16, 128]); dl_s = T(P1, "dl_s", [16, 6, 128]); tmph = T(P1, "tmph", [16, 128])
                Eg = T(P1, "Eg", [16, 6, 16]); egbc = T(P1, "egbc", [128, 6, 16]); tmpS = T(P1, "tmpS", [128, 4, 128])
                wsb = T(P1, "wsb", [16, 8]); vvf = T(P1, "vvf", [16, 256])
                head(16, xs_sb[:16], "xs_sb")
                for blk in range(8):
                    n = 512 if blk < 7 else IN_O - 7 * 512
                    inproj(16, blk * 512, n, pb[blk % 4][:16, 0:n], PB[blk % 4])
                    S.op("act" if blk % 2 else "dve",
                         (lambda e, blk=blk, n=n: e.copy(out=spin[:, blk * 512:blk * 512 + n], in_=pb[blk % 4][:16, 0:n])) if blk % 2 else
                         (lambda e, blk=blk, n=n: e.tensor_copy(out=spin[:, blk * 512:blk * 512 + n], in_=pb[blk % 4][:16, 0:n])),
                         reads=[PB[blk % 4]], writes=["spin"])
                S.op("act", lambda e: e.activation(out=sgt[:16], in_=spin[:, 2828:3852], func=AF.Silu), reads=["spin"], writes=["sgt"])
                S.dma("sp", lambda e: e.dma_start(out=nconv_s[e_, :, 0:2, :], in_=sconv[e_, :, 1:3, :]), key="nconv_s1")
                S.dma("sp", lambda e: e.dma_start(out=nconv_s[e_, :, 2, :], in_=spin[:, 0:2304]), reads=["spin"], key="nconv_s2")
                for ch in range(4):
                    c0 = ch * 576
                    S.dma("sp", lambda e, c0=c0: e.dma_start(out=cwb[:], in_=conv_w[e_, :, c0:c0 + 576].partition_broadcast(16)), writes=["cwb"])
                    S.dma("sp", lambda e, c0=c0: e.dma_start(out=scv[:], in_=sconv[e_, :, :, c0:c0 + 576]), writes=["scv"])
                    S.op("dve", lambda e, c0=c0: e.tensor_tensor(out=cvs[:, c0:c0 + 576], in0=spin[:, c0:c0 + 576], in1=cwb[:, 3, :], op=ALU.mult), reads=["spin", "cwb"], writes=["cvs"])
                    for j in range(3):
                        S.op("dve", lambda e, j=j: e.tensor_tensor(out=tmpc[:], in0=scv[:, j, :], in1=cwb[:, j, :], op=ALU.mult), reads=["scv", "cwb"], writes=["tmpc"])
                        S.op("dve", lambda e, c0=c0: e.tensor_tensor(out=cvs[:, c0:c0 + 576], in0=cvs[:, c0:c0 + 576], in1=tmpc[:], op=ALU.add), reads=["cvs", "tmpc"], writes=["cvs"])
                S.op("act", lambda e: e.activation(out=cvs[:], in_=cvs[:], func=AF.Silu), reads=["cvs"], writes=["cvs"])
                for part, dst, dkey, scl in ((0, qn, "qn", DKS), (1, kn, "kn", 1.0)):
                    xa = cvs[:, part * 768:(part + 1) * 768]
                    S.op("dve", lambda e, xa=xa: e.tensor_tensor(out=wk768[:16], in0=xa, in1=xa, op=ALU.mult), reads=["cvs"], writes=["wk768"])
                    S.op("dve", lambda e: e.tensor_reduce(out=s6[:16, 0:6], in_=wk768[:16].rearrange("p (h d) -> p h d", d=128), axis=AX.X, op=ALU.add), reads=["wk768"], writes=["s6"])
                    rstd_from_ss(16, s6[:16, 0:6], s6[:16, 6:12], 1.0, 6, ["s6"], ["s6"])
                    S.op("dve", lambda e, xa=xa, dst=dst, scl=scl: e.scalar_tensor_tensor(out=dst[:].rearrange("p h d -> p (h d)"), in0=xa, scalar=float(scl),
                                                                                       in1=V(s6[:16, 6:12], [[1, 6], [0, 128]]), op0=ALU.mult, op1=ALU.mult),
                         reads=["cvs", "s6"], writes=[dkey])
                gate_scalars(16, spin[:, 2816:2822], spin[:, 2822:2828], "spin", gs, dtb, nea)
                S.op("act", lambda e: e.activation(out=gs[:16, 30:36], in_=gs[:16, 6:12], func=AF.Exp), reads=["gs"], writes=["gs"])
                S.op("dve", lambda e: e.tensor_tensor(out=wk768[:16], in0=qn[:].rearrange("p h d -> p (h d)"), in1=kn[:].rearrange("p h d -> p (h d)"), op=ALU.mult),
                     reads=["qn", "kn"], writes=["wk768"])
                S.op("dve", lambda e: e.tensor_reduce(out=gs[:16, 36:42], in_=wk768[:16].rearrange("p (h d) -> p h d", d=128), axis=AX.X, op=ALU.add), reads=["wk768"], writes=["gs"])
                for h in range(6):
                    S.op("pe", lambda e, h=h: e.transpose(out=pb[0][:, h * 16:(h + 1) * 16], in_=kn[:16, h, :], identity=identf[:16, :16]), reads=["kn", "identf"], writes=[PB[0]])
                    S.op("pe", lambda e, h=h: e.transpose(out=pb[0][:, 96 + h * 16:96 + (h + 1) * 16], in_=qn[:16, h, :], identity=identf[:16, :16]), reads=["qn", "identf"], writes=[PB[0]])
                S.op("act", lambda e: e.copy(out=kqT[:].rearrange("p a b -> p (a b)"), in_=pb[0][:, 0:192]), reads=[PB[0]], writes=["kqT"])
                S.op("dve", lambda e: e.tensor_tensor(out=Eg[:], in0=V(gs[:16, 30:36], [[1, 6], [0, 16]]), in1=V(identf[:16, 0:16], [[0, 6], [1, 16]]), op=ALU.mult),
                     reads=["gs", "identf"], writes=["Eg"])
                S.op("pe", lambda e: e.matmul(pb[5][:, 0:96], lhsT=ones[:16, :], rhs=Eg[:].rearrange("p h b -> p (h b)"), start=True, stop=True), reads=["ones", "Eg"], writes=[PB[5]])
                S.op("act", lambda e: e.copy(out=egbc[:].rearrange("p h b -> p (h b)"), in_=pb[5][:, 0:96]), reads=[PB[5]], writes=["egbc"])
                i16v = i16[:].rearrange("p (a b) -> p a b", b=16)
                for h in range(6):
                    par = h % 2
                    S.dma("sp", lambda e, h=h, par=par: e.dma_start(out=Sst[par][:], in_=sdelta[e_, :, h].rearrange("b k v -> k b v")), writes=[f"Sst{par}"])
                    S.op("dve", lambda e, h=h: e.tensor_tensor(out=kTm[:], in0=V(kqT[:, h, 0:1], [[0, 16], [1, 16]]), in1=i16v, op=ALU.mult), reads=["kqT", "i16"], writes=["kTm"])
                    S.op("dve", lambda e, h=h: e.tensor_tensor(out=qTm[:], in0=V(kqT[:, 6 + h, 0:1], [[0, 16], [1, 16]]), in1=i16v, op=ALU.mult), reads=["kqT", "i16"], writes=["qTm"])
                    for b in range(16):
                        S.op("pe", lambda e, b=b, par=par: e.matmul(pb[1][:16, 0:128], lhsT=kTm[:, b, :], rhs=Sst[par][:, b, :], start=(b == 0), stop=(b == 15)),
                             reads=["kTm", f"Sst{par}"], writes=[PB[1]])
                    for b in range(16):
                        S.op("pe", lambda e, b=b, par=par: e.matmul(pb[2][:16, 0:128], lhsT=qTm[:, b, :], rhs=Sst[par][:, b, :], start=(b == 0), stop=(b == 15)),
                             reads=["qTm", f"Sst{par}"], writes=[PB[2]])
                    vh = cvs[:, 1536 + h * 128:1536 + (h + 1) * 128]
                    S.op("dve", lambda e, h=h: e.tensor_scalar(out=tmph[:], in0=pb[1][:16, 0:128], scalar1=gs[:16, 30 + h:31 + h], scalar2=None, op0=ALU.mult), reads=[PB[1], "gs"], writes=["tmph"])
                    S.op("dve", lambda e, vh=vh: e.tensor_tensor(out=tmph[:], in0=vh, in1=tmph[:], op=ALU.subtract), reads=["cvs", "tmph"], writes=["tmph"])
                    S.op("dve", lambda e, h=h: e.tensor_scalar(out=dl_s[:, h, :], in0=tmph[:], scalar1=gs[:16, h:h + 1], scalar2=None, op0=ALU.mult), reads=["tmph", "gs"], writes=["dl_s"])
                    S.op("dve", lambda e, h=h: e.tensor_scalar(out=tmph[:], in0=pb[2][:16, 0:128], scalar1=gs[:16, 30 + h:31 + h], scalar2=None, op0=ALU.mult), reads=[PB[2], "gs", "dl_s"], writes=["tmph"])
                    S.op("dve", lambda e, h=h: e.scalar_tensor_tensor(out=otok[:16, h, :], in0=dl_s[:, h, :], scalar=gs[:16, 36 + h:37 + h], in1=tmph[:], op0=ALU.mult, op1=ALU.add),
                         reads=["dl_s", "gs", "tmph"], writes=["otok"])
                    S.op("dve", lambda e, h=h: e.tensor_tensor(out=Rh[:], in0=V(dl_s[:, h, 0:1], [[0, 16], [1, 128]]), in1=V(identf[:16, 0:1], [[1, 16], [0, 128]]), op=ALU.mult),
                         reads=["dl_s", "identf"], writes=["Rh"])
                    for q4 in range(4):
                        bank = 3 + q4 % 2
                        S.op("pe", lambda e, q4=q4, h=h, bank=bank: e.matmul(pb[bank][:, :], lhsT=kn[:16, h, :], rhs=Rh[:, q4 * 4:(q4 + 1) * 4, :].rearrange("p b d -> p (b d)"),
                                                                          start=True, stop=True), reads=["kn", "Rh"], writes=[PB[bank]])
                        S.op("dve", lambda e, q4=q4, h=h, par=par: e.tensor_tensor(out=tmpS[:], in0=Sst[par][:, q4 * 4:(q4 + 1) * 4, :], in1=V(egbc[:, h, q4 * 4:q4 * 4 + 1], [[1, 4], [0, 128]]), op=ALU.mult),
                             reads=[f"Sst{par}", "egbc"], writes=["tmpS"])
                        S.op("dve", lambda e, q4=q4, par=par, bank=bank: e.tensor_tensor(out=Sst[par][:, q4 * 4:(q4 + 1) * 4, :], in0=tmpS[:], in1=pb[bank][:, :].rearrange("p (b d) -> p b d", d=128), op=ALU.add),
                             reads=["tmpS", PB[bank]], writes=[f"Sst{par}"])
                    S.dma("sp", lambda e, h=h, par=par: e.dma_start(out=ndelta_s[e_, :, h].rearrange("b k v -> k b v"), in_=Sst[par][:]), reads=[f"Sst{par}"])
                headnorm(16, onb, "onb", 1.0, 0)
                gelu(16, spin[:, 2304:2816], "spin", x2t, gl)
                sg_vv(16, x2t, gl, g8, vnw, vvf[:], "vvf")
                S.dma("sp", lambda e: e.dma_start(out=nsgv_s[e_], in_=vvf[:]), reads=["vvf"])
                S.dma("sp", lambda e: e.dma_start(out=wsb[:, 0:4], in_=w_s[e_, :, 0, 0:1].rearrange("g o -> o g").partition_broadcast(16).rearrange("p o g -> p (o g)"), allow_slow_non_contiguous=True), writes=["wsb"])
                S.dma("sp", lambda e: e.dma_start(out=wsb[:, 4:8], in_=b_s[e_, :, 0:1].rearrange("g o -> o g").partition_broadcast(16).rearrange("p o g -> p (o g)"), allow_slow_non_contiguous=True), writes=["wsb"])
                S.op("dve", lambda e: e.tensor_tensor(out=vvf[:].rearrange("p (g d) -> p g d", d=64), in0=vvf[:].rearrange("p (g d) -> p g d", d=64), in1=V(wsb[:, 0:4], [[1, 4], [0, 64]]), op=ALU.mult),
                     reads=["vvf", "wsb"], writes=["vvf"])
                S.op("dve", lambda e: e.tensor_tensor(out=vvf[:].rearrange("p (g d) -> p g d", d=64), in0=vvf[:].rearrange("p (g d) -> p g d", d=64), in1=V(wsb[:, 4:8], [[1, 4], [0, 64]]), op=ALU.add),
                     reads=["vvf", "wsb"], writes=["vvf"])
                S.op("dve", lambda e: e.tensor_tensor(out=pre[:16, 768:1024], in0=vvf[:], in1=gl[:16, 0:256], op=ALU.mult), reads=["vvf", "gl"], writes=["pre"])
                tail(16, xs_sb[:16], "xs_sb", xs_sb[:16], "xs_sb")
            S.fence()
            with ExitStack() as P2:
                xin = [T(P2, f"xin{i}", [128, 1024]) for i in range(2)]
                ext = [T(P2, f"ext{i}", [128, 18, 131]) for i in range(2)]
                cvb = T(P2, "cvb", [128, 18, 128]); sqrn = T(P2, "sqrn", [128, 12, 128])
                qkT = T(P2, "qkT", [128, 12, 128], BF16); vT = T(P2, "vT", [128, 6, 128], BF16)
                ktok = T(P2, "ktok", [128, 6, 128], BF16); vtok = T(P2, "vtok", [128, 6, 128], BF16)
                LT = T(P2, "LT", [128, 6, 128]); dec = T(P2, "dec", [128, 6, 128]); dlt = T(P2, "dlt", [128, 6, 128]); dle = dec
                Am = [T(P2, f"Am{i}", [128, 6, 128], BF16) for i in range(2)]
                Bm = [T(P2, f"Bm{i}", [128, 6, 128], BF16) for i in range(2)]
                Qm = [T(P2, f"Qm{i}", [128, 6, 128], BF16) for i in range(2)]
                aq = T(P2, "aq", [128, 6, 128], BF16); rr = T(P2, "rr", [128, 6, 128], BF16); dlb = T(P2, "dlb", [128, 6, 128], BF16); kd = T(P2, "kd", [128, 6, 128], BF16)
                otmp = T(P2, "otmp", [128, 6, 128]); Sf = T(P2, "Sf", [128, 6, 128]); Sb = T(P2, "Sb", [128, 6, 128], BF16)
                gs2 = T(P2, "gs2", [128, 32]); vvb = T(P2, "vvb", [128, 256], BF16)
                wsf = T(P2, "wsf", [128, 4, 128]); WsT = T(P2, "WsT", [128, 4, 128], BF16); bsT = T(P2, "bsT", [128, 4]); cw = T(P2, "cw", [128, 18, 4])
                S.dma("sp", lambda e: e.dma_start(out=wsf[:], in_=w_s[e_].rearrange("g t s -> t g s")), writes=["wsf"])
                S.dma("sp", lambda e: e.dma_start(out=bsT[:], in_=b_s[e_].rearrange("g t -> t g"), allow_slow_non_contiguous=True), writes=["bsT"])
                for j in range(4):
                    S.dma("sp", lambda e, j=j: e.dma_start(out=cw[:, :, j], in_=conv_w[e_, j:j + 1, :].rearrange("o (cb p) -> p (o cb)", p=128), allow_slow_non_contiguous=True), writes=["cw"])
                for g in range(4):
                    S.op("pe", lambda e, g=g: e.transpose(out=pb[0][:, g * 128:(g + 1) * 128], in_=wsf[:, g, :], identity=identf[:]), reads=["wsf", "identf"], writes=[PB[0]])
                S.op("dve", lambda e: e.tensor_tensor(out=WsT[:], in0=pb[0][:, :].rearrange("p (g t) -> p g t", t=128), in1=V(mle[:, 0:1], [[0, 4], [1, 128]]), op=ALU.mult),
                     reads=[PB[0], "mle"], writes=["WsT"])
                S.op("pool", lambda e: e.memset(Sf[:], 0.0), writes=["Sf"])
                S.op("pool", lambda e: e.memset(Sb[:], 0.0), writes=["Sb"])
                CV = [f"cvb{cb}" for cb in range(18)]
                for i in range(NT):
                    par = i % 2
                    for ii in ([0, 1] if i == 0 else [i + 1]):
                        if ii < NT:
                            S.dma("sp", lambda e, ii=ii: e.dma_start(out=xin[ii % 2][:], in_=y_p[ii * 128:(ii + 1) * 128, :]), reads=[f"yd{ii}"], writes=[f"xin{ii % 2}"])
                    head(128, xin[par][:], f"xin{par}")
                    inproj(128, 2304, 512, pb[0][:, :], PB[0])
                    inproj(128, 2816, 12, pb[1][:, 0:12], PB[1])
                    gate_silu(128, 2828)
                    gelu(128, pb[0][:, :], PB[0], x2t, gl)
                    sg_vv(128, x2t, gl, g8, vnw, vvb[:], "vvb")
                    for g in range(4):
                        S.op("pe", lambda e, g=g: e.matmul(pb[4][:, g * 64:(g + 1) * 64], lhsT=WsT[:, g, :], rhs=vvb[:, g * 64:(g + 1) * 64], start=True, stop=True),
                             reads=["WsT", "vvb"], writes=[PB[4]])
                    for g in range(4):
                        S.op("dve", lambda e, g=g: e.scalar_tensor_tensor(out=pre[:, 768 + g * 64:768 + (g + 1) * 64], in0=pb[4][:, g * 64:(g + 1) * 64], scalar=bsT[:, g:g + 1],
                                                                        in1=gl[:, g * 64:(g + 1) * 64], op0=ALU.add, op1=ALU.mult), reads=[PB[4], "bsT", "gl"], writes=["pre"])
                    gate_scalars(128, pb[1][:, 0:6], pb[1][:, 6:12], PB[1], gs, dtb, nea)
                    if i == 0:
                        S.op("pool", lambda e, par=par: e.memset(ext[par][:, :, 0:3], 0.0), writes=[f"ext{par}"])
                    else:
                        S.op("pool", lambda e, par=par: e.tensor_copy(out=ext[par][:, :, 0:3], in_=ext[1 - par][:, :, 128:131]), reads=[f"ext{1 - par}"], writes=[f"ext{par}"])
                    for bk in range(5):
                        cbs = list(range(bk * 4, min(bk * 4 + 4, 18)))
                        bank = 2 + bk % 2
                        for jj, cb in enumerate(cbs):
                            for kc in range(8):
                                S.op("pe", lambda e, jj=jj, cb=cb, kc=kc, bank=bank: e.matmul(pb[bank][:, jj * 128:(jj + 1) * 128], lhsT=win[:, kc, cb * 128:(cb + 1) * 128], rhs=hT[:, kc, :],
                                                                                          start=(kc == 0), stop=(kc == 7)), reads=["win", "hT"], writes=[PB[bank]])
                        n = len(cbs)
                        if bk % 2:
                            S.op("act", lambda e, n=n, bank=bank, cb0=cbs[0], par=par: e.copy(out=ext[par][:, cb0:cb0 + n, 3:131], in_=pb[bank][:, 0:n * 128].rearrange("p (a t) -> p a t", t=128)),
                                 reads=[PB[bank]], writes=[f"ext{par}"])
                        else:
                            S.op("dve", lambda e, n=n, bank=bank, cb0=cbs[0], par=par: e.tensor_copy(out=ext[par][:, cb0:cb0 + n, 3:131], in_=pb[bank][:, 0:n * 128].rearrange("p (a t) -> p a t", t=128)),
                                 reads=[PB[bank]], writes=[f"ext{par}"])
                    if i == NT - 1:
                        for r_ in range(3):
                            S.dma("sp", lambda e, par=par, r_=r_: e.dma_start(out=nconv_p[e_, r_:r_ + 1, :].rearrange("o (cb p) -> p (o cb)", p=128), in_=ext[par][:, :, 128 + r_], allow_slow_non_contiguous=True), reads=[f"ext{par}"])
                    for j in range(4):
                        for cb in range(18):
                            if j == 0:
                                S.op("dve", lambda e, cb=cb, par=par: e.tensor_scalar(out=cvb[:, cb, :], in0=ext[par][:, cb, 0:128], scalar1=cw[:, cb, 0:1], scalar2=None, op0=ALU.mult),
                                     reads=[f"ext{par}", "cw"], writes=[CV[cb]])
                            else:
                                S.op("dve", lambda e, cb=cb, par=par, j=j: e.scalar_tensor_tensor(out=cvb[:, cb, :], in0=ext[par][:, cb, j:j + 128], scalar=cw[:, cb, j:j + 1], in1=cvb[:, cb, :],
                                                                                               op0=ALU.mult, op1=ALU.add), reads=[f"ext{par}", "cw", CV[cb]], writes=[CV[cb]])
                    S.op("act", lambda e: e.activation(out=cvb[:].rearrange("p a t -> p (a t)"), in_=cvb[:].rearrange("p a t -> p (a t)"), func=AF.Silu), reads=CV, writes=CV)
                    S.op("act", lambda e: e.activation(out=sqrn[:].rearrange("p a t -> p (a t)"), in_=cvb[:, 0:12, :].rearrange("p a t -> p (a t)"), func=AF.Square), reads=CV, writes=["sqrn"])
                    for m in range(3):
                        S.op("pe", lambda e, m=m: e.matmul(pb[m][:, :], lhsT=ones[:], rhs=sqrn[:, m * 4:(m + 1) * 4, :].rearrange("p a t -> p (a t)"), start=True, stop=True),
                             reads=["ones", "sqrn"], writes=[PB[m]])
                    for m in range(3):
                        S.op("act", lambda e, m=m: e.activation(out=sqrn[:, m * 4:(m + 1) * 4, :].rearrange("p a t -> p (a t)"), in_=pb[m][:, :], func=AF.Ln, bias=EPS), reads=[PB[m], "sqrn"], writes=["sqrn"])
                    S.op("act", lambda e: e.activation(out=sqrn[:].rearrange("p a t -> p (a t)"), in_=sqrn[:].rearrange("p a t -> p (a t)"), func=AF.Exp, scale=-0.5), reads=["sqrn"], writes=["sqrn"])
                    S.op("dve", lambda e: e.scalar_tensor_tensor(out=qkT[:, 0:6, :].rearrange("p a t -> p (a t)"), in0=cvb[:, 0:6, :].rearrange("p a t -> p (a t)"), scalar=float(DKS),
                                                                in1=sqrn[:, 0:6, :].rearrange("p a t -> p (a t)"), op0=ALU.mult, op1=ALU.mult), reads=CV + ["sqrn"], writes=["qkT"])
                    S.op("dve", lambda e: e.tensor_tensor(out=qkT[:, 6:12, :], in0=cvb[:, 6:12, :], in1=sqrn[:, 6:12, :], op=ALU.mult), reads=CV + ["sqrn"], writes=["qkT"])
                    S.op("act", lambda e: e.copy(out=vT[:], in_=cvb[:, 12:18, :]), reads=CV, writes=["vT"])
                    for h in range(6):
                        S.op("pe", lambda e, h=h: e.transpose(out=pT1[:, h, :], in_=qkT[:, 6 + h, :], identity=identb[:]), reads=["qkT", "identb"], writes=["pT1"])
                        S.op("pe", lambda e, h=h: e.transpose(out=pT2[:, h, :], in_=vT[:, h, :], identity=identb[:]), reads=["vT", "identb"], writes=["pT2"])
                    S.op("dve", lambda e: e.tensor_copy(out=ktok[:], in_=pT1[:, 0:6, :]), reads=["pT1"], writes=["ktok"])
                    S.op("act", lambda e: e.copy(out=vtok[:], in_=pT2[:, 0:6, :]), reads=["pT2"], writes=["vtok"])
                    S.op("dve", lambda e: e.tensor_tensor(out=LT[:], in0=V(mle[:, 0:1], [[0, 6], [1, 128]]), in1=V(gs[:, 6:7], [[1, 6], [0, 128]]), op=ALU.mult), reads=["mle", "gs"], writes=["LT"])
                    for m in range(2):
                        S.op("pe", lambda e, m=m: e.matmul(pb[3 + m][:, 0:384], lhsT=su[:], rhs=LT[:, m * 3:(m + 1) * 3, :].rearrange("p a t -> p (a t)"), start=True, stop=True),
                             reads=["su", "LT"], writes=[PB[3 + m]])
                        S.op("act", lambda e, m=m: e.activation(out=dec[:, m * 3:(m + 1) * 3, :].rearrange("p a t -> p (a t)"), in_=pb[3 + m][:, 0:384], func=AF.Exp), reads=[PB[3 + m]], writes=["dec"])
                    S.op("dve", lambda e: e.tensor_tensor(out=dlt[:], in0=dec[:], in1=V(mlt[:, 0:1], [[0, 6], [1, 128]]), op=ALU.mult), reads=["dec", "mlt"], writes=["dlt"])
                    S.op("dve", lambda e: e.tensor_tensor(out=dec[:], in0=dec[:], in1=V(mle[:, 0:1], [[0, 6], [1, 128]]), op=ALU.mult), reads=["dec", "mle"], writes=["dec"])
                    for m, lm in enumerate((mle, su, ones)):
                        S.op("pe", lambda e, m=m, lm=lm: e.matmul(pb[5][:, m * 6:(m + 1) * 6], lhsT=lm[:], rhs=gs[:, 6:12], start=True, stop=True), reads=["mle", "su", "ones", "gs"], writes=[PB[5]])
                    S.op("act", lambda e: e.activation(out=gs2[:, 0:18], in_=pb[5][:, 0:18], func=AF.Exp), reads=[PB[5]], writes=["gs2"])
                    S.op("dve", lambda e: e.tensor_scalar(out=gs2[:, 18:24], in0=gs[:, 0:6], scalar1=-1.0, scalar2=None, op0=ALU.mult), reads=["gs"], writes=["gs2"])
                    S.op("dve", lambda e: e.tensor_scalar(out=gs2[:, 24:30], in0=gs2[:, 0:6], scalar1=-1.0, scalar2=None, op0=ALU.mult), reads=["gs2"], writes=["gs2"])

                    def hb_(h):
                        return h // 3, slice((h % 3) * 128, (h % 3 + 1) * 128)
                    for h in range(6):
                        bq, cs = hb_(h)
                        S.op("pe", lambda e, h=h, bq=bq, cs=cs: e.matmul(pb[0 + bq][:, cs], lhsT=qkT[:, 6 + h, :], rhs=qkT[:, 6 + h, :], start=True, stop=True), reads=["qkT"], writes=[PB[0 + bq]])
                        S.op("pe", lambda e, h=h, bq=bq, cs=cs: e.matmul(pb[2 + bq][:, cs], lhsT=qkT[:, 6 + h, :], rhs=qkT[:, h, :], start=True, stop=True), reads=["qkT"], writes=[PB[2 + bq]])
                    for h in range(6):
                        bq, cs = hb_(h)
                        S.op("dve", lambda e, h=h, bq=bq, cs=cs: e.scalar_tensor_tensor(out=Bm[0][:, h, :], in0=pb[bq][:, cs], scalar=gs2[:, 18 + h:19 + h], in1=dlt[:, h, :], op0=ALU.mult, op1=ALU.mult),
                             reads=[PB[bq], "gs2", "dlt"], writes=["Bm0"])
                    for m in range(2):
                        S.op("dve", lambda e, m=m: e.tensor_tensor(out=aq[:, m * 3:(m + 1) * 3, :].rearrange("p a t -> p (a t)"), in0=pb[2 + m][:, 0:384], in1=dle[:, m * 3:(m + 1) * 3, :].rearrange("p a t -> p (a t)"), op=ALU.mult),
                             reads=[PB[2 + m], "dec"], writes=["aq"])
                    for h in range(6):
                        S.op("pe", lambda e, h=h: e.transpose(out=pT1[:, h, :], in_=Bm[0][:, h, :], identity=identb[:]), reads=["Bm0", "identb"], writes=["pT1"])
                    S.op("act", lambda e: e.copy(out=Am[0][:], in_=pT1[:, 0:6, :]), reads=["pT1"], writes=["Am0"])
                    S.op("dve", lambda e: e.tensor_tensor(out=Qm[0][:], in0=Bm[0][:], in1=V(identb[:, 0:1], [[0, 6], [1, 128]]), op=ALU.add), reads=["Bm0", "identb"], writes=["Qm0"])
                    for s_ in range(6):
                        c_, n_ = s_ % 2, (s_ + 1) % 2
                        for h in range(6):
                            bq, cs = hb_(h)
                            S.op("pe", lambda e, h=h, bq=bq, cs=cs, c_=c_: e.matmul(pb[bq][:, cs], lhsT=Bm[c_][:, h, :], rhs=Am[c_][:, h, :], start=True, stop=True),
                                 reads=[f"Bm{c_}", f"Am{c_}"], writes=[PB[bq]])
                        for m in range(2):
                            S.op("act", lambda e, m=m, n_=n_: e.copy(out=Am[n_][:, m * 3:(m + 1) * 3, :].rearrange("p a t -> p (a t)"), in_=pb[m][:, 0:384]), reads=[PB[m]], writes=[f"Am{n_}"])
                        if s_ < 5:
                            for h in range(6):
                                bq, cs = hb_(h)
                                S.op("pe", lambda e, h=h, bq=bq, cs=cs, c_=c_: e.matmul(pb[2 + bq][:, cs], lhsT=Am[c_][:, h, :], rhs=Bm[c_][:, h, :], start=True, stop=True),
                                     reads=[f"Bm{c_}", f"Am{c_}"], writes=[PB[2 + bq]])
                            for m in range(2):
                                S.op("dve", lambda e, m=m, n_=n_: e.tensor_copy(out=Bm[n_][:, m * 3:(m + 1) * 3, :].rearrange("p a t -> p (a t)"), in_=pb[2 + m][:, 0:384]), reads=[PB[2 + m]], writes=[f"Bm{n_}"])
                        for h in range(6):
                            bq, cs = hb_(h)
                            S.op("pe", lambda e, h=h, bq=bq, cs=cs, c_=c_, n_=n_: e.matmul(pb[4 + bq][:, cs], lhsT=Am[n_][:, h, :], rhs=Qm[c_][:, h, :], start=True, stop=True),
                                 reads=[f"Am{n_}", f"Qm{c_}"], writes=[PB[4 + bq]])
                        for m in range(2):
                            S.op("dve", lambda e, m=m, c_=c_, n_=n_: e.tensor_tensor(out=Qm[n_][:, m * 3:(m + 1) * 3, :].rearrange("p a t -> p (a t)"), in0=pb[4 + m][:, 0:384],
                                                                                 in1=Qm[c_][:, m * 3:(m + 1) * 3, :].rearrange("p a t -> p (a t)"), op=ALU.add), reads=[PB[4 + m], f"Qm{c_}"], writes=[f"Qm{n_}"])
                    Qf = Qm[0]
                    for h in range(6):
                        bq, cs = hb_(h)
                        S.op("pe", lambda e, h=h, bq=bq, cs=cs: e.matmul(pb[bq][:, cs], lhsT=qkT[:, 6 + h, :], rhs=Sb[:, h, :], start=True, stop=True), reads=["qkT", "Sb"], writes=[PB[bq]])
                    for h in range(6):
                        bq, cs = hb_(h)
                        S.op("dve", lambda e, h=h, bq=bq, cs=cs: e.scalar_tensor_tensor(out=rr[:, h, :], in0=pb[bq][:, cs], scalar=gs2[:, 24 + h:25 + h], in1=vtok[:, h, :], op0=ALU.mult, op1=ALU.add),
                             reads=[PB[bq], "gs2", "vtok"], writes=["rr"])
                    for h in range(6):
                        bq, cs = hb_(h)
                        S.op("pe", lambda e, h=h, bq=bq, cs=cs: e.matmul(pb[2 + bq][:, cs], lhsT=Qf[:, h, :], rhs=rr[:, h, :], start=True, stop=True), reads=["Qm0", "rr"], writes=[PB[2 + bq]])
                    for h in range(6):
                        bq, cs = hb_(h)
                        S.op("act", lambda e, h=h, bq=bq, cs=cs: e.activation(out=dlb[:, h, :], in_=pb[2 + bq][:, cs], func=AF.Copy, scale=gs[:, h:h + 1]), reads=[PB[2 + bq], "gs"], writes=["dlb"])
                    for h in range(6):
                        bq, cs = hb_(h)
                        S.op("pe", lambda e, h=h, bq=bq, cs=cs: e.matmul(pb[bq][:, cs], lhsT=qkT[:, h, :], rhs=Sb[:, h, :], start=True, stop=True), reads=["qkT", "Sb"], writes=[PB[bq]])
                        S.op("pe", lambda e, h=h, bq=bq, cs=cs: e.matmul(pb[4 + bq][:, cs], lhsT=aq[:, h, :], rhs=dlb[:, h, :], start=True, stop=True), reads=["aq", "dlb"], writes=[PB[4 + bq]])
                    for h in range(6):
                        bq, cs = hb_(h)
                        S.op("act", lambda e, h=h, bq=bq, cs=cs: e.activation(out=otmp[:, h, :], in_=pb[bq][:, cs], func=AF.Copy, scale=gs2[:, h:h + 1]), reads=[PB[bq], "gs2"], writes=["otmp"])
                    for m in range(2):
                        S.op("dve", lambda e, m=m: e.tensor_tensor(out=otok[:, m * 3:(m + 1) * 3, :].rearrange("p a t -> p (a t)"), in0=pb[4 + m][:, 0:384],
                                                                 in1=otmp[:, m * 3:(m + 1) * 3, :].rearrange("p a t -> p (a t)"), op=ALU.add), reads=[PB[4 + m], "otmp"], writes=["otok"])
                    S.op("dve", lambda e: e.tensor_tensor(out=kd[:], in0=ktok[:], in1=V(gs2[:, 6:7], [[1, 6], [0, 128]]), op=ALU.mult), reads=["ktok", "gs2"], writes=["kd"])
                    for h in range(6):
                        bq, cs = hb_(h)
                        S.op("pe", lambda e, h=h, bq=bq, cs=cs: e.matmul(pb[2 + bq][:, cs], lhsT=kd[:, h, :], rhs=dlb[:, h, :], start=True, stop=True), reads=["kd", "dlb"], writes=[PB[2 + bq]])
                    for h in range(6):
                        bq, cs = hb_(h)
                        S.op("dve", lambda e, h=h, bq=bq, cs=cs: e.scalar_tensor_tensor(out=Sf[:, h, :], in0=Sf[:, h, :], scalar=gs2[:, 12 + h:13 + h], in1=pb[2 + bq][:, cs], op0=ALU.mult, op1=ALU.add),
                             reads=["Sf", "gs2", PB[2 + bq]], writes=["Sf"])
                    S.op("act", lambda e: e.copy(out=Sb[:], in_=Sf[:]), reads=["Sf"], writes=["Sb"])
                    if i == NT - 1:
                        S.dma("sp", lambda e: e.dma_start(out=ndelta_p[e_].rearrange("h k v -> k h v"), in_=Sf[:]), reads=["Sf"])
                    headnorm(128, onb, "onb", 1.0, 0)
                    tail(128, xin[par][:], f"xin{par}", pre[:], "pre")
                    S.dma("sp", lambda e, i=i: e.dma_start(out=y_p[i * 128:(i + 1) * 128, :], in_=pre[:]), reads=["pre"], writes=[f"yd{i}"])
            S.fence()

    return nc, S, es, locals()


def _finish_build(nlayers=4):
    nc, S, es, env = build()
    even_layer = env["even_layer"]
    odd_layer = env.get("odd_layer")
    for layer in range(nlayers):
        if layer % 2 == 0:
            even_layer(layer)
        elif odd_layer is not None:
            odd_layer(layer)
    xs_sb = env["xs_sb"]; y_s = env["y_s"]
    S.dma("sp", lambda e: e.dma_start(out=y_s, in_=xs_sb[:16]), reads=["xs_sb"], key="final_ys")
    S.emit()
    es.close()
    return nc


OUT_NAMES = ["y_p", "y_s", "nk_p", "nv_p", "nk_s", "nv_s", "npool_p", "npool_s", "nconv_p", "nconv_s", "ndelta_p", "ndelta_s", "nsgv_s"]
_NC_CACHE = {}


def kernel(x_prompt, x_sample, cache_k, cache_v, state_pool, state_conv, state_delta, page_table,
           norm_w, w_in_e, w_out_e, pool_w, pool_scale, qn_w, kn_w, lam_qk, subln_w,
           w_in_o, w_out_o, conv_w, a_log, dt_bias, onorm_w, vnorm_w, w_s, b_s, _nlayers=4):
    f = lambda a: np.ascontiguousarray(np.asarray(a, dtype=np.float32))
    if _nlayers not in _NC_CACHE:
        _NC_CACHE[_nlayers] = _finish_build(_nlayers)
    nc = _NC_CACHE[_nlayers]
    ckf = f(cache_k).reshape(2 * NPHYS * 128, 768)
    cvf = f(cache_v).reshape(2 * NPHYS * 128, 768)
    consts = host_consts()
    shared = dict(ck=ckf, cv=cvf, norm_w=f(norm_w), w_in_e=f(w_in_e), w_out_e=f(w_out_e), pool_w=f(pool_w), pool_scale=f(pool_scale),
                  qn_w=f(qn_w), kn_w=f(kn_w), lam_qk=f(lam_qk).reshape(2, 256), subln_w=f(subln_w), w_in_o=f(w_in_o), w_out_o=f(w_out_o),
                  conv_w=f(conv_w), a_log=f(a_log), dt_bias=f(dt_bias), onorm_w=f(onorm_w), vnorm_w=f(vnorm_w).reshape(2, 256), w_s=f(w_s), b_s=f(b_s))
    shared.update(consts)
    xpr = f(x_prompt); xsa = f(x_sample)[:, 0, :]
    sp_ = f(state_pool); scv = f(state_conv); sdl = f(state_delta)
    ptab = np.ascontiguousarray(np.asarray(page_table, dtype=np.int32))
    in_maps = []
    for c in range(8):
        sl = slice(c * 16, (c + 1) * 16)
        m = dict(shared)
        m.update(xp=xpr[c], xs=np.ascontiguousarray(xsa[sl]), spool=np.ascontiguousarray(sp_[:, sl]), sconv=np.ascontiguousarray(scv[:, sl]),
                 sdelta=np.ascontiguousarray(sdl[:, sl]), pt=np.ascontiguousarray(ptab[sl].reshape(1, 256)))
        in_maps.append(m)
    res = run_bass_kernel_spmd(nc, in_maps, core_ids=list(range(8)))
    R = res.results
    g = lambda n: [np.asarray(r[n], dtype=np.float32) for r in R]
    y_prompt = np.stack(g("y_p"), 0)
    y_sample = np.concatenate(g("y_s"), 0)[:, None, :]
    nkp = np.stack(g("nk_p"), 1).reshape(2, 8, 2048, 12, 64)
    nvp = np.stack(g("nv_p"), 1).reshape(2, 8, 2048, 6, 128)
    nks = np.concatenate(g("nk_s"), 1).reshape(2, 128, 1, 12, 64)
    nvs = np.concatenate(g("nv_s"), 1).reshape(2, 128, 1, 6, 128)
    npp = np.stack(g("npool_p"), 1)
    nps = np.concatenate(g("npool_s"), 1)
    ncp = np.stack(g("nconv_p"), 1)
    ncs = np.concatenate(g("nconv_s"), 1)
    ndp = np.stack(g("ndelta_p"), 1)
    nds = np.concatenate(g("ndelta_s"), 1)
    nsg = np.concatenate(g("nsgv_s"), 1)[:, :, None, :]
    return (y_prompt, y_sample, nkp, nvp, nks, nvs, npp, nps, ncp, ncs, ndp, nds, nsg)
```
